# Optimizing a Trainium2 kernel written in Bass

```python
import jax, jax.numpy as jnp
from jax import lax
import numpy as np

D_MODEL = 4096
BATCH = 2
SEQ = 8192
DEPTH = 1

LRU_WIDTH = D_MODEL // 2
LRU_HEADS = 16
LRU_HEAD_DIM = LRU_WIDTH // LRU_HEADS
LRU_CONV_WIDTH = 4
LRU_C = 8.0
RWKV_WIDTH = D_MODEL - LRU_WIDTH
RWKV_HEAD_DIM = 64
RWKV_HEADS = RWKV_WIDTH // RWKV_HEAD_DIM
DECAY_RANK = 96
ICLR_RANK = 96
GATE_RANK = 256
MIX_WIDTH = LRU_WIDTH + RWKV_WIDTH
RWKV_PROJ_WIDTH = 3 * RWKV_WIDTH + DECAY_RANK + ICLR_RANK + GATE_RANK
IN_PROJ_WIDTH = 2 * LRU_WIDTH + RWKV_PROJ_WIDTH
D_FF = 11008
FFN_CONV_WIDTH = 3
NORM_EPS = 1e-6
GN_EPS = 64e-5
L2_EPS = 1e-12

kernel_name = "hymba_rglru_rwkv7_convffn"


def rmsnorm(x, g):
    xf = x.astype(jnp.float32)
    out = xf * lax.rsqrt(jnp.mean(xf * xf, axis=-1, keepdims=True) + NORM_EPS)
    return (out * g.astype(jnp.float32)).astype(x.dtype)


def causal_dwconv(x, w, b):
    width = w.shape[0]
    seq = x.shape[1]
    xp = jnp.pad(x, ((0, 0), (width - 1, 0), (0, 0)))
    return sum(xp[:, j:j + seq] * w[j] for j in range(width)) + b


def token_shift(z):
    return jnp.pad(z[:, :-1], ((0, 0), (1, 0), (0, 0)))


def rg_lru(x, w_a, b_a, w_i, b_i, lam):
    bsz, seq, _ = x.shape
    xf = x.astype(jnp.float32)
    xh = xf.reshape(bsz, seq, LRU_HEADS, LRU_HEAD_DIM)
    r = jax.nn.sigmoid(jnp.einsum('bshi,hij->bshj', xh, w_a.astype(jnp.float32)) + b_a).reshape(bsz, seq, LRU_WIDTH)
    i = jax.nn.sigmoid(jnp.einsum('bshi,hij->bshj', xh, w_i.astype(jnp.float32)) + b_i).reshape(bsz, seq, LRU_WIDTH)
    log_a = -LRU_C * r * jax.nn.softplus(-lam.astype(jnp.float32))
    a = jnp.exp(log_a)
    u = jnp.sqrt(-jnp.expm1(2.0 * log_a)) * (i * xf)

    def combine(left, right):
        a_l, b_l = left
        a_r, b_r = right
        return a_l * a_r, a_r * b_l + b_r

    _, h = lax.associative_scan(combine, (a, u), axis=1)
    return h.astype(x.dtype)


def rwkv7_step(state, inp):
    r, w, k, v, kk, a = inp
    sa = jnp.einsum('bhij,bhj->bhi', state, kk)
    state = (state * w[:, :, None, :]
             - sa[..., None] * (kk * a)[:, :, None, :]
             + v[..., None] * k[:, :, None, :])
    y = jnp.einsum('bhij,bhj->bhi', state, r)
    return state, y


def rwkv7_mix(z, mu, w0, w2, a0, a2, g2, k_k, k_a, r_k, gn_w, gn_b):
    bsz, seq, _ = z.shape
    z = z + (token_shift(z) - z) * mu
    idx = np.cumsum([RWKV_WIDTH, RWKV_WIDTH, RWKV_WIDTH, DECAY_RANK, ICLR_RANK]).tolist()
    r, k, v, zw, za, zg = jnp.split(z, idx, axis=-1)
    f32 = jnp.float32
    w_log = -jax.nn.softplus(-(w0.astype(f32) + jnp.tanh(zw.astype(f32)) @ w2.astype(f32))) - 0.5
    decay = jnp.exp(-jnp.exp(w_log))
    a = jax.nn.sigmoid(a0.astype(f32) + za.astype(f32) @ a2.astype(f32))
    g = jax.nn.sigmoid(zg) @ g2

    def heads(t):
        return t.astype(f32).reshape(bsz, seq, RWKV_HEADS, RWKV_HEAD_DIM)

    r, k, v, decay, a = heads(r), heads(k), heads(v), heads(decay), heads(a)
    kk = k * k_k.astype(f32).reshape(RWKV_HEADS, RWKV_HEAD_DIM)
    kk = kk / jnp.maximum(jnp.sqrt(jnp.sum(kk * kk, axis=-1, keepdims=True)), L2_EPS)
    k = k * (1.0 + (a - 1.0) * k_a.astype(f32).reshape(RWKV_HEADS, RWKV_HEAD_DIM))

    xs = tuple(jnp.moveaxis(t, 1, 0) for t in (r, decay, k, v, kk, a))
    state0 = jnp.zeros((bsz, RWKV_HEADS, RWKV_HEAD_DIM, RWKV_HEAD_DIM), f32)
    _, y = lax.scan(rwkv7_step, state0, xs)
    y = jnp.moveaxis(y, 0, 1)

    mean = jnp.mean(y, axis=-1, keepdims=True)
    var = jnp.mean(jnp.square(y - mean), axis=-1, keepdims=True)
    y = (y - mean) * lax.rsqrt(var + GN_EPS)
    y = y * gn_w.astype(f32).reshape(RWKV_HEADS, RWKV_HEAD_DIM) + gn_b.astype(f32).reshape(RWKV_HEADS, RWKV_HEAD_DIM)
    bonus = jnp.sum(r * k * r_k.astype(f32).reshape(RWKV_HEADS, RWKV_HEAD_DIM), axis=-1, keepdims=True) * v
    out = (y + bonus).reshape(bsz, seq, RWKV_WIDTH).astype(z.dtype)
    return out * g


def setup_inputs(seed: int = 0) -> dict:
    key = jax.random.key(seed)
    ks = jax.random.split(key, 32)
    f32 = jnp.float32

    def nrm(k, shape, scale):
        return jax.random.normal(k, shape, f32) * scale

    a_c = jax.random.uniform(ks[10], (DEPTH, LRU_WIDTH), f32, 0.9, 0.999)
    a_init = a_c ** (1.0 / LRU_C)
    lam = jnp.log(a_init) - jnp.log1p(-a_init)
    return {
        "x": nrm(ks[0], (BATCH, SEQ, D_MODEL), 1.0),
        "g_mix": 1.0 + nrm(ks[1], (DEPTH, D_MODEL), 0.02),
        "w_in": nrm(ks[2], (DEPTH, D_MODEL, IN_PROJ_WIDTH), D_MODEL ** -0.5),
        "conv_lru_w": nrm(ks[3], (DEPTH, LRU_CONV_WIDTH, LRU_WIDTH), LRU_CONV_WIDTH ** -0.5),
        "conv_lru_b": nrm(ks[4], (DEPTH, LRU_WIDTH), 0.01),
        "lru_w_a": nrm(ks[5], (DEPTH, LRU_HEADS, LRU_HEAD_DIM, LRU_HEAD_DIM), LRU_HEAD_DIM ** -0.5),
        "lru_b_a": nrm(ks[6], (DEPTH, LRU_HEADS, LRU_HEAD_DIM), 0.01),
        "lru_w_i": nrm(ks[7], (DEPTH, LRU_HEADS, LRU_HEAD_DIM, LRU_HEAD_DIM), LRU_HEAD_DIM ** -0.5),
        "lru_b_i": nrm(ks[8], (DEPTH, LRU_HEADS, LRU_HEAD_DIM), 0.01),
        "lru_lambda": lam,
        "rwkv_mu": jax.random.uniform(ks[11], (DEPTH, RWKV_PROJ_WIDTH), f32, 0.0, 1.0),
        "rwkv_w0": jax.random.uniform(ks[12], (DEPTH, RWKV_WIDTH), f32, -6.0, -1.0),
        "rwkv_w2": nrm(ks[13], (DEPTH, DECAY_RANK, RWKV_WIDTH), 0.1 * DECAY_RANK ** -0.5),
        "rwkv_a0": nrm(ks[14], (DEPTH, RWKV_WIDTH), 0.1),
        "rwkv_a2": nrm(ks[15], (DEPTH, ICLR_RANK, RWKV_WIDTH), 0.5 * ICLR_RANK ** -0.5),
        "rwkv_g2": nrm(ks[16], (DEPTH, GATE_RANK, RWKV_WIDTH), GATE_RANK ** -0.5),
        "rwkv_k_k": 0.85 + nrm(ks[17], (DEPTH, RWKV_WIDTH), 0.02),
        "rwkv_k_a": 1.0 + nrm(ks[18], (DEPTH, RWKV_WIDTH), 0.02),
        "rwkv_r_k": nrm(ks[19], (DEPTH, RWKV_WIDTH), 0.1),
        "rwkv_gn_w": 1.0 + nrm(ks[20], (DEPTH, RWKV_WIDTH), 0.02),
        "rwkv_gn_b": nrm(ks[21], (DEPTH, RWKV_WIDTH), 0.01),
        "w_out": nrm(ks[22], (DEPTH, MIX_WIDTH, D_MODEL), MIX_WIDTH ** -0.5),
        "g_ffn": 1.0 + nrm(ks[23], (DEPTH, D_MODEL), 0.02),
        "w_ffn_gate": nrm(ks[24], (DEPTH, D_MODEL, D_FF), D_MODEL ** -0.5),
        "ffn_conv_w": nrm(ks[25], (DEPTH, FFN_CONV_WIDTH, D_FF), FFN_CONV_WIDTH ** -0.5),
        "ffn_conv_b": nrm(ks[26], (DEPTH, D_FF), 0.01),
        "w_ffn_up": nrm(ks[27], (DEPTH, D_MODEL, D_FF), D_MODEL ** -0.5),
        "w_ffn_down": nrm(ks[28], (DEPTH, D_FF, D_MODEL), D_FF ** -0.5),
        "g_final": 1.0 + nrm(ks[29], (D_MODEL,), 0.02),
    }


def reference(x, g_mix, w_in, conv_lru_w, conv_lru_b, lru_w_a, lru_b_a, lru_w_i, lru_b_i,
              lru_lambda, rwkv_mu, rwkv_w0, rwkv_w2, rwkv_a0, rwkv_a2, rwkv_g2, rwkv_k_k,
              rwkv_k_a, rwkv_r_k, rwkv_gn_w, rwkv_gn_b, w_out, g_ffn, w_ffn_gate, ffn_conv_w,
              ffn_conv_b, w_ffn_up, w_ffn_down, g_final):
    for l in range(DEPTH):
        h = rmsnorm(x, g_mix[l])
        z = h @ w_in[l]
        z_lru_x = z[..., :LRU_WIDTH]
        z_lru_gate = z[..., LRU_WIDTH:2 * LRU_WIDTH]
        z_rwkv = z[..., 2 * LRU_WIDTH:]
        xc = causal_dwconv(z_lru_x, conv_lru_w[l], conv_lru_b[l])
        y_lru = rg_lru(xc, lru_w_a[l], lru_b_a[l], lru_w_i[l], lru_b_i[l], lru_lambda[l]) * jax.nn.gelu(z_lru_gate)
        y_rwkv = rwkv7_mix(z_rwkv, rwkv_mu[l], rwkv_w0[l], rwkv_w2[l], rwkv_a0[l], rwkv_a2[l],
                           rwkv_g2[l], rwkv_k_k[l], rwkv_k_a[l], rwkv_r_k[l], rwkv_gn_w[l], rwkv_gn_b[l])
        y = jnp.concatenate([y_lru, y_rwkv], axis=-1)
        x = x + y @ w_out[l]
        h = rmsnorm(x, g_ffn[l])
        gate = causal_dwconv(h @ w_ffn_gate[l], ffn_conv_w[l], ffn_conv_b[l])
        x = x + (jax.nn.silu(gate) * (h @ w_ffn_up[l])) @ w_ffn_down[l]
    return rmsnorm(x, g_final)
```

```python
import contextlib
import os
import numpy as np
import concourse.bass as bass
import concourse.mybir as mybir
from concourse.bass_utils import run_bass_kernel_spmd

F32 = mybir.dt.float32
BF16 = mybir.dt.bfloat16
ALU = mybir.AluOpType
AF = mybir.ActivationFunctionType

SAME_ENGINE_SYNC = True
NORM_EPS = 1e-6
GN_EPS = 64e-5
C = 64


class _Op:
    __slots__ = ("eng", "fn", "deps", "idx", "is_dma", "sem_key", "sem_val", "needs_inc")


class _Stop(Exception):
    pass


class _Rec:
    def __init__(self):
        self.call = None

    def __getattr__(self, name):
        def f(*a, **k):
            self.call = (name, a, k)
            return None
        return f


class Prog:
    ENGINES = ("tensor", "vector", "scalar", "gpsimd", "sync")

    def __init__(self, nc):
        self.nc = nc
        self.ops = []
        self.last_writer = {}
        self.readers = {}
        self.dma_counts = {}

    def add(self, eng, fn, reads=(), writes=(), dma=None):
        op = _Op()
        op.eng = eng
        rec = _Rec()
        fn(rec)
        op.fn = rec.call
        op.idx = len(self.ops)
        op.is_dma = dma is not None
        op.sem_key = dma
        op.needs_inc = False
        op.sem_val = None
        reads = list(reads) + ["__phase__"]
        deps = set()
        for b in reads:
            lw = self.last_writer.get(b)
            if lw is not None:
                deps.add(lw)
        for b in writes:
            lw = self.last_writer.get(b)
            if lw is not None:
                deps.add(lw)
            for r in self.readers.get(b, ()):
                deps.add(r)
        deps.discard(op.idx)
        op.deps = deps
        for b in reads:
            self.readers.setdefault(b, []).append(op.idx)
        for b in writes:
            self.last_writer[b] = op.idx
            self.readers[b] = []
        if op.is_dma:
            c = self.dma_counts.get(dma, 0) + 16
            self.dma_counts[dma] = c
            op.sem_val = c
        self.ops.append(op)
        return op.idx

    def barrier(self, tile_ap):
        self.add("gpsimd", lambda e: e.memset(tile_ap, 0.0), reads=[], writes=["__phase__", "__bar__"])

    def emit(self, stack):
        nc = self.nc
        ops = self.ops
        for op in ops:
            for d in op.deps:
                p = ops[d]
                if p.is_dma:
                    continue
                if p.eng == op.eng and not op.is_dma and (p.eng == "tensor" or not SAME_ENGINE_SYNC):
                    continue
                p.needs_inc = True
        esem = {e: stack.enter_context(nc.semaphore("es_" + e)) for e in self.ENGINES}
        dsem = {}
        for i, k in enumerate(self.dma_counts):
            dsem[k] = stack.enter_context(nc.semaphore("ds%d" % i))
        cnt = {e: 0 for e in self.ENGINES}
        for op in ops:
            if not op.is_dma and op.needs_inc:
                cnt[op.eng] += 1
                op.sem_val = cnt[op.eng]
        per_eng = {e: [] for e in self.ENGINES}
        for op in ops:
            per_eng[op.eng].append(op)
        block = stack.enter_context(nc.Block())

        def make(e):
            def body(eng):
                waited = {}
                for op in per_eng[e]:
                    need = {}
                    for d in op.deps:
                        p = ops[d]
                        if p.is_dma:
                            s, v, key = dsem[p.sem_key], p.sem_val, ("d", p.sem_key)
                        else:
                            if not p.needs_inc:
                                continue
                            if p.eng == e and not op.is_dma and (e == "tensor" or not SAME_ENGINE_SYNC):
                                continue
                            s, v, key = esem[p.eng], p.sem_val, ("e", p.eng)
                        if waited.get(key, 0) >= v:
                            continue
                        if key not in need or need[key][1] < v:
                            need[key] = (s, v)
                    for key, (s, v) in need.items():
                        eng.wait_ge(s, v)
                        waited[key] = v
                    ins = getattr(eng, op.fn[0])(*op.fn[1], **op.fn[2])
                    if op.is_dma:
                        ins.then_inc(dsem[op.sem_key], 16)
                    elif op.needs_inc:
                        ins.then_inc(esem[e], 1)
                if e == "sync":
                    for k, c in self.dma_counts.items():
                        eng.wait_ge(dsem[k], c)
            return body

        block.tensor(make("tensor"))
        block.vector(make("vector"))
        block.scalar(make("scalar"))
        block.gpsimd(make("gpsimd"))
        block.sync(make("sync"))


class Cfg:
    def __init__(self, D, S, NT, LH, RP, DFF):
        self.D, self.S, self.NT, self.LH, self.RP, self.DFF = D, S, NT, LH, RP, DFF
        self.NDC = D // 128
        self.NF = DFF // 128
        self.NKM = LH + RP
        self.NCT = 2 * LH + 3 * RP + 4
        self.INW = 2 * LH * 128 + 3 * RP * 128 + 96 + 96 + 256
        self.TBW = min(1024, S)
        self.TBC = min(512, NT)
        o = 0
        def take(n):
            nonlocal o
            r = o
            o += n
            return r
        self.o_gmix = take(self.NDC); self.o_gffn = take(self.NDC); self.o_gfin = take(self.NDC)
        self.o_cw = take(4 * LH); self.o_cb = take(LH); self.o_ba = take(LH); self.o_bi = take(LH); self.o_lam = take(LH)
        self.o_mu = take(3 * RP + 4)
        self.o_w0 = take(RP); self.o_a0 = take(RP); self.o_kk = take(RP); self.o_ka = take(RP)
        self.o_rk = take(RP); self.o_gw = take(RP); self.o_gb = take(RP)
        self.o_fw = take(3 * self.NF); self.o_fb = take(self.NF)
        self.NP = o

    def coltile(self, ct):
        LH, RP = self.LH, self.RP
        nfull = 2 * LH + 3 * RP
        if ct < nfull:
            return ct * 128, 128
        base = nfull * 128
        return [(base, 96), (base + 96, 96), (base + 192, 128), (base + 320, 128)][ct - nfull]


REAL = Cfg(4096, 8192, 2048, 16, 16, 11008)


def host_consts():
    c = np.zeros((128, 7, 128), np.float32)
    c[:, 0, :] = np.eye(128)
    bd = np.zeros((128, 128), np.float32); bd[:64, :64] = 1; bd[64:, 64:] = 1
    c[:, 1, :] = bd
    c[:, 2, :] = bd / 64.0
    c[:, 3, :] = 1.0
    t = np.arange(128) % 64
    s = np.arange(64)
    c[:, 4, 0:64] = (s[None, :] < t[:, None])
    c[:, 5, 0:64] = (s[None, :] > t[:, None])
    c[:, 5, 64:128] = (s[None, :] >= t[:, None])
    c[:, 6, :] = 1.0
    c[:, 6, 0::64] = 0.0
    return c


def host_pvec(cfg, inp):
    LH, RP = cfg.LH, cfg.RP
    pv = np.zeros((128, cfg.NP), np.float32)
    def put(o, vec):
        v = np.asarray(vec, np.float32).reshape(-1, 128)
        pv[:, o:o + v.shape[0]] = v.T
    put(cfg.o_gmix, inp["g_mix"][0]); put(cfg.o_gffn, inp["g_ffn"][0]); put(cfg.o_gfin, inp["g_final"])
    for j in range(4):
        put(cfg.o_cw + j * LH, inp["conv_lru_w"][0, j])
    put(cfg.o_cb, inp["conv_lru_b"][0]); put(cfg.o_ba, inp["lru_b_a"][0].reshape(-1)); put(cfg.o_bi, inp["lru_b_i"][0].reshape(-1))
    put(cfg.o_lam, inp["lru_lambda"][0])
    mu = inp["rwkv_mu"][0]
    W = RP * 128
    put(cfg.o_mu, mu[:3 * W])
    for i, (a, n) in enumerate([(3 * W, 96), (3 * W + 96, 96), (3 * W + 192, 128), (3 * W + 320, 128)]):
        pv[:n, cfg.o_mu + 3 * RP + i] = mu[a:a + n]
    for o, k in [(cfg.o_w0, "rwkv_w0"), (cfg.o_a0, "rwkv_a0"), (cfg.o_kk, "rwkv_k_k"), (cfg.o_ka, "rwkv_k_a"),
                 (cfg.o_rk, "rwkv_r_k"), (cfg.o_gw, "rwkv_gn_w"), (cfg.o_gb, "rwkv_gn_b")]:
        put(o, inp[k][0])
    for j in range(3):
        put(cfg.o_fw + j * cfg.NF, inp["ffn_conv_w"][0, j])
    put(cfg.o_fb, inp["ffn_conv_b"][0])
    return pv


def build(cfg):
    D, S, NT, LH, RP, DFF = cfg.D, cfg.S, cfg.NT, cfg.LH, cfg.RP, cfg.DFF
    NDC, NF, NKM, NCT = cfg.NDC, cfg.NF, cfg.NKM, cfg.NCT
    EXT = 128
    BASE = S - NT - EXT
    NTE = NT + EXT
    assert BASE >= 0
    nc = bass.Bass("TRN2", target_bir_lowering=False)
    dt_in = lambda n, s, d=F32: nc.dram_tensor(n, s, d, kind="ExternalInput").ap()
    x_in = dt_in("x", [S, D])
    pmask = dt_in("pmask", [128, S])
    consts_in = dt_in("consts", [128, 7, 128])
    pvec_in = dt_in("pvec", [128, cfg.NP])
    w_in = dt_in("w_in", [D, cfg.INW])
    lru_wa = dt_in("lru_wa", [LH, 128, 128])
    lru_wi = dt_in("lru_wi", [LH, 128, 128])
    w2_in = dt_in("w2", [96, RP * 128])
    a2_in = dt_in("a2", [96, RP * 128])
    g2_in = dt_in("g2", [256, RP * 128])
    w_out = dt_in("w_out", [NKM * 128, D])
    w_gate = dt_in("w_gate", [D, DFF])
    w_up = dt_in("w_up", [D, DFF])
    w_down = dt_in("w_down", [DFF, D])
    out = nc.dram_tensor("out", [NT, D], F32, kind="ExternalOutput").ap()
    scr = lambda n, s, d=F32: nc.dram_tensor(n, s, d, kind="Internal").ap()
    zTl = [scr("zT%d" % i, [128, S]) for i in range(NCT)]
    xTs = scr("xTs", [NDC, 128, NTE])
    yTs = scr("yTs", [NKM, 128, NTE], BF16)
    x1s = scr("x1s", [NDC, 128, NTE])
    x2s = scr("x2s", [NDC, 128, NTE])

    P = Prog(nc)
    with contextlib.ExitStack() as st:
        AF32 = 50400
        arena = nc.alloc_sbuf_tensor("arena", [128, AF32], F32)
        cst = nc.alloc_sbuf_tensor("cst", [128, 7, 128], F32)
        pv = nc.alloc_sbuf_tensor("pv", [128, cfg.NP], F32)
        dv = nc.alloc_sbuf_tensor("dv", [128, 4 * RP + 8 + LH * 2], F32)
        bar = nc.alloc_sbuf_tensor("bar", [128, 8], F32)
        banks = [nc.alloc_psum_tensor("pb%d" % i, [128, 512], F32) for i in range(8)]
        off = {"f": 0}

        def reset():
            off["f"] = 0

        def tf(n):
            n = (n + 7) // 8 * 8
            a = arena[:, off["f"]:off["f"] + n]
            off["f"] += n
            assert off["f"] <= AF32, off["f"]
            return a

        def tb(n):
            n = (n + 15) // 16 * 16
            return tf(n // 2).bitcast(BF16)

        ident = cst[:, 0, :]; BD = cst[:, 1, :]; BDS = cst[:, 2, :]; ONES = cst[:, 3, :]
        MK_L = cst[:, 4, 0:64]; MK_T = cst[:, 5, :]; MK_C = cst[:, 6, :]
        pc = lambda o: pv[:, o:o + 1]

        def dma(key, out_ap, in_ap, r, w, eng="sync"):
            P.add(eng, lambda e: e.dma_start(out=out_ap, in_=in_ap), reads=r, writes=w, dma=key)

        V = lambda fn, r, w: P.add("vector", fn, r, w)
        A = lambda fn, r, w: P.add("scalar", fn, r, w)
        G = lambda fn, r, w: P.add("gpsimd", fn, r, w)
        T = lambda fn, r, w: P.add("tensor", fn, r, w)
        rr = {"i": 0}

        def E(fn, r, w):
            rr["i"] += 1
            P.add("vector", fn, r, w)

        dma("cst", cst[:], consts_in, [], ["cst"])
        dma("pv", pv[:], pvec_in, [], ["pv"])
        o_c1 = 0; o_c2 = LH
        A(lambda e: e.activation(out=dv[:, 0:LH], in_=pv[:, cfg.o_lam:cfg.o_lam + LH], func=AF.Exp, scale=-1.0), ["pv"], ["dv"])
        A(lambda e: e.activation(out=dv[:, 0:LH], in_=dv[:, 0:LH], func=AF.Ln, bias=1.0), ["dv"], ["dv"])
        V(lambda e: e.tensor_scalar(out=dv[:, LH:2 * LH], in0=dv[:, 0:LH], scalar1=-16.0, scalar2=None, op0=ALU.mult), ["dv"], ["dv2"])
        V(lambda e: e.tensor_scalar(out=dv[:, 0:LH], in0=dv[:, 0:LH], scalar1=-8.0, scalar2=None, op0=ALU.mult), ["dv", "dv2"], ["dv"])
        dc = lambda o: dv[:, o:o + 1]

        wslot = {"i": 0}

        def project(W, c0, m, nk, rhs_fn, rhs_keys, nsub, n, sink, wst, wbf, bank_ids, tagc):
            k0 = 0
            first = True
            pieces = []
            while k0 < nk:
                kn = min(32 if nk <= 32 else 29, nk - k0)
                pieces.append((k0, kn))
                k0 += kn
            assert len(pieces) == 1 or nsub <= len(bank_ids)
            for pi, (k0, kn) in enumerate(pieces):
                sl = wslot["i"] % 2
                wslot["i"] += 1
                ws = wst[sl].rearrange("p (k c) -> p k c", c=128)
                wb_ = wbf[sl].rearrange("p (k c) -> p k c", c=128)
                for ka in range(0, kn, 8):
                    kb = min(kn, ka + 8)
                    dma(("wst", sl), ws[:, ka:kb, 0:m], W[(k0 + ka) * 128:(k0 + kb) * 128, c0:c0 + m].rearrange("(k p) c -> p k c", p=128),
                        [], [("wst", sl)])
                G(lambda e, ws=ws, wb_=wb_, kn=kn: e.tensor_copy(out=wb_[:, 0:kn, 0:m], in_=ws[:, 0:kn, 0:m]), [("wst", sl)], [("wbf", sl)])
                for sb in range(nsub):
                    bk = bank_ids[(tagc * nsub + sb) % len(bank_ids)] if len(pieces) == 1 else bank_ids[sb]
                    for k in range(kn):
                        kk = k0 + k
                        T(lambda e, bk=bk, wb_=wb_, k=k, kk=kk, sb=sb: e.matmul(banks[bk][0:m, 0:n], lhsT=wb_[:, k, 0:m], rhs=rhs_fn(kk, sb),
                                                                        start=(kk == 0), stop=(kk == nk - 1)),
                          [("wbf", sl)] + rhs_keys, [("bank", bk)])
                    if pi == len(pieces) - 1:
                        sink(sb, banks[bk][0:m, 0:n], ("bank", bk))

        reset()
        TBW = cfg.TBW
        xs = [tf(D) for _ in range(2)]
        xTt = tf(NDC * 128).rearrange("p (d t) -> p d t", t=128)
        sqt = tf(NDC * 128).rearrange("p (d t) -> p d t", t=128)
        rstd = tf(128)
        wst = [tf(32 * 128) for _ in range(2)]
        zst = [tf(512) for _ in range(4)]
        hT = tb(NDC * TBW).rearrange("p (d t) -> p d t", t=TBW)
        wbf = [tb(32 * 128) for _ in range(2)]
        ti = 0
        zc = 0
        for wb in range(S // TBW):
            for tt in range(TBW // 128):
                t0 = wb * TBW + tt * 128
                sl = ti % 2
                ti += 1
                dma(("xs", sl), xs[sl], x_in[t0:t0 + 128, :], [], [("xs", sl)])
                for g4 in range(NDC // 4 if NDC >= 4 else 1):
                    nd = min(4, NDC)
                    bk = g4 % 2
                    for j in range(nd):
                        d = g4 * 4 + j
                        T(lambda e, bk=bk, j=j, d=d, sl=sl: e.matmul(banks[bk][:, j * 128:(j + 1) * 128], lhsT=xs[sl][:, d * 128:(d + 1) * 128],
                                                                    rhs=ident, start=True, stop=True), [("xs", sl), "cst"], [("bank", bk)])
                    V(lambda e, bk=bk, g4=g4, nd=nd: e.tensor_copy(out=xTt[:, g4 * 4:g4 * 4 + nd, :], in_=banks[bk][:, 0:nd * 128].rearrange("p (d t) -> p d t", t=128)),
                      [("bank", bk)], [("xTt", g4)])
                    A(lambda e, g4=g4, nd=nd: e.activation(out=sqt[:, g4 * 4:g4 * 4 + nd, :], in_=xTt[:, g4 * 4:g4 * 4 + nd, :], func=AF.Square),
                      [("xTt", g4)], [("sqt", g4)])
                for d in range(NDC):
                    T(lambda e, d=d: e.matmul(banks[2][:, 0:128], lhsT=ONES, rhs=sqt[:, d, :], start=(d == 0), stop=(d == NDC - 1)),
                      [("sqt", d // 4), "cst"], [("bank", 2)])
                V(lambda e: e.tensor_scalar(out=rstd, in0=banks[2][:, 0:128], scalar1=1.0 / D, scalar2=NORM_EPS, op0=ALU.mult, op1=ALU.add), [("bank", 2)], ["rstd"])
                A(lambda e: e.activation(out=rstd, in_=rstd, func=AF.Sqrt), ["rstd"], ["rstd"])
                V(lambda e: e.reciprocal(out=rstd, in_=rstd), ["rstd"], ["rstd"])
                for d in range(NDC):
                    E(lambda e, d=d, tt=tt: e.scalar_tensor_tensor(out=hT[:, d, tt * 128:(tt + 1) * 128], in0=xTt[:, d, :], scalar=pc(cfg.o_gmix + d), in1=rstd,
                                                                  op0=ALU.mult, op1=ALU.mult), [("xTt", d // 4), "rstd", "pv"], [("hT", tt)])
                if t0 >= BASE:
                    o = t0 - BASE
                    for da in range(0, NDC, 8):
                        db = min(NDC, da + 8)
                        dma("xTs", xTs[da:db, :, o:o + 128].rearrange("d p t -> p d t"), xTt[:, da:db, :], [("xTt", g) for g in range(max(1, NDC // 4))], [("xTs", o // 128, da)])
            nsub = max(1, TBW // 512)
            n = min(512, TBW)
            for ct in range(NCT):
                c0, m = cfg.coltile(ct)

                def sink(sb, ps, key, ct=ct, m=m, wb=wb, n=n):
                    nonlocal zc
                    zs = zc % 4
                    zc += 1
                    A(lambda e, zs=zs: e.copy(out=zst[zs][0:m, 0:n], in_=ps), [key], [("zst", zs)])
                    tok = wb * TBW + sb * n
                    dma(("zst", zs), zTl[ct][0:m, tok:tok + n], zst[zs][0:m, 0:n], [("zst", zs)], [("zT", ct, tok // 256), ("zT", ct, tok // 256 + 1)] if n == 512 else [("zT", ct, tok // 256)])
                project(w_in, c0, m, NDC, lambda k, sb: hT[:, k, sb * n:(sb + 1) * n], [("hT", t) for t in range(TBW // 128)], nsub, n, sink, wst, wbf, [3, 4, 5, 6], ct)
        P.barrier(bar[:, 0:1])

        reset()
        STOP = float(os.environ.get('KSTOP', '9'))
        KEXP = os.environ.get('KEXP', '')
        LH_ = LH if STOP >= 2 else 0
        TB = min(512, S)
        NBL = S // TB
        wa_t = tf(128); wi_t = tf(128)
        hprev = tf(8)[:, 0:1]
        lx = [tf(TB + 3) for _ in range(2)]
        lg = [tf(TB) for _ in range(2)]
        lm = [tf(TB) for _ in range(2)]
        nm = ["xc", "r", "ig", "a", "a2", "u", "h", "t1", "t2", "y"]
        lt = {k: tf(TB) for k in nm}
        ybf = [tb(TB) for _ in range(2)]
        li = 0
        for h in range(LH_):
            dma("wa", wa_t, lru_wa[h], [], ["wa"])
            dma("wi", wi_t, lru_wi[h], [], ["wi"])
            for b in range(NBL):
                t0 = b * TB
                sl = li % 2
                li += 1
                if b == 0:
                    G(lambda e, sl=sl: e.memset(lx[sl][:, 0:3], 0.0), [], [("lx", sl)])
                    dma(("lx", sl), lx[sl][:, 3:TB + 3], zTl[h][:, 0:TB], [("zT", h, j) for j in range(0, max(1, TB // 256))], [("lx", sl)])
                else:
                    dma(("lx", sl), lx[sl][:, 0:TB + 3], zTl[h][:, t0 - 3:t0 + TB], [("zT", h, j) for j in range((t0 - 3) // 256, (t0 + TB - 1) // 256 + 1)], [("lx", sl)])
                dma(("lm", sl), lm[sl], pmask[:, t0:t0 + TB], [], [("lm", sl)])
                need_y = t0 + TB > BASE
                if need_y:
                    dma(("lg", sl), lg[sl], zTl[LH + h][:, t0:t0 + TB], [("zT", LH + h, j) for j in range(t0 // 256, (t0 + TB - 1) // 256 + 1)], [("lg", sl)])
                X = lx[sl]
                A(lambda e, X=X, h=h: e.activation(out=lt["xc"], in_=X[:, 3:TB + 3], func=AF.Identity, scale=pc(cfg.o_cw + 3 * LH + h), bias=pc(cfg.o_cb + h)),
                  [("lx", sl), "pv"], ["xc"])
                for j in range(3):
                    V(lambda e, X=X, h=h, j=j: e.scalar_tensor_tensor(out=lt["xc"], in0=X[:, j:j + TB], scalar=pc(cfg.o_cw + j * LH + h), in1=lt["xc"], op0=ALU.mult, op1=ALU.add),
                      [("lx", sl), "pv", "xc"], ["xc"])
                T(lambda e: e.matmul(banks[0][:, 0:TB], lhsT=wa_t, rhs=lt["xc"], start=True, stop=True), ["wa", "xc"], [("bank", 0)])
                T(lambda e: e.matmul(banks[1][:, 0:TB], lhsT=wi_t, rhs=lt["xc"], start=True, stop=True), ["wi", "xc"], [("bank", 1)])
                A(lambda e, h=h: e.activation(out=lt["r"], in_=banks[0][:, 0:TB], func=AF.Sigmoid, bias=pc(cfg.o_ba + h)), [("bank", 0), "pv"], ["r"])
                A(lambda e, h=h: e.activation(out=lt["ig"], in_=banks[1][:, 0:TB], func=AF.Sigmoid, bias=pc(cfg.o_bi + h)), [("bank", 1), "pv"], ["ig"])
                A(lambda e, h=h: e.activation(out=lt["a"], in_=lt["r"], func=AF.Exp, scale=dc(h)), ["r", "dv"], ["a"])
                A(lambda e, h=h: e.activation(out=lt["a2"], in_=lt["r"], func=AF.Exp, scale=dc(LH + h)), ["r", "dv2"], ["a2"])
                V(lambda e: e.tensor_scalar(out=lt["a2"], in0=lt["a2"], scalar1=-1.0, scalar2=1.0, op0=ALU.mult, op1=ALU.add), ["a2"], ["a2"])
                V(lambda e: e.tensor_scalar(out=lt["a2"], in0=lt["a2"], scalar1=0.0, scalar2=None, op0=ALU.max), ["a2"], ["a2"])
                A(lambda e: e.activation(out=lt["a2"], in_=lt["a2"], func=AF.Sqrt), ["a2"], ["a2"])
                G(lambda e: e.tensor_tensor(out=lt["u"], in0=lt["ig"], in1=lt["xc"], op=ALU.mult), ["ig", "xc"], ["u"])
                G(lambda e, sl=sl: e.tensor_tensor(out=lt["u"], in0=lt["u"], in1=lm[sl], op=ALU.mult), ["u", ("lm", sl)], ["u"])
                V(lambda e: e.tensor_tensor(out=lt["u"], in0=lt["u"], in1=lt["a2"], op=ALU.mult), ["u", "a2"], ["u"])
                if b == 0:
                    V(lambda e: e.tensor_tensor_scan(out=lt["h"], data0=lt["a"], data1=lt["u"], initial=0.0, op0=ALU.mult, op1=ALU.add), ["a", "u", "hprev"], ["h"])
                else:
                    V(lambda e: e.tensor_tensor_scan(out=lt["h"], data0=lt["a"], data1=lt["u"], initial=hprev, op0=ALU.mult, op1=ALU.add), ["a", "u", "hprev"], ["h"])
                V(lambda e: e.tensor_copy(out=hprev, in_=lt["h"][:, TB - 1:TB]), ["h"], ["hprev"])
                if need_y:
                    Gz = lg[sl]
                    A(lambda e, Gz=Gz: e.activation(out=lt["t1"], in_=Gz, func=AF.Square), [("lg", sl)], ["t1"])
                    V(lambda e: e.tensor_scalar(out=lt["t1"], in0=lt["t1"], scalar1=0.044715, scalar2=1.0, op0=ALU.mult, op1=ALU.add), ["t1"], ["t1"])
                    G(lambda e, Gz=Gz: e.tensor_tensor(out=lt["t1"], in0=lt["t1"], in1=Gz, op=ALU.mult), ["t1", ("lg", sl)], ["t1"])
                    A(lambda e: e.activation(out=lt["t2"], in_=lt["t1"], func=AF.Sigmoid, scale=1.5957691216057308), ["t1"], ["t2"])
                    G(lambda e, Gz=Gz: e.tensor_tensor(out=lt["t2"], in0=lt["t2"], in1=Gz, op=ALU.mult), ["t2", ("lg", sl)], ["t2"])
                    V(lambda e, sl=sl: e.tensor_tensor(out=ybf[sl], in0=lt["t2"], in1=lt["h"], op=ALU.mult), ["t2", "h"], [("ybf", sl)])
                    o = t0 - BASE
                    lo = max(0, -o)
                    dma(("ybf", sl), yTs[h, :, o + lo:o + TB], ybf[sl][:, lo:TB], [("ybf", sl)], [("yTs", h, t0)])
        P.barrier(bar[:, 1:2])

        reset()
        U = 256 if S >= 256 else S
        NCH = U // C
        NU = S // U
        w2p = tf(128); a2p = tf(128); g2p = tf(256).rearrange("p (k c) -> p k c", c=128)
        Zs = [tf(64) for _ in range(2)]
        ldn = ["r", "k", "v", "zw", "za", "g0", "g1"]
        ld = [{k: tf(U + 1) for k in ldn} for _ in range(2)]
        en = ["rs", "ks", "vs", "zws", "zas", "g0s", "g1s", "tmp", "tw", "lw", "av", "gv", "kk", "sq", "k2", "be", "cum", "Ein", "Eni", "Epv", "Een",
              "KT", "BT", "KH", "BH", "rk", "bon", "Lm", "P0", "P1", "GT", "PTs", "Ysb", "yc", "sq2", "yn"]
        et = {k: tf(U) for k in en}
        AR = tf(2 * U).rearrange("p (c q t) -> p c q t", q=2, t=C)
        MT = tf(4 * U).rearrange("p (c q t) -> p c q t", q=4, t=C)
        TM = tf(4 * U).rearrange("p (c q t) -> p c q t", q=4, t=C)
        Xa = [tf(2 * U).rearrange("p (c q t) -> p c q t", q=2, t=C) for _ in range(2)]
        PP = [tf(2 * U).rearrange("p (c q t) -> p c q t", q=2, t=C) for _ in range(2)]
        yob = [tb(U) for _ in range(2)]
        mkc = tf(U)
        for i in range(max(1, U // 128)):
            G(lambda e, i=i: e.tensor_copy(out=mkc[:, i * 128:min(U, (i + 1) * 128)], in_=MK_C[:, 0:min(U, 128)]), ["cst"], ["mkc"])
        v3 = lambda ap: ap.rearrange("p (c t) -> p c t", t=C)
        H2 = [slice(0, 64), slice(64, 128)]
        ui = 0
        for p in range(RP if STOP >= 3 else 0):
            pcols = slice(p * 128, (p + 1) * 128)
            zstate = {"i": 0}
            dma("w2p", w2p[0:96, :], w2_in[:, pcols], [], ["w2p"])
            dma("a2p", a2p[0:96, :], a2_in[:, pcols], [], ["a2p"])
            dma("g2p", g2p, g2_in[:, pcols].rearrange("(k q) c -> q k c", q=128), [], ["g2p"])
            for u in range(NU):
              try:
                t0 = u * U
                need_y = t0 + U > BASE
                sl = ui % 2
                ui += 1
                L = ld[sl]
                cts = {"r": 2 * LH + p, "k": 2 * LH + RP + p, "v": 2 * LH + 2 * RP + p, "zw": 2 * LH + 3 * RP, "za": 2 * LH + 3 * RP + 1,
                       "g0": 2 * LH + 3 * RP + 2, "g1": 2 * LH + 3 * RP + 3}
                rows = {"r": 128, "k": 128, "v": 128, "zw": 96, "za": 96, "g0": 128, "g1": 128}
                mucol = {"r": p, "k": RP + p, "v": 2 * RP + p, "zw": 3 * RP, "za": 3 * RP + 1, "g0": 3 * RP + 2, "g1": 3 * RP + 3}
                outn = {"r": "rs", "k": "ks", "v": "vs", "zw": "zws", "za": "zas", "g0": "g0s", "g1": "g1s"}
                for k in ldn:
                    if k in ("g0", "g1") and not need_y:
                        continue
                    R = rows[k]
                    ct = cts[k]
                    if t0 == 0:
                        G(lambda e, k=k, R=R: e.memset(L[k][0:R, 0:1], 0.0), [], [("ld", sl, k)])
                        dma(("ld", sl, k), L[k][0:R, 1:U + 1], zTl[ct][0:R, 0:U], [("zT", ct, j) for j in range(max(1, U // 256))], [("ld", sl, k)])
                    else:
                        dma(("ld", sl, k), L[k][0:R, 0:U + 1], zTl[ct][0:R, t0 - 1:t0 + U], [("zT", ct, j) for j in range((t0 - 1) // 256, (t0 + U - 1) // 256 + 1)], [("ld", sl, k)])
                    tmpk = "tmp_" + k
                    G(lambda e, k=k, R=R: e.tensor_tensor(out=et["tmp"][0:R, :], in0=L[k][0:R, 0:U], in1=L[k][0:R, 1:U + 1], op=ALU.subtract), [("ld", sl, k)], ["tmp"])
                    V(lambda e, k=k, R=R: e.scalar_tensor_tensor(out=et[outn[k]][0:R, :], in0=et["tmp"][0:R, :], scalar=pv[0:R, cfg.o_mu + mucol[k]:cfg.o_mu + mucol[k] + 1],
                                                                 in1=L[k][0:R, 1:U + 1], op0=ALU.mult, op1=ALU.add), ["tmp", ("ld", sl, k), "pv"], [outn[k]])
                if STOP < 3.05:
                    raise _Stop()
                A(lambda e: e.activation(out=et["tw"][0:96, :], in_=et["zws"][0:96, :], func=AF.Tanh), ["zws"], ["tw"])
                T(lambda e: e.matmul(banks[0][:, 0:U], lhsT=w2p[0:96, :], rhs=et["tw"][0:96, :], start=True, stop=True), ["w2p", "tw"], [("bank", 0)])
                T(lambda e: e.matmul(banks[0][:, U:2 * U], lhsT=a2p[0:96, :], rhs=et["zas"][0:96, :], start=True, stop=True), ["a2p", "zas"], [("bank", 0)])
                A(lambda e, p=p: e.activation(out=et["lw"], in_=banks[0][:, 0:U], func=AF.Sigmoid, bias=pc(cfg.o_w0 + p)), [("bank", 0), "pv"], ["lw"])
                A(lambda e, p=p: e.activation(out=et["av"], in_=banks[0][:, U:2 * U], func=AF.Sigmoid, bias=pc(cfg.o_a0 + p)), [("bank", 0), "pv"], ["av"])
                V(lambda e: e.tensor_scalar(out=et["lw"], in0=et["lw"], scalar1=-0.6065306597126334, scalar2=None, op0=ALU.mult), ["lw"], ["lw"])
                if need_y:
                    A(lambda e: e.activation(out=et["g0s"], in_=et["g0s"], func=AF.Sigmoid), ["g0s"], ["g0s"])
                    A(lambda e: e.activation(out=et["g1s"], in_=et["g1s"], func=AF.Sigmoid), ["g1s"], ["g1s"])
                    T(lambda e: e.matmul(banks[1][:, 0:U], lhsT=g2p[:, 0, :], rhs=et["g0s"], start=True, stop=False), ["g2p", "g0s"], [("bank", 1)])
                    T(lambda e: e.matmul(banks[1][:, 0:U], lhsT=g2p[:, 1, :], rhs=et["g1s"], start=False, stop=True), ["g2p", "g1s"], [("bank", 1)])
                    A(lambda e: e.copy(out=et["gv"], in_=banks[1][:, 0:U]), [("bank", 1)], ["gv"])
                if STOP < 3.1:
                    raise _Stop()
                V(lambda e, p=p: e.tensor_scalar(out=et["kk"], in0=et["ks"], scalar1=pc(cfg.o_kk + p), scalar2=None, op0=ALU.mult), ["ks", "pv"], ["kk"])
                A(lambda e: e.activation(out=et["sq"], in_=et["kk"], func=AF.Square), ["kk"], ["sq"])
                T(lambda e: e.matmul(banks[1][:, U:2 * U], lhsT=BD, rhs=et["sq"], start=True, stop=True), ["cst", "sq"], [("bank", 1)])
                A(lambda e: e.activation(out=et["sq"], in_=banks[1][:, U:2 * U], func=AF.Sqrt), [("bank", 1)], ["sq"])
                V(lambda e: e.tensor_scalar(out=et["sq"], in0=et["sq"], scalar1=1e-12, scalar2=None, op0=ALU.max), ["sq"], ["sq"])
                V(lambda e: e.reciprocal(out=et["sq"], in_=et["sq"]), ["sq"], ["sq"])
                V(lambda e: e.tensor_tensor(out=et["kk"], in0=et["kk"], in1=et["sq"], op=ALU.mult), ["kk", "sq"], ["kk"])
                V(lambda e, p=p: e.tensor_scalar(out=et["k2"], in0=et["av"], scalar1=-1.0, scalar2=pc(cfg.o_ka + p), op0=ALU.add, op1=ALU.mult), ["av", "pv"], ["k2"])
                V(lambda e: e.scalar_tensor_tensor(out=et["k2"], in0=et["k2"], scalar=1.0, in1=et["ks"], op0=ALU.add, op1=ALU.mult), ["k2", "ks"], ["k2"])
                G(lambda e: e.tensor_tensor(out=et["be"], in0=et["kk"], in1=et["av"], op=ALU.mult), ["kk", "av"], ["be"])
                if STOP < 3.15:
                    raise _Stop()
                V(lambda e: e.tensor_tensor_scan(out=et["cum"], data0=mkc, data1=et["lw"], initial=0.0, op0=ALU.mult, op1=ALU.add), ["lw", "mkc"], ["cum"])
                A(lambda e: e.activation(out=et["Ein"], in_=et["cum"], func=AF.Exp), ["cum"], ["Ein"])
                A(lambda e: e.activation(out=et["Eni"], in_=et["cum"], func=AF.Exp, scale=-1.0), ["cum"], ["Eni"])
                G(lambda e: e.tensor_tensor(out=et["Epv"], in0=et["cum"], in1=et["lw"], op=ALU.subtract), ["cum", "lw"], ["Epv"])
                A(lambda e: e.activation(out=et["Epv"], in_=et["Epv"], func=AF.Exp), ["Epv"], ["Epv"])
                for c in range(NCH):
                    V(lambda e, c=c: e.tensor_scalar(out=v3(et["Een"])[:, c, :], in0=v3(et["cum"])[:, c, :], scalar1=v3(et["cum"])[:, c, C - 1:C], scalar2=-1.0,
                                                     op0=ALU.subtract, op1=ALU.mult), ["cum"], ["Een"])
                A(lambda e: e.activation(out=et["Een"], in_=et["Een"], func=AF.Exp), ["Een"], ["Een"])
                if STOP < 3.17:
                    raise _Stop()
                V(lambda e: e.scalar_tensor_tensor(out=AR[:, :, 0, :], in0=v3(et["kk"]), scalar=-1.0, in1=v3(et["Epv"]), op0=ALU.mult, op1=ALU.mult), ["kk", "Epv"], ["AR0"])
                G(lambda e: e.tensor_tensor(out=AR[:, :, 1, :], in0=v3(et["rs"]), in1=v3(et["Ein"]), op=ALU.mult), ["rs", "Ein"], ["AR1"])
                G(lambda e: e.tensor_tensor(out=et["KT"], in0=et["k2"], in1=et["Eni"], op=ALU.mult), ["k2", "Eni"], ["KT"])
                V(lambda e: e.tensor_tensor(out=et["BT"], in0=et["be"], in1=et["Eni"], op=ALU.mult), ["be", "Eni"], ["BT"])
                G(lambda e: e.tensor_tensor(out=et["KH"], in0=et["k2"], in1=et["Een"], op=ALU.mult), ["k2", "Een"], ["KH"])
                V(lambda e: e.tensor_tensor(out=et["BH"], in0=et["be"], in1=et["Een"], op=ALU.mult), ["be", "Een"], ["BH"])
                if need_y:
                    V(lambda e, p=p: e.scalar_tensor_tensor(out=et["rk"], in0=et["rs"], scalar=pc(cfg.o_rk + p), in1=et["k2"], op0=ALU.mult, op1=ALU.mult), ["rs", "k2", "pv"], ["rk"])
                    T(lambda e: e.matmul(banks[2][:, 0:U], lhsT=BD, rhs=et["rk"], start=True, stop=True), ["cst", "rk"], [("bank", 2)])
                    V(lambda e: e.tensor_tensor(out=et["bon"], in0=banks[2][:, 0:U], in1=et["vs"], op=ALU.mult), [("bank", 2), "vs"], ["bon"])
                if STOP < 3.2:
                    raise _Stop()
                KTv, BTv, KHv, BHv, Vv = v3(et["KT"]), v3(et["BT"]), v3(et["KH"]), v3(et["BH"]), v3(et["vs"])
                pL = banks[3][:, 0:U].rearrange("p (c t) -> p c t", t=C)
                pM = [banks[4][:, 0:2 * U].rearrange("p (c q t) -> p c q t", q=2, t=C), banks[5][:, 0:2 * U].rearrange("p (c q t) -> p c q t", q=2, t=C)]
                for c in range(NCH):
                    for hh in H2:
                        T(lambda e, c=c, hh=hh: e.matmul(pL[hh, c, :], lhsT=AR[hh, c, 0, :], rhs=BTv[hh, c, :], start=True, stop=True), ["AR0", "BT"], [("bank", 3)])
                        T(lambda e, c=c, hh=hh: e.matmul(pM[0][hh, c, :, :], lhsT=BTv[hh, c, :], rhs=AR[hh, c, :, :], start=True, stop=True), ["AR0", "AR1", "BT"], [("bank", 4)])
                        T(lambda e, c=c, hh=hh: e.matmul(pM[1][hh, c, :, :], lhsT=KTv[hh, c, :], rhs=AR[hh, c, :, :], start=True, stop=True), ["AR0", "AR1", "KT"], [("bank", 5)])
                V(lambda e: e.tensor_tensor(out=v3(et["Lm"]), in0=pL, in1=MK_L.unsqueeze(1).broadcast_to([128, NCH, C]), op=ALU.mult), [("bank", 3), "cst"], ["Lm"])
                mk4 = MK_T.rearrange("p (q t) -> p q t", t=C).unsqueeze(1).broadcast_to([128, NCH, 2, C])
                V(lambda e: e.tensor_tensor(out=MT[:, :, 0:2, :], in0=pM[0], in1=mk4, op=ALU.mult), [("bank", 4), "cst"], ["MT01"])
                V(lambda e: e.tensor_tensor(out=MT[:, :, 2:4, :], in0=pM[1], in1=mk4, op=ALU.mult), [("bank", 5), "cst"], ["MT23"])
                if STOP < 3.3:
                    raise _Stop()
                pT = [banks[6][:, 0:2 * U].rearrange("p (c q t) -> p c q t", q=2, t=C), banks[7][:, 0:2 * U].rearrange("p (c q t) -> p c q t", q=2, t=C)]
                srcs = [(AR[:, :, 0, :], "AR0"), (Vv, "vs"), (BHv, "BH"), (KHv, "KH")]
                for c in range(NCH):
                    for hh in H2:
                        for q, (sap, skey) in enumerate(srcs):
                            T(lambda e, c=c, hh=hh, q=q, sap=sap: e.matmul(pT[q // 2][hh, c, q % 2, :], lhsT=sap[hh, c, :], rhs=ident[hh, hh], start=True, stop=True),
                              [skey, "cst"], [("bank", 6 + q // 2)])
                A(lambda e: e.copy(out=TM[:, :, 0:2, :], in_=pT[0]), [("bank", 6)], ["TM01"])
                A(lambda e: e.copy(out=TM[:, :, 2:4, :], in_=pT[1]), [("bank", 7)], ["TM23"])
                if STOP < 3.35:
                    raise _Stop()
                pX = banks[4][:, 0:2 * U].rearrange("p (c q t) -> p c q t", q=2, t=C)
                pP = banks[5][:, 0:2 * U].rearrange("p (c q t) -> p c q t", q=2, t=C)
                for c in range(NCH):
                    for hh in H2:
                        T(lambda e, c=c, hh=hh: e.matmul(pX[hh, c, 1, :], lhsT=MT[hh, c, 2, :], rhs=TM[hh, c, 1, :], start=True, stop=True), ["MT23", "TM01"], [("bank", 4)])
                G(lambda e: e.tensor_copy(out=Xa[0][:, :, 0, :], in_=TM[:, :, 0, :]), ["TM01"], [("Xa", 0)])
                V(lambda e: e.tensor_copy(out=Xa[0][:, :, 1, :], in_=pX[:, :, 1, :]), [("bank", 4)], [("Xa", 0)])
                if STOP < 3.4:
                    raise _Stop()
                curP = v3(et["Lm"]); curPT = MT[:, :, 0, :]; kP, kPT = "Lm", "MT01"
                for lvl in range(int(os.environ.get('KLVL', '6'))):
                    xi, xo = lvl % 2, (lvl + 1) % 2
                    for c in range(NCH):
                        for hh in H2:
                            T(lambda e, c=c, hh=hh, curPT=curPT, xi=xi: e.matmul(pX[hh, c, :, :], lhsT=(BTv if KEXP == 'B' else curPT)[hh, c, :], rhs=(AR if KEXP == 'A' else Xa[xi])[hh, c, :, :], start=True, stop=True),
                              [kPT, ("Xa", xi)], [("bank", 4)])
                    if os.environ.get('KADD', '1') == '1':
                        V(lambda e, xi=xi, xo=xo: e.tensor_tensor(out=Xa[xo], in0=pX, in1=Xa[xi], op=ALU.add), [("bank", 4), ("Xa", xi)], [("Xa", xo)])
                    if lvl < 5 and os.environ.get('KSQ', '1') == '1':
                        last = lvl == 4
                        for c in range(NCH):
                            for hh in H2:
                                if not last:
                                    T(lambda e, c=c, hh=hh, curP=curP, curPT=curPT: e.matmul(pP[hh, c, 0, :], lhsT=curPT[hh, c, :], rhs=curP[hh, c, :], start=True, stop=True),
                                      [kP, kPT], [("bank", 5)])
                                T(lambda e, c=c, hh=hh, curP=curP, curPT=curPT: e.matmul(pP[hh, c, 1, :], lhsT=curP[hh, c, :], rhs=curPT[hh, c, :], start=True, stop=True),
                                  [kP, kPT], [("bank", 5)])
                        ps = lvl % 2
                        if last:
                            A(lambda e, ps=ps: e.copy(out=PP[ps][:, :, 1, :], in_=pP[:, :, 1, :]), [("bank", 5)], [("PP", ps)])
                        else:
                            A(lambda e, ps=ps: e.copy(out=PP[ps], in_=pP), [("bank", 5)], [("PP", ps)])
                        curP = PP[ps][:, :, 0, :]; curPT = PP[ps][:, :, 1, :]; kP = kPT = ("PP", ps)
                XF = Xa[0]
                kXF = ("Xa", 0)
                if STOP < 3.5:
                    raise _Stop()
                pG = banks[3][:, 0:2 * U].rearrange("p (c q t) -> p c q t", q=2, t=C)
                for c in range(NCH):
                    for hh in H2:
                        if need_y:
                            T(lambda e, c=c, hh=hh: e.matmul(pG[hh, c, 0, :], lhsT=XF[hh, c, 0, :], rhs=MT[hh, c, 1, :], start=True, stop=True), [kXF, "MT01"], [("bank", 3)])
                        T(lambda e, c=c, hh=hh: e.matmul(pG[hh, c, 1, :], lhsT=XF[hh, c, 0, :], rhs=TM[hh, c, 2, :], start=True, stop=True), [kXF, "TM23"], [("bank", 3)])
                KS5 = os.environ.get('KS5', 'va')
                if need_y and 'v' in KS5:
                    V(lambda e: e.tensor_tensor(out=v3(et["GT"]), in0=pG[:, :, 0, :], in1=AR[:, :, 1, :], op=ALU.add), [("bank", 3), "AR1"], ["GT"])
                if 'a' in KS5:
                    V(lambda e: e.tensor_copy(out=v3(et["PTs"]), in_=pG[:, :, 1, :]), [("bank", 3)], ["PTs"])
                if STOP < 3.6:
                    raise _Stop()
                pZ = banks[6][:, 0:U].rearrange("p (c t) -> p c t", t=C)
                pY = banks[7][:, 0:U].rearrange("p (c t) -> p c t", t=C)
                GTv, PTv = v3(et["GT"]), v3(et["PTs"])
                Einv = v3(et["Ein"])
                for c in range(NCH):
                    first = (u == 0 and c == 0)
                    zi = zstate["i"] % 2
                    zo = (zstate["i"] + 1) % 2
                    zstate["i"] += 1
                    if first:
                        G(lambda e, zi=zi: e.memset(Zs[zi], 0.0), [], [("Z", zi)])
                    for hh in H2:
                        T(lambda e, c=c, hh=hh: e.matmul(pZ[hh, c, :], lhsT=TM[hh, c, 2, :], rhs=XF[hh, c, 1, :], start=True, stop=False), ["TM23", kXF], [("bank", 6)])
                        T(lambda e, c=c, hh=hh: e.matmul(pZ[hh, c, :], lhsT=TM[hh, c, 3, :], rhs=TM[hh, c, 1, :], start=False, stop=False), ["TM23", "TM01"], [("bank", 6)])
                        T(lambda e, c=c, hh=hh, zi=zi: e.matmul(pZ[hh, c, :], lhsT=PTv[hh, c, :], rhs=Zs[zi][hh, :], start=False, stop=True), ["PTs", ("Z", zi)], [("bank", 6)])
                        if need_y:
                            T(lambda e, c=c, hh=hh: e.matmul(pY[hh, c, :], lhsT=XF[hh, c, 1, :], rhs=MT[hh, c, 1, :], start=True, stop=False), [kXF, "MT01"], [("bank", 7)])
                            T(lambda e, c=c, hh=hh: e.matmul(pY[hh, c, :], lhsT=TM[hh, c, 1, :], rhs=MT[hh, c, 3, :], start=False, stop=False), ["TM01", "MT23"], [("bank", 7)])
                            T(lambda e, c=c, hh=hh, zi=zi: e.matmul(pY[hh, c, :], lhsT=Zs[zi][hh, :], rhs=GTv[hh, c, :], start=False, stop=True), [("Z", zi), "GT"], [("bank", 7)])
                    V(lambda e, c=c, zi=zi, zo=zo: e.scalar_tensor_tensor(out=Zs[zo], in0=Zs[zi], scalar=Einv[:, c, C - 1:C], in1=pZ[:, c, :], op0=ALU.mult, op1=ALU.add),
                      [("Z", zi), "Ein", ("bank", 6)], [("Z", zo)])
                if STOP < 3.7:
                    raise _Stop()
                if need_y:
                    A(lambda e: e.copy(out=et["Ysb"], in_=banks[7][:, 0:U]), [("bank", 7)], ["Ysb"])
                    T(lambda e: e.matmul(banks[2][:, 0:U], lhsT=BDS, rhs=et["Ysb"], start=True, stop=True), ["cst", "Ysb"], [("bank", 2)])
                    V(lambda e: e.tensor_tensor(out=et["yc"], in0=et["Ysb"], in1=banks[2][:, 0:U], op=ALU.subtract), ["Ysb", ("bank", 2)], ["yc"])
                    A(lambda e: e.activation(out=et["sq2"], in_=et["yc"], func=AF.Square), ["yc"], ["sq2"])
                    T(lambda e: e.matmul(banks[2][:, U:2 * U], lhsT=BDS, rhs=et["sq2"], start=True, stop=True), ["cst", "sq2"], [("bank", 2)])
                    V(lambda e: e.tensor_scalar(out=et["sq2"], in0=banks[2][:, U:2 * U], scalar1=GN_EPS, scalar2=None, op0=ALU.add), [("bank", 2)], ["sq2"])
                    A(lambda e: e.activation(out=et["sq2"], in_=et["sq2"], func=AF.Sqrt), ["sq2"], ["sq2"])
                    V(lambda e: e.reciprocal(out=et["sq2"], in_=et["sq2"]), ["sq2"], ["sq2"])
                    V(lambda e: e.tensor_tensor(out=et["yn"], in0=et["yc"], in1=et["sq2"], op=ALU.mult), ["yc", "sq2"], ["yn"])
                    V(lambda e, p=p: e.tensor_scalar(out=et["yn"], in0=et["yn"], scalar1=pc(cfg.o_gw + p), scalar2=pc(cfg.o_gb + p), op0=ALU.mult, op1=ALU.add), ["yn", "pv"], ["yn"])
                    G(lambda e: e.tensor_tensor(out=et["yn"], in0=et["yn"], in1=et["bon"], op=ALU.add), ["yn", "bon"], ["yn"])
                    ys = ui % 2
                    V(lambda e, ys=ys: e.tensor_tensor(out=yob[ys], in0=et["yn"], in1=et["gv"], op=ALU.mult), ["yn", "gv"], [("yob", ys)])
                    o = t0 - BASE
                    lo = max(0, -o)
                    dma(("yob", ys), yTs[LH + p, :, o + lo:o + U], yob[ys][:, lo:U], [("yob", ys)], [("yTs", LH + p, t0)])
              except _Stop:
                pass
        P.barrier(bar[:, 2:3])

        reset()
        TBC = cfg.TBC
        wst = [tf(32 * 128) for _ in range(2)]
        xt_ = [tf(TBC) for _ in range(4)]
        sqc = [tf(TBC) for _ in range(2)]
        rst = tf(TBC)
        gat = [tf(TBC + 8) for _ in range(2)]
        upt = [tf(TBC) for _ in range(2)]
        sgt = [tf(TBC) for _ in range(2)]
        ot = [tf(512) for _ in range(2)]
        ghs = tf(NF * 2).rearrange("p (f t) -> p f t", t=2)
        pm2 = tf(8)
        wbf = [tb(32 * 128) for _ in range(2)]
        h2T = tb(NDC * TBC).rearrange("p (d t) -> p d t", t=TBC)
        R1 = tb(max(NF, NKM) * TBC)
        yTb = R1[:, 0:NKM * TBC].rearrange("p (k t) -> p k t", t=TBC)
        actT = R1[:, 0:NF * TBC].rearrange("p (k t) -> p k t", t=TBC)
        dma("pm2", pm2[:, 0:2], pmask[:, S - NT - 2:S - NT], [], ["pm2"])
        cnt = {"x": 0, "p": 0}
        blocks = [(0, EXT, True)] + [(EXT + i * TBC, TBC, False) for i in range(NT // TBC)]
        for (so, n, halo_only) in (blocks if STOP >= 4 else []):
            for ka in range(0, NKM, 8):
                kb = min(NKM, ka + 8)
                dma("yTb", yTb[:, ka:kb, 0:n], yTs[ka:kb, :, so:so + n].rearrange("k p t -> p k t"), [], ["yTb"])
            for d in range(NDC):
                def sink(sb, ps, key, d=d):
                    s4 = cnt["x"] % 4
                    s2 = cnt["x"] % 2
                    cnt["x"] += 1
                    X_ = xt_[s4][:, 0:n]
                    dma(("xt", s4), X_, xTs[d, :, so:so + n], [], [("xt", s4)])
                    V(lambda e: e.tensor_tensor(out=X_, in0=ps, in1=X_, op=ALU.add), [key, ("xt", s4)], [("xt", s4)])
                    dma(("xt", s4), x1s[d, :, so:so + n], X_, [("xt", s4)], [("x1s", d, so)])
                    A(lambda e: e.activation(out=sqc[s2][:, 0:n], in_=X_, func=AF.Square), [("xt", s4)], [("sqc", s2)])
                    T(lambda e: e.matmul(banks[0][:, 0:n], lhsT=ONES, rhs=sqc[s2][:, 0:n], start=(d == 0), stop=(d == NDC - 1)), [("sqc", s2), "cst"], [("bank", 0)])
                cnt["p"] += 1
                project(w_out, d * 128, 128, NKM, lambda k, sb: yTb[:, k, 0:n], ["yTb"], 1, n, sink, wst, wbf, [1, 2], cnt["p"])
            V(lambda e: e.tensor_scalar(out=rst[:, 0:n], in0=banks[0][:, 0:n], scalar1=1.0 / D, scalar2=NORM_EPS, op0=ALU.mult, op1=ALU.add), [("bank", 0)], ["rst"])
            A(lambda e: e.activation(out=rst[:, 0:n], in_=rst[:, 0:n], func=AF.Sqrt), ["rst"], ["rst"])
            V(lambda e: e.reciprocal(out=rst[:, 0:n], in_=rst[:, 0:n]), ["rst"], ["rst"])
            for d in range(NDC):
                s4 = cnt["x"] % 4
                cnt["x"] += 1
                dma(("xt", s4), xt_[s4][:, 0:n], x1s[d, :, so:so + n], [("x1s", d, so)], [("xt", s4)])
                V(lambda e, d=d, s4=s4: e.scalar_tensor_tensor(out=h2T[:, d, 0:n], in0=xt_[s4][:, 0:n], scalar=pc(cfg.o_gffn + d), in1=rst[:, 0:n], op0=ALU.mult, op1=ALU.mult),
                  [("xt", s4), "pv", "rst"], ["h2T"])
            P.barrier(bar[:, 3:4])
            for f in range(NF):
                s2 = f % 2
                G_ = gat[s2]
                def sink_g(sb, ps, key, G_=G_, s2=s2):
                    A(lambda e: e.copy(out=G_[:, 2:n + 2], in_=ps), [key], [("gat", s2)])
                def sink_u(sb, ps, key, s2=s2):
                    A(lambda e: e.copy(out=upt[s2][:, 0:n], in_=ps), [key], [("upt", s2)])
                cnt["p"] += 1
                project(w_gate, f * 128, 128, NDC, lambda k, sb: h2T[:, k, 0:n], ["h2T"], 1, n, sink_g, wst, wbf, [1, 2], cnt["p"])
                if halo_only:
                    V(lambda e, G_=G_, f=f: e.tensor_tensor(out=ghs[:, f, :], in0=G_[:, n:n + 2], in1=pm2[:, 0:2], op=ALU.mult), [("gat", s2), "pm2"], ["ghs"])
                    continue
                cnt["p"] += 1
                project(w_up, f * 128, 128, NDC, lambda k, sb: h2T[:, k, 0:n], ["h2T"], 1, n, sink_u, wst, wbf, [1, 2], cnt["p"])
                V(lambda e, G_=G_, f=f: e.tensor_copy(out=G_[:, 0:2], in_=ghs[:, f, :]), ["ghs", ("gat", s2)], [("gat", s2)])
                V(lambda e, G_=G_, f=f: e.tensor_copy(out=ghs[:, f, :], in_=G_[:, n:n + 2]), [("gat", s2)], ["ghs"])
                A(lambda e, G_=G_, f=f, s2=s2: e.activation(out=sgt[s2][:, 0:n], in_=G_[:, 2:n + 2], func=AF.Identity, scale=pc(cfg.o_fw + 2 * NF + f), bias=pc(cfg.o_fb + f)),
                  [("gat", s2), "pv"], [("sgt", s2)])
                for j in range(2):
                    V(lambda e, G_=G_, f=f, j=j, s2=s2: e.scalar_tensor_tensor(out=sgt[s2][:, 0:n], in0=G_[:, j:j + n], scalar=pc(cfg.o_fw + j * NF + f), in1=sgt[s2][:, 0:n], op0=ALU.mult, op1=ALU.add),
                      [("gat", s2), "pv", ("sgt", s2)], [("sgt", s2)])
                A(lambda e, s2=s2: e.activation(out=sgt[s2][:, 0:n], in_=sgt[s2][:, 0:n], func=AF.Silu), [("sgt", s2)], [("sgt", s2)])
                V(lambda e, f=f, s2=s2: e.tensor_tensor(out=actT[:, f, 0:n], in0=sgt[s2][:, 0:n], in1=upt[s2][:, 0:n], op=ALU.mult), [("sgt", s2), ("upt", s2)], ["actT"])
            if halo_only:
                P.barrier(bar[:, 4:5])
                continue
            for d in range(NDC):
                def sink_d(sb, ps, key, d=d):
                    s4 = cnt["x"] % 4
                    s2 = cnt["x"] % 2
                    cnt["x"] += 1
                    X_ = xt_[s4][:, 0:n]
                    dma(("xt", s4), X_, x1s[d, :, so:so + n], [("x1s", d, so)], [("xt", s4)])
                    V(lambda e: e.tensor_tensor(out=X_, in0=ps, in1=X_, op=ALU.add), [key, ("xt", s4)], [("xt", s4)])
                    dma(("xt", s4), x2s[d, :, so:so + n], X_, [("xt", s4)], [("x2s", d, so)])
                    A(lambda e: e.activation(out=sqc[s2][:, 0:n], in_=X_, func=AF.Square), [("xt", s4)], [("sqc", s2)])
                    T(lambda e: e.matmul(banks[0][:, 0:n], lhsT=ONES, rhs=sqc[s2][:, 0:n], start=(d == 0), stop=(d == NDC - 1)), [("sqc", s2), "cst"], [("bank", 0)])
                project(w_down, d * 128, 128, NF, lambda k, sb: actT[:, k, 0:n], ["actT"], 1, n, sink_d, wst, wbf, [1 + d % 2], d)
            V(lambda e: e.tensor_scalar(out=rst[:, 0:n], in0=banks[0][:, 0:n], scalar1=1.0 / D, scalar2=NORM_EPS, op0=ALU.mult, op1=ALU.add), [("bank", 0)], ["rst"])
            A(lambda e: e.activation(out=rst[:, 0:n], in_=rst[:, 0:n], func=AF.Sqrt), ["rst"], ["rst"])
            V(lambda e: e.reciprocal(out=rst[:, 0:n], in_=rst[:, 0:n]), ["rst"], ["rst"])
            orow = so - EXT
            for d in range(NDC):
                s4 = cnt["x"] % 4
                cnt["x"] += 1
                X_ = xt_[s4][:, 0:n]
                dma(("xt", s4), X_, x2s[d, :, so:so + n], [("x2s", d, so)], [("xt", s4)])
                V(lambda e, d=d, X_=X_: e.scalar_tensor_tensor(out=X_, in0=X_, scalar=pc(cfg.o_gfin + d), in1=rst[:, 0:n], op0=ALU.mult, op1=ALU.mult),
                  [("xt", s4), "pv", "rst"], [("xt", s4)])
                bk = 3 + d % 2
                for tt in range(n // 128):
                    T(lambda e, X_=X_, tt=tt, bk=bk: e.matmul(banks[bk][:, tt * 128:(tt + 1) * 128], lhsT=X_[:, tt * 128:(tt + 1) * 128], rhs=ident, start=True, stop=True),
                      [("xt", s4), "cst"], [("bank", bk)])
                os_ = d % 2
                A(lambda e, bk=bk, os_=os_: e.copy(out=ot[os_][:, 0:n], in_=banks[bk][:, 0:n]), [("bank", bk)], [("ot", os_)])
                dma(("ot", os_), out[orow:orow + n, d * 128:(d + 1) * 128].rearrange("(tt p) c -> p tt c", p=128), ot[os_][:, 0:n].rearrange("p (tt c) -> p tt c", c=128),
                    [("ot", os_)], [])
            P.barrier(bar[:, 5:6])
        P.emit(st)
    return nc


def run(cfg, inp, B, SEQ):
    S, NT, D = cfg.S, cfg.NT, cfg.D
    NQ = SEQ // NT
    x = np.asarray(inp["x"], np.float32)
    f = lambda k: np.ascontiguousarray(np.asarray(inp[k], np.float32)[0])
    shared = {
        "consts": host_consts(), "pvec": host_pvec(cfg, {k: np.asarray(v, np.float32) for k, v in inp.items()}),
        "w_in": f("w_in"), "lru_wa": f("lru_w_a"), "lru_wi": f("lru_w_i"), "w2": f("rwkv_w2"), "a2": f("rwkv_a2"), "g2": f("rwkv_g2"),
        "w_out": f("w_out"), "w_gate": f("w_ffn_gate"), "w_up": f("w_ffn_up"), "w_down": f("w_ffn_down"),
    }
    in_maps = []
    for b in range(B):
        for q in range(NQ):
            n_real = (q + 1) * NT
            xp = np.zeros((S, D), np.float32)
            xp[S - n_real:] = x[b, :n_real]
            pm = np.zeros((128, S), np.float32)
            pm[:, S - n_real:] = 1.0
            m = dict(shared)
            m["x"] = xp
            m["pmask"] = pm
            in_maps.append(m)
    nc = build(cfg)
    res = run_bass_kernel_spmd(nc, in_maps, core_ids=list(range(len(in_maps))))
    out = np.zeros((B, SEQ, D), np.float32)
    i = 0
    for b in range(B):
        for q in range(NQ):
            out[b, q * NT:(q + 1) * NT] = res.results[i]["out"]
            i += 1
    return out


def kernel(**inputs):
    return run(REAL, inputs, 2, 8192)
```

```python
import contextlib
import os
import numpy as np
import concourse.bass as bass
import concourse.mybir as mybir
from concourse.bass_utils import run_bass_kernel_spmd

F32 = mybir.dt.float32
BF16 = mybir.dt.bfloat16
ALU = mybir.AluOpType
AF = mybir.ActivationFunctionType

SAME_ENGINE_SYNC = True
NORM_EPS = 1e-6
GN_EPS = 64e-5
C = 64


class _Op:
    __slots__ = ("eng", "fn", "deps", "idx", "is_dma", "sem_key", "sem_val", "needs_inc")


class _Stop(Exception):
    pass


class _Rec:
    def __init__(self):
        self.call = None

    def __getattr__(self, name):
        def f(*a, **k):
            self.call = (name, a, k)
            return None
        return f


class Prog:
    ENGINES = ("tensor", "vector", "scalar", "gpsimd", "sync")

    def __init__(self, nc):
        self.nc = nc
        self.ops = []
        self.last_writer = {}
        self.readers = {}
        self.dma_counts = {}

    def add(self, eng, fn, reads=(), writes=(), dma=None):
        op = _Op()
        op.eng = eng
        rec = _Rec()
        fn(rec)
        op.fn = rec.call
        op.idx = len(self.ops)
        op.is_dma = dma is not None
        op.sem_key = dma
        op.needs_inc = False
        op.sem_val = None
        reads = list(reads) + ["__phase__"]
        deps = set()
        for b in reads:
            lw = self.last_writer.get(b)
            if lw is not None:
                deps.add(lw)
        for b in writes:
            lw = self.last_writer.get(b)
            if lw is not None:
                deps.add(lw)
            for r in self.readers.get(b, ()):
                deps.add(r)
        deps.discard(op.idx)
        op.deps = deps
        for b in reads:
            self.readers.setdefault(b, []).append(op.idx)
        for b in writes:
            self.last_writer[b] = op.idx
            self.readers[b] = []
        if op.is_dma:
            c = self.dma_counts.get(dma, 0) + 16
            self.dma_counts[dma] = c
            op.sem_val = c
        self.ops.append(op)
        return op.idx

    def barrier(self, tile_ap):
        self.add("gpsimd", lambda e: e.memset(tile_ap, 0.0), reads=[], writes=["__phase__", "__bar__"])

    def emit(self, stack):
        nc = self.nc
        ops = self.ops
        for op in ops:
            for d in op.deps:
                p = ops[d]
                if p.is_dma:
                    continue
                if p.eng == op.eng and not op.is_dma and (p.eng == "tensor" or not SAME_ENGINE_SYNC):
                    continue
                p.needs_inc = True
        esem = {e: stack.enter_context(nc.semaphore("es_" + e)) for e in self.ENGINES}
        dsem = {}
        for i, k in enumerate(self.dma_counts):
            dsem[k] = stack.enter_context(nc.semaphore("ds%d" % i))
        cnt = {e: 0 for e in self.ENGINES}
        for op in ops:
            if not op.is_dma and op.needs_inc:
                cnt[op.eng] += 1
                op.sem_val = cnt[op.eng]
        per_eng = {e: [] for e in self.ENGINES}
        for op in ops:
            per_eng[op.eng].append(op)
        block = stack.enter_context(nc.Block())

        def make(e):
            def body(eng):
                waited = {}
                for op in per_eng[e]:
                    need = {}
                    for d in op.deps:
                        p = ops[d]
                        if p.is_dma:
                            s, v, key = dsem[p.sem_key], p.sem_val, ("d", p.sem_key)
                        else:
                            if not p.needs_inc:
                                continue
                            if p.eng == e and not op.is_dma and (e == "tensor" or not SAME_ENGINE_SYNC):
                                continue
                            s, v, key = esem[p.eng], p.sem_val, ("e", p.eng)
                        if waited.get(key, 0) >= v:
                            continue
                        if key not in need or need[key][1] < v:
                            need[key] = (s, v)
                    for key, (s, v) in need.items():
                        eng.wait_ge(s, v)
                        waited[key] = v
                    ins = getattr(eng, op.fn[0])(*op.fn[1], **op.fn[2])
                    if op.is_dma:
                        ins.then_inc(dsem[op.sem_key], 16)
                    elif op.needs_inc:
                        ins.then_inc(esem[e], 1)
                if e == "sync":
                    for k, c in self.dma_counts.items():
                        eng.wait_ge(dsem[k], c)
            return body

        block.tensor(make("tensor"))
        block.vector(make("vector"))
        block.scalar(make("scalar"))
        block.gpsimd(make("gpsimd"))
        block.sync(make("sync"))


class Cfg:
    def __init__(self, D, S, NT, LH, RP, DFF):
        self.D, self.S, self.NT, self.LH, self.RP, self.DFF = D, S, NT, LH, RP, DFF
        self.NDC = D // 128
        self.NF = DFF // 128
        self.NKM = LH + RP
        self.NCT = 2 * LH + 3 * RP + 4
        self.INW = 2 * LH * 128 + 3 * RP * 128 + 96 + 96 + 256
        self.TBW = min(1024, S)
        self.TBC = min(512, NT)
        o = 0
        def take(n):
            nonlocal o
            r = o
            o += n
            return r
        self.o_gmix = take(self.NDC); self.o_gffn = take(self.NDC); self.o_gfin = take(self.NDC)
        self.o_cw = take(4 * LH); self.o_cb = take(LH); self.o_ba = take(LH); self.o_bi = take(LH); self.o_lam = take(LH)
        self.o_mu = take(3 * RP + 4)
        self.o_w0 = take(RP); self.o_a0 = take(RP); self.o_kk = take(RP); self.o_ka = take(RP)
        self.o_rk = take(RP); self.o_gw = take(RP); self.o_gb = take(RP)
        self.o_fw = take(3 * self.NF); self.o_fb = take(self.NF)
        self.NP = o

    def coltile(self, ct):
        LH, RP = self.LH, self.RP
        nfull = 2 * LH + 3 * RP
        if ct < nfull:
            return ct * 128, 128
        base = nfull * 128
        return [(base, 96), (base + 96, 96), (base + 192, 128), (base + 320, 128)][ct - nfull]


REAL = Cfg(4096, 8192, 2048, 16, 16, 11008)


def host_consts():
    c = np.zeros((128, 7, 128), np.float32)
    c[:, 0, :] = np.eye(128)
    bd = np.zeros((128, 128), np.float32); bd[:64, :64] = 1; bd[64:, 64:] = 1
    c[:, 1, :] = bd
    c[:, 2, :] = bd / 64.0
    c[:, 3, :] = 1.0
    t = np.arange(128) % 64
    s = np.arange(64)
    c[:, 4, 0:64] = (s[None, :] < t[:, None])
    c[:, 5, 0:64] = (s[None, :] > t[:, None])
    c[:, 5, 64:128] = (s[None, :] >= t[:, None])
    c[:, 6, :] = 1.0
    c[:, 6, 0::64] = 0.0
    return c


def host_pvec(cfg, inp):
    LH, RP = cfg.LH, cfg.RP
    pv = np.zeros((128, cfg.NP), np.float32)
    def put(o, vec):
        v = np.asarray(vec, np.float32).reshape(-1, 128)
        pv[:, o:o + v.shape[0]] = v.T
    put(cfg.o_gmix, inp["g_mix"][0]); put(cfg.o_gffn, inp["g_ffn"][0]); put(cfg.o_gfin, inp["g_final"])
    for j in range(4):
        put(cfg.o_cw + j * LH, inp["conv_lru_w"][0, j])
    put(cfg.o_cb, inp["conv_lru_b"][0]); put(cfg.o_ba, inp["lru_b_a"][0].reshape(-1)); put(cfg.o_bi, inp["lru_b_i"][0].reshape(-1))
    put(cfg.o_lam, inp["lru_lambda"][0])
    mu = inp["rwkv_mu"][0]
    W = RP * 128
    put(cfg.o_mu, mu[:3 * W])
    for i, (a, n) in enumerate([(3 * W, 96), (3 * W + 96, 96), (3 * W + 192, 128), (3 * W + 320, 128)]):
        pv[:n, cfg.o_mu + 3 * RP + i] = mu[a:a + n]
    for o, k in [(cfg.o_w0, "rwkv_w0"), (cfg.o_a0, "rwkv_a0"), (cfg.o_kk, "rwkv_k_k"), (cfg.o_ka, "rwkv_k_a"),
                 (cfg.o_rk, "rwkv_r_k"), (cfg.o_gw, "rwkv_gn_w"), (cfg.o_gb, "rwkv_gn_b")]:
        put(o, inp[k][0])
    for j in range(3):
        put(cfg.o_fw + j * cfg.NF, inp["ffn_conv_w"][0, j])
    put(cfg.o_fb, inp["ffn_conv_b"][0])
    return pv


def build(cfg):
    D, S, NT, LH, RP, DFF = cfg.D, cfg.S, cfg.NT, cfg.LH, cfg.RP, cfg.DFF
    NDC, NF, NKM, NCT = cfg.NDC, cfg.NF, cfg.NKM, cfg.NCT
    EXT = 128
    BASE = S - NT - EXT
    NTE = NT + EXT
    assert BASE >= 0
    nc = bass.Bass("TRN2", target_bir_lowering=False)
    dt_in = lambda n, s, d=F32: nc.dram_tensor(n, s, d, kind="ExternalInput").ap()
    x_in = dt_in("x", [S, D])
    pmask = dt_in("pmask", [128, S])
    consts_in = dt_in("consts", [128, 7, 128])
    pvec_in = dt_in("pvec", [128, cfg.NP])
    w_in = dt_in("w_in", [D, cfg.INW])
    lru_wa = dt_in("lru_wa", [LH, 128, 128])
    lru_wi = dt_in("lru_wi", [LH, 128, 128])
    w2_in = dt_in("w2", [96, RP * 128])
    a2_in = dt_in("a2", [96, RP * 128])
    g2_in = dt_in("g2", [256, RP * 128])
    w_out = dt_in("w_out", [NKM * 128, D])
    w_gate = dt_in("w_gate", [D, DFF])
    w_up = dt_in("w_up", [D, DFF])
    w_down = dt_in("w_down", [DFF, D])
    out = nc.dram_tensor("out", [NT, D], F32, kind="ExternalOutput").ap()
    scr = lambda n, s, d=F32: nc.dram_tensor(n, s, d, kind="Internal").ap()
    zTl = [scr("zT%d" % i, [128, S]) for i in range(NCT)]
    xTs = scr("xTs", [NDC, 128, NTE])
    yTs = scr("yTs", [NKM, 128, NTE], BF16)
    x1s = scr("x1s", [NDC, 128, NTE])
    x2s = scr("x2s", [NDC, 128, NTE])
    Wb_in = scr("Wb_in", [NCT, 128, NDC, 128], BF16)
    Wb_out = scr("Wb_out", [NDC, 128, NKM, 128], BF16)
    Wb_gate = scr("Wb_gate", [NF, 128, NDC, 128], BF16)
    Wb_up = scr("Wb_up", [NF, 128, NDC, 128], BF16)
    Wb_down = scr("Wb_down", [NDC, 128, NF, 128], BF16)

    P = Prog(nc)
    with contextlib.ExitStack() as st:
        AF32 = 50400
        arena = nc.alloc_sbuf_tensor("arena", [128, AF32], F32)
        cst = nc.alloc_sbuf_tensor("cst", [128, 7, 128], F32)
        pv = nc.alloc_sbuf_tensor("pv", [128, cfg.NP], F32)
        dv = nc.alloc_sbuf_tensor("dv", [128, 4 * RP + 8 + LH * 2], F32)
        bar = nc.alloc_sbuf_tensor("bar", [128, 8], F32)
        banks = [nc.alloc_psum_tensor("pb%d" % i, [128, 512], F32) for i in range(8)]
        off = {"f": 0}

        def reset():
            off["f"] = 0

        def tf(n):
            n = (n + 7) // 8 * 8
            a = arena[:, off["f"]:off["f"] + n]
            off["f"] += n
            assert off["f"] <= AF32, off["f"]
            return a

        def tb(n):
            n = (n + 15) // 16 * 16
            return tf(n // 2).bitcast(BF16)

        ident = cst[:, 0, :]; BD = cst[:, 1, :]; BDS = cst[:, 2, :]; ONES = cst[:, 3, :]
        MK_L = cst[:, 4, 0:64]; MK_T = cst[:, 5, :]; MK_C = cst[:, 6, :]
        pc = lambda o: pv[:, o:o + 1]

        def dma(key, out_ap, in_ap, r, w, eng="sync"):
            P.add(eng, lambda e: e.dma_start(out=out_ap, in_=in_ap), reads=r, writes=w, dma=key)

        V = lambda fn, r, w: P.add("vector", fn, r, w)
        A = lambda fn, r, w: P.add("scalar", fn, r, w)
        G = lambda fn, r, w: P.add("gpsimd", fn, r, w)
        T = lambda fn, r, w: P.add("tensor", fn, r, w)
        rr = {"i": 0}

        def E(fn, r, w):
            rr["i"] += 1
            P.add("vector", fn, r, w)

        dma("cst", cst[:], consts_in, [], ["cst"])
        dma("pv", pv[:], pvec_in, [], ["pv"])
        o_c1 = 0; o_c2 = LH
        A(lambda e: e.activation(out=dv[:, 0:LH], in_=pv[:, cfg.o_lam:cfg.o_lam + LH], func=AF.Exp, scale=-1.0), ["pv"], ["dv"])
        A(lambda e: e.activation(out=dv[:, 0:LH], in_=dv[:, 0:LH], func=AF.Ln, bias=1.0), ["dv"], ["dv"])
        V(lambda e: e.tensor_scalar(out=dv[:, LH:2 * LH], in0=dv[:, 0:LH], scalar1=-16.0, scalar2=None, op0=ALU.mult), ["dv"], ["dv2"])
        V(lambda e: e.tensor_scalar(out=dv[:, 0:LH], in0=dv[:, 0:LH], scalar1=-8.0, scalar2=None, op0=ALU.mult), ["dv", "dv2"], ["dv"])
        dc = lambda o: dv[:, o:o + 1]

        wslot = {"i": 0}

        NWS = 4

        def project(Wb, ct, m, nk, rhs_fn, rhs_keys, nsub, n, sink, wst, wbf, bank_ids, tagc):
            k0 = 0
            pieces = []
            while k0 < nk:
                kn = min(32 if nk <= 32 else 29, nk - k0)
                pieces.append((k0, kn))
                k0 += kn
            assert len(pieces) == 1 or nsub <= len(bank_ids)
            for pi, (k0, kn) in enumerate(pieces):
                sl = wslot["i"] % NWS
                wslot["i"] += 1
                wb_ = wbf[sl].rearrange("p (k c) -> p k c", c=128)
                dma(("wbf", sl), wb_[:, 0:kn, :], Wb[ct, :, k0:k0 + kn, :], [], [("wbf", sl)])
                for sb in range(nsub):
                    bk = bank_ids[(tagc * nsub + sb) % len(bank_ids)] if len(pieces) == 1 else bank_ids[sb]
                    for k in range(kn):
                        kk = k0 + k
                        T(lambda e, bk=bk, wb_=wb_, k=k, kk=kk, sb=sb: e.matmul(banks[bk][0:m, 0:n], lhsT=wb_[:, k, 0:m], rhs=rhs_fn(kk, sb),
                                                                        start=(kk == 0), stop=(kk == nk - 1)),
                          [("wbf", sl)] + rhs_keys, [("bank", bk)])
                    if pi == len(pieces) - 1:
                        sink(sb, banks[bk][0:m, 0:n], ("bank", bk))

        reset()
        wsW = [tf(8 * 512) for _ in range(3)]
        wbW = [tb(8 * 512) for _ in range(3)]
        wcnt = {"i": 0}
        cast_eng = ["gpsimd", "scalar", "scalar"]

        def precast(W, Wb, nk, tiles):
            groups = []
            i = 0
            while i < len(tiles):
                if tiles[i][2] == 128:
                    j = i
                    while j < len(tiles) and j - i < 4 and tiles[j][2] == 128 and tiles[j][1] == tiles[i][1] + (j - i) * 128:
                        j += 1
                    groups.append(tiles[i:j])
                    i = j
                else:
                    groups.append(tiles[i:i + 1])
                    i += 1
            for grp in groups:
                ct0, c0, _ = grp[0]
                ng = len(grp)
                mm = grp[0][2] if ng == 1 else 128
                width = mm if ng == 1 else ng * 128
                for k0 in range(0, nk, 8):
                    kn = min(8, nk - k0)
                    sl = wcnt["i"] % 3
                    eng = cast_eng[wcnt["i"] % 3]
                    wcnt["i"] += 1
                    src = wsW[sl][:, 0:kn * width].rearrange("p (k c) -> p k c", c=width)
                    dma(("wsW", sl), src, W[k0 * 128:(k0 + kn) * 128, c0:c0 + width].rearrange("(k p) c -> p k c", p=128), [], [("wsW", sl)])
                    dst = wbW[sl][:, 0:ng * kn * 128].rearrange("p (j k c) -> p j k c", j=ng, c=128)
                    srcv = src if ng == 1 else None
                    if ng == 1:
                        o_ap, i_ap = dst[:, 0, :, 0:mm], src
                    else:
                        o_ap, i_ap = dst, src.rearrange("p k (j c) -> p j k c", c=128)
                    if eng == "scalar":
                        P.add(eng, lambda e: e.copy(out=o_ap, in_=i_ap), [("wsW", sl)], [("wbW", sl)])
                    else:
                        P.add(eng, lambda e: e.tensor_copy(out=o_ap, in_=i_ap), [("wsW", sl)], [("wbW", sl)])
                    if ng == 1:
                        dma(("wbW", sl), Wb[ct0, :, k0:k0 + kn, 0:mm], dst[:, 0, :, 0:mm], [("wbW", sl)], [], eng=eng)
                    else:
                        dma(("wbW", sl), Wb[ct0:ct0 + ng, :, k0:k0 + kn, :].rearrange("j p k c -> p j k c"), dst, [("wbW", sl)], [], eng=eng)

        precast(w_in, Wb_in, NDC, [(ct,) + cfg.coltile(ct) for ct in range(NCT)])
        precast(w_out, Wb_out, NKM, [(d, d * 128, 128) for d in range(NDC)])
        precast(w_gate, Wb_gate, NDC, [(f, f * 128, 128) for f in range(NF)])
        precast(w_up, Wb_up, NDC, [(f, f * 128, 128) for f in range(NF)])
        precast(w_down, Wb_down, NF, [(d, d * 128, 128) for d in range(NDC)])
        P.barrier(bar[:, 6:7])

        reset()
        TBW = cfg.TBW
        xs = [tf(D) for _ in range(2)]
        xTt = tf(NDC * 128).rearrange("p (d t) -> p d t", t=128)
        sqt = tf(NDC * 128).rearrange("p (d t) -> p d t", t=128)
        rstd = tf(128)
        wst = None
        zst = [tf(512) for _ in range(4)]
        hT = tb(NDC * TBW).rearrange("p (d t) -> p d t", t=TBW)
        wbf = [tb(32 * 128) for _ in range(NWS)]
        ti = 0
        zc = 0
        for wb in range(S // TBW):
            for tt in range(TBW // 128):
                t0 = wb * TBW + tt * 128
                sl = ti % 2
                ti += 1
                dma(("xs", sl), xs[sl], x_in[t0:t0 + 128, :], [], [("xs", sl)])
                for g4 in range(NDC // 4 if NDC >= 4 else 1):
                    nd = min(4, NDC)
                    bk = g4 % 2
                    for j in range(nd):
                        d = g4 * 4 + j
                        T(lambda e, bk=bk, j=j, d=d, sl=sl: e.matmul(banks[bk][:, j * 128:(j + 1) * 128], lhsT=xs[sl][:, d * 128:(d + 1) * 128],
                                                                    rhs=ident, start=True, stop=True), [("xs", sl), "cst"], [("bank", bk)])
                    V(lambda e, bk=bk, g4=g4, nd=nd: e.tensor_copy(out=xTt[:, g4 * 4:g4 * 4 + nd, :], in_=banks[bk][:, 0:nd * 128].rearrange("p (d t) -> p d t", t=128)),
                      [("bank", bk)], [("xTt", g4)])
                    A(lambda e, g4=g4, nd=nd: e.activation(out=sqt[:, g4 * 4:g4 * 4 + nd, :], in_=xTt[:, g4 * 4:g4 * 4 + nd, :], func=AF.Square),
                      [("xTt", g4)], [("sqt", g4)])
                for d in range(NDC):
                    T(lambda e, d=d: e.matmul(banks[2][:, 0:128], lhsT=ONES, rhs=sqt[:, d, :], start=(d == 0), stop=(d == NDC - 1)),
                      [("sqt", d // 4), "cst"], [("bank", 2)])
                V(lambda e: e.tensor_scalar(out=rstd, in0=banks[2][:, 0:128], scalar1=1.0 / D, scalar2=NORM_EPS, op0=ALU.mult, op1=ALU.add), [("bank", 2)], ["rstd"])
                A(lambda e: e.activation(out=rstd, in_=rstd, func=AF.Sqrt), ["rstd"], ["rstd"])
                V(lambda e: e.reciprocal(out=rstd, in_=rstd), ["rstd"], ["rstd"])
                for d in range(NDC):
                    E(lambda e, d=d, tt=tt: e.scalar_tensor_tensor(out=hT[:, d, tt * 128:(tt + 1) * 128], in0=xTt[:, d, :], scalar=pc(cfg.o_gmix + d), in1=rstd,
                                                                  op0=ALU.mult, op1=ALU.mult), [("xTt", d // 4), "rstd", "pv"], [("hT", tt)])
                if t0 >= BASE:
                    o = t0 - BASE
                    for da in range(0, NDC, 8):
                        db = min(NDC, da + 8)
                        dma("xTs", xTs[da:db, :, o:o + 128].rearrange("d p t -> p d t"), xTt[:, da:db, :], [("xTt", g) for g in range(max(1, NDC // 4))], [("xTs", o // 128, da)])
            nsub = max(1, TBW // 512)
            n = min(512, TBW)
            for ct in range(NCT):
                c0, m = cfg.coltile(ct)

                def sink(sb, ps, key, ct=ct, m=m, wb=wb, n=n):
                    nonlocal zc
                    zs = zc % 4
                    zc += 1
                    A(lambda e, zs=zs: e.copy(out=zst[zs][0:m, 0:n], in_=ps), [key], [("zst", zs)])
                    tok = wb * TBW + sb * n
                    dma(("zst", zs), zTl[ct][0:m, tok:tok + n], zst[zs][0:m, 0:n], [("zst", zs)], [("zT", ct, tok // 256), ("zT", ct, tok // 256 + 1)] if n == 512 else [("zT", ct, tok // 256)], eng="scalar")
                project(Wb_in, ct, m, NDC, lambda k, sb: hT[:, k, sb * n:(sb + 1) * n], [("hT", t) for t in range(TBW // 128)], nsub, n, sink, wst, wbf, [3, 4, 5, 6], ct)
        P.barrier(bar[:, 0:1])

        reset()
        STOP = float(os.environ.get('KSTOP', '9'))
        KEXP = os.environ.get('KEXP', '')
        LH_ = LH if STOP >= 2 else 0
        TB = min(512, S)
        NBL = S // TB
        wa_t = tf(128); wi_t = tf(128)
        hprev = tf(8)[:, 0:1]
        lx = [tf(TB + 3) for _ in range(2)]
        lg = [tf(TB) for _ in range(2)]
        lm = [tf(TB) for _ in range(2)]
        nm = ["xc", "r", "ig", "a", "a2", "u", "h", "t1", "t2", "y"]
        lt = {k: tf(TB) for k in nm}
        ybf = [tb(TB) for _ in range(2)]
        li = 0
        for h in range(LH_):
            dma("wa", wa_t, lru_wa[h], [], ["wa"])
            dma("wi", wi_t, lru_wi[h], [], ["wi"])
            for b in range(NBL):
                t0 = b * TB
                sl = li % 2
                li += 1
                if b == 0:
                    G(lambda e, sl=sl: e.memset(lx[sl][:, 0:3], 0.0), [], [("lx", sl)])
                    dma(("lx", sl), lx[sl][:, 3:TB + 3], zTl[h][:, 0:TB], [("zT", h, j) for j in range(0, max(1, TB // 256))], [("lx", sl)])
                else:
                    dma(("lx", sl), lx[sl][:, 0:TB + 3], zTl[h][:, t0 - 3:t0 + TB], [("zT", h, j) for j in range((t0 - 3) // 256, (t0 + TB - 1) // 256 + 1)], [("lx", sl)])
                dma(("lm", sl), lm[sl], pmask[:, t0:t0 + TB], [], [("lm", sl)])
                need_y = t0 + TB > BASE
                if need_y:
                    dma(("lg", sl), lg[sl], zTl[LH + h][:, t0:t0 + TB], [("zT", LH + h, j) for j in range(t0 // 256, (t0 + TB - 1) // 256 + 1)], [("lg", sl)])
                X = lx[sl]
                A(lambda e, X=X, h=h: e.activation(out=lt["xc"], in_=X[:, 3:TB + 3], func=AF.Identity, scale=pc(cfg.o_cw + 3 * LH + h), bias=pc(cfg.o_cb + h)),
                  [("lx", sl), "pv"], ["xc"])
                for j in range(3):
                    V(lambda e, X=X, h=h, j=j: e.scalar_tensor_tensor(out=lt["xc"], in0=X[:, j:j + TB], scalar=pc(cfg.o_cw + j * LH + h), in1=lt["xc"], op0=ALU.mult, op1=ALU.add),
                      [("lx", sl), "pv", "xc"], ["xc"])
                T(lambda e: e.matmul(banks[0][:, 0:TB], lhsT=wa_t, rhs=lt["xc"], start=True, stop=True), ["wa", "xc"], [("bank", 0)])
                T(lambda e: e.matmul(banks[1][:, 0:TB], lhsT=wi_t, rhs=lt["xc"], start=True, stop=True), ["wi", "xc"], [("bank", 1)])
                A(lambda e, h=h: e.activation(out=lt["r"], in_=banks[0][:, 0:TB], func=AF.Sigmoid, bias=pc(cfg.o_ba + h)), [("bank", 0), "pv"], ["r"])
                A(lambda e, h=h: e.activation(out=lt["ig"], in_=banks[1][:, 0:TB], func=AF.Sigmoid, bias=pc(cfg.o_bi + h)), [("bank", 1), "pv"], ["ig"])
                A(lambda e, h=h: e.activation(out=lt["a"], in_=lt["r"], func=AF.Exp, scale=dc(h)), ["r", "dv"], ["a"])
                A(lambda e, h=h: e.activation(out=lt["a2"], in_=lt["r"], func=AF.Exp, scale=dc(LH + h)), ["r", "dv2"], ["a2"])
                V(lambda e: e.tensor_scalar(out=lt["a2"], in0=lt["a2"], scalar1=-1.0, scalar2=1.0, op0=ALU.mult, op1=ALU.add), ["a2"], ["a2"])
                V(lambda e: e.tensor_scalar(out=lt["a2"], in0=lt["a2"], scalar1=0.0, scalar2=None, op0=ALU.max), ["a2"], ["a2"])
                A(lambda e: e.activation(out=lt["a2"], in_=lt["a2"], func=AF.Sqrt), ["a2"], ["a2"])
                G(lambda e: e.tensor_tensor(out=lt["u"], in0=lt["ig"], in1=lt["xc"], op=ALU.mult), ["ig", "xc"], ["u"])
                G(lambda e, sl=sl: e.tensor_tensor(out=lt["u"], in0=lt["u"], in1=lm[sl], op=ALU.mult), ["u", ("lm", sl)], ["u"])
                V(lambda e: e.tensor_tensor(out=lt["u"], in0=lt["u"], in1=lt["a2"], op=ALU.mult), ["u", "a2"], ["u"])
                if b == 0:
                    V(lambda e: e.tensor_tensor_scan(out=lt["h"], data0=lt["a"], data1=lt["u"], initial=0.0, op0=ALU.mult, op1=ALU.add), ["a", "u", "hprev"], ["h"])
                else:
                    V(lambda e: e.tensor_tensor_scan(out=lt["h"], data0=lt["a"], data1=lt["u"], initial=hprev, op0=ALU.mult, op1=ALU.add), ["a", "u", "hprev"], ["h"])
                V(lambda e: e.tensor_copy(out=hprev, in_=lt["h"][:, TB - 1:TB]), ["h"], ["hprev"])
                if need_y:
                    Gz = lg[sl]
                    A(lambda e, Gz=Gz: e.activation(out=lt["t1"], in_=Gz, func=AF.Square), [("lg", sl)], ["t1"])
                    V(lambda e: e.tensor_scalar(out=lt["t1"], in0=lt["t1"], scalar1=0.044715, scalar2=1.0, op0=ALU.mult, op1=ALU.add), ["t1"], ["t1"])
                    G(lambda e, Gz=Gz: e.tensor_tensor(out=lt["t1"], in0=lt["t1"], in1=Gz, op=ALU.mult), ["t1", ("lg", sl)], ["t1"])
                    A(lambda e: e.activation(out=lt["t2"], in_=lt["t1"], func=AF.Sigmoid, scale=1.5957691216057308), ["t1"], ["t2"])
                    G(lambda e, Gz=Gz: e.tensor_tensor(out=lt["t2"], in0=lt["t2"], in1=Gz, op=ALU.mult), ["t2", ("lg", sl)], ["t2"])
                    V(lambda e, sl=sl: e.tensor_tensor(out=ybf[sl], in0=lt["t2"], in1=lt["h"], op=ALU.mult), ["t2", "h"], [("ybf", sl)])
                    o = t0 - BASE
                    lo = max(0, -o)
                    dma(("ybf", sl), yTs[h, :, o + lo:o + TB], ybf[sl][:, lo:TB], [("ybf", sl)], [("yTs", h, t0)], eng="gpsimd")
        P.barrier(bar[:, 1:2])

        reset()
        U = 256 if S >= 256 else S
        NCH = U // C
        NU = S // U
        w2p = tf(128); a2p = tf(128); g2p = tf(256).rearrange("p (k c) -> p k c", c=128)
        Zs = [tf(64) for _ in range(2)]
        ldn = ["r", "k", "v", "zw", "za", "g0", "g1"]
        ld = [{k: tf(U + 1) for k in ldn} for _ in range(2)]
        en = ["rs", "ks", "vs", "zws", "zas", "g0s", "g1s", "tmp", "tw", "lw", "av", "gv", "kk", "sq", "k2", "be", "cum", "Ein", "Eni", "Epv", "Een",
              "KT", "BT", "KH", "BH", "rk", "bon", "Lm", "P0", "P1", "GT", "PTs", "Ysb", "yc", "sq2", "yn"]
        et = {k: tf(U) for k in en}
        AR = tf(2 * U).rearrange("p (c q t) -> p c q t", q=2, t=C)
        MT = tf(4 * U).rearrange("p (c q t) -> p c q t", q=4, t=C)
        TM = tf(4 * U).rearrange("p (c q t) -> p c q t", q=4, t=C)
        Xa = [tf(2 * U).rearrange("p (c q t) -> p c q t", q=2, t=C) for _ in range(2)]
        PP = [tf(2 * U).rearrange("p (c q t) -> p c q t", q=2, t=C) for _ in range(2)]
        yob = [tb(U) for _ in range(2)]
        mkc = tf(U)
        for i in range(max(1, U // 128)):
            G(lambda e, i=i: e.tensor_copy(out=mkc[:, i * 128:min(U, (i + 1) * 128)], in_=MK_C[:, 0:min(U, 128)]), ["cst"], ["mkc"])
        v3 = lambda ap: ap.rearrange("p (c t) -> p c t", t=C)
        H2 = [slice(0, 64), slice(64, 128)]
        ui = 0
        for p in range(RP if STOP >= 3 else 0):
            pcols = slice(p * 128, (p + 1) * 128)
            zstate = {"i": 0}
            dma("w2p", w2p[0:96, :], w2_in[:, pcols], [], ["w2p"])
            dma("a2p", a2p[0:96, :], a2_in[:, pcols], [], ["a2p"])
            dma("g2p", g2p, g2_in[:, pcols].rearrange("(k q) c -> q k c", q=128), [], ["g2p"])
            for u in range(NU):
              try:
                t0 = u * U
                need_y = t0 + U > BASE
                sl = ui % 2
                ui += 1
                L = ld[sl]
                cts = {"r": 2 * LH + p, "k": 2 * LH + RP + p, "v": 2 * LH + 2 * RP + p, "zw": 2 * LH + 3 * RP, "za": 2 * LH + 3 * RP + 1,
                       "g0": 2 * LH + 3 * RP + 2, "g1": 2 * LH + 3 * RP + 3}
                rows = {"r": 128, "k": 128, "v": 128, "zw": 96, "za": 96, "g0": 128, "g1": 128}
                mucol = {"r": p, "k": RP + p, "v": 2 * RP + p, "zw": 3 * RP, "za": 3 * RP + 1, "g0": 3 * RP + 2, "g1": 3 * RP + 3}
                outn = {"r": "rs", "k": "ks", "v": "vs", "zw": "zws", "za": "zas", "g0": "g0s", "g1": "g1s"}
                for k in ldn:
                    if k in ("g0", "g1") and not need_y:
                        continue
                    R = rows[k]
                    ct = cts[k]
                    if t0 == 0:
                        G(lambda e, k=k, R=R: e.memset(L[k][0:R, 0:1], 0.0), [], [("ld", sl, k)])
                        dma(("ld", sl, k), L[k][0:R, 1:U + 1], zTl[ct][0:R, 0:U], [("zT", ct, j) for j in range(max(1, U // 256))], [("ld", sl, k)])
                    else:
                        dma(("ld", sl, k), L[k][0:R, 0:U + 1], zTl[ct][0:R, t0 - 1:t0 + U], [("zT", ct, j) for j in range((t0 - 1) // 256, (t0 + U - 1) // 256 + 1)], [("ld", sl, k)])
                    tmpk = "tmp_" + k
                    G(lambda e, k=k, R=R: e.tensor_tensor(out=et["tmp"][0:R, :], in0=L[k][0:R, 0:U], in1=L[k][0:R, 1:U + 1], op=ALU.subtract), [("ld", sl, k)], ["tmp"])
                    V(lambda e, k=k, R=R: e.scalar_tensor_tensor(out=et[outn[k]][0:R, :], in0=et["tmp"][0:R, :], scalar=pv[0:R, cfg.o_mu + mucol[k]:cfg.o_mu + mucol[k] + 1],
                                                                 in1=L[k][0:R, 1:U + 1], op0=ALU.mult, op1=ALU.add), ["tmp", ("ld", sl, k), "pv"], [outn[k]])
                if STOP < 3.05:
                    raise _Stop()
                A(lambda e: e.activation(out=et["tw"][0:96, :], in_=et["zws"][0:96, :], func=AF.Tanh), ["zws"], ["tw"])
                T(lambda e: e.matmul(banks[0][:, 0:U], lhsT=w2p[0:96, :], rhs=et["tw"][0:96, :], start=True, stop=True), ["w2p", "tw"], [("bank", 0)])
                T(lambda e: e.matmul(banks[0][:, U:2 * U], lhsT=a2p[0:96, :], rhs=et["zas"][0:96, :], start=True, stop=True), ["a2p", "zas"], [("bank", 0)])
                A(lambda e, p=p: e.activation(out=et["lw"], in_=banks[0][:, 0:U], func=AF.Sigmoid, bias=pc(cfg.o_w0 + p)), [("bank", 0), "pv"], ["lw"])
                A(lambda e, p=p: e.activation(out=et["av"], in_=banks[0][:, U:2 * U], func=AF.Sigmoid, bias=pc(cfg.o_a0 + p)), [("bank", 0), "pv"], ["av"])
                V(lambda e: e.tensor_scalar(out=et["lw"], in0=et["lw"], scalar1=-0.6065306597126334, scalar2=None, op0=ALU.mult), ["lw"], ["lw"])
                if need_y:
                    A(lambda e: e.activation(out=et["g0s"], in_=et["g0s"], func=AF.Sigmoid), ["g0s"], ["g0s"])
                    A(lambda e: e.activation(out=et["g1s"], in_=et["g1s"], func=AF.Sigmoid), ["g1s"], ["g1s"])
                    T(lambda e: e.matmul(banks[1][:, 0:U], lhsT=g2p[:, 0, :], rhs=et["g0s"], start=True, stop=False), ["g2p", "g0s"], [("bank", 1)])
                    T(lambda e: e.matmul(banks[1][:, 0:U], lhsT=g2p[:, 1, :], rhs=et["g1s"], start=False, stop=True), ["g2p", "g1s"], [("bank", 1)])
                    A(lambda e: e.copy(out=et["gv"], in_=banks[1][:, 0:U]), [("bank", 1)], ["gv"])
                if STOP < 3.1:
                    raise _Stop()
                V(lambda e, p=p: e.tensor_scalar(out=et["kk"], in0=et["ks"], scalar1=pc(cfg.o_kk + p), scalar2=None, op0=ALU.mult), ["ks", "pv"], ["kk"])
                A(lambda e: e.activation(out=et["sq"], in_=et["kk"], func=AF.Square), ["kk"], ["sq"])
                T(lambda e: e.matmul(banks[1][:, U:2 * U], lhsT=BD, rhs=et["sq"], start=True, stop=True), ["cst", "sq"], [("bank", 1)])
                A(lambda e: e.activation(out=et["sq"], in_=banks[1][:, U:2 * U], func=AF.Sqrt), [("bank", 1)], ["sq"])
                V(lambda e: e.tensor_scalar(out=et["sq"], in0=et["sq"], scalar1=1e-12, scalar2=None, op0=ALU.max), ["sq"], ["sq"])
                V(lambda e: e.reciprocal(out=et["sq"], in_=et["sq"]), ["sq"], ["sq"])
                V(lambda e: e.tensor_tensor(out=et["kk"], in0=et["kk"], in1=et["sq"], op=ALU.mult), ["kk", "sq"], ["kk"])
                V(lambda e, p=p: e.tensor_scalar(out=et["k2"], in0=et["av"], scalar1=-1.0, scalar2=pc(cfg.o_ka + p), op0=ALU.add, op1=ALU.mult), ["av", "pv"], ["k2"])
                V(lambda e: e.scalar_tensor_tensor(out=et["k2"], in0=et["k2"], scalar=1.0, in1=et["ks"], op0=ALU.add, op1=ALU.mult), ["k2", "ks"], ["k2"])
                G(lambda e: e.tensor_tensor(out=et["be"], in0=et["kk"], in1=et["av"], op=ALU.mult), ["kk", "av"], ["be"])
                if STOP < 3.15:
                    raise _Stop()
                V(lambda e: e.tensor_tensor_scan(out=et["cum"], data0=mkc, data1=et["lw"], initial=0.0, op0=ALU.mult, op1=ALU.add), ["lw", "mkc"], ["cum"])
                A(lambda e: e.activation(out=et["Ein"], in_=et["cum"], func=AF.Exp), ["cum"], ["Ein"])
                A(lambda e: e.activation(out=et["Eni"], in_=et["cum"], func=AF.Exp, scale=-1.0), ["cum"], ["Eni"])
                G(lambda e: e.tensor_tensor(out=et["Epv"], in0=et["cum"], in1=et["lw"], op=ALU.subtract), ["cum", "lw"], ["Epv"])
                A(lambda e: e.activation(out=et["Epv"], in_=et["Epv"], func=AF.Exp), ["Epv"], ["Epv"])
                for c in range(NCH):
                    V(lambda e, c=c: e.tensor_scalar(out=v3(et["Een"])[:, c, :], in0=v3(et["cum"])[:, c, :], scalar1=v3(et["cum"])[:, c, C - 1:C], scalar2=-1.0,
                                                     op0=ALU.subtract, op1=ALU.mult), ["cum"], ["Een"])
                A(lambda e: e.activation(out=et["Een"], in_=et["Een"], func=AF.Exp), ["Een"], ["Een"])
                if STOP < 3.17:
                    raise _Stop()
                V(lambda e: e.scalar_tensor_tensor(out=AR[:, :, 0, :], in0=v3(et["kk"]), scalar=-1.0, in1=v3(et["Epv"]), op0=ALU.mult, op1=ALU.mult), ["kk", "Epv"], ["AR0"])
                G(lambda e: e.tensor_tensor(out=AR[:, :, 1, :], in0=v3(et["rs"]), in1=v3(et["Ein"]), op=ALU.mult), ["rs", "Ein"], ["AR1"])
                G(lambda e: e.tensor_tensor(out=et["KT"], in0=et["k2"], in1=et["Eni"], op=ALU.mult), ["k2", "Eni"], ["KT"])
                V(lambda e: e.tensor_tensor(out=et["BT"], in0=et["be"], in1=et["Eni"], op=ALU.mult), ["be", "Eni"], ["BT"])
                G(lambda e: e.tensor_tensor(out=et["KH"], in0=et["k2"], in1=et["Een"], op=ALU.mult), ["k2", "Een"], ["KH"])
                V(lambda e: e.tensor_tensor(out=et["BH"], in0=et["be"], in1=et["Een"], op=ALU.mult), ["be", "Een"], ["BH"])
                if need_y:
                    V(lambda e, p=p: e.scalar_tensor_tensor(out=et["rk"], in0=et["rs"], scalar=pc(cfg.o_rk + p), in1=et["k2"], op0=ALU.mult, op1=ALU.mult), ["rs", "k2", "pv"], ["rk"])
                    T(lambda e: e.matmul(banks[2][:, 0:U], lhsT=BD, rhs=et["rk"], start=True, stop=True), ["cst", "rk"], [("bank", 2)])
                    V(lambda e: e.tensor_tensor(out=et["bon"], in0=banks[2][:, 0:U], in1=et["vs"], op=ALU.mult), [("bank", 2), "vs"], ["bon"])
                if STOP < 3.2:
                    raise _Stop()
                KTv, BTv, KHv, BHv, Vv = v3(et["KT"]), v3(et["BT"]), v3(et["KH"]), v3(et["BH"]), v3(et["vs"])
                pL = banks[3][:, 0:U].rearrange("p (c t) -> p c t", t=C)
                pM = [banks[4][:, 0:2 * U].rearrange("p (c q t) -> p c q t", q=2, t=C), banks[5][:, 0:2 * U].rearrange("p (c q t) -> p c q t", q=2, t=C)]
                for c in range(NCH):
                    for hh in H2:
                        T(lambda e, c=c, hh=hh: e.matmul(pL[hh, c, :], lhsT=AR[hh, c, 0, :], rhs=BTv[hh, c, :], start=True, stop=True), ["AR0", "BT"], [("bank", 3)])
                        T(lambda e, c=c, hh=hh: e.matmul(pM[0][hh, c, :, :], lhsT=BTv[hh, c, :], rhs=AR[hh, c, :, :], start=True, stop=True), ["AR0", "AR1", "BT"], [("bank", 4)])
                        T(lambda e, c=c, hh=hh: e.matmul(pM[1][hh, c, :, :], lhsT=KTv[hh, c, :], rhs=AR[hh, c, :, :], start=True, stop=True), ["AR0", "AR1", "KT"], [("bank", 5)])
                V(lambda e: e.tensor_tensor(out=v3(et["Lm"]), in0=pL, in1=MK_L.unsqueeze(1).broadcast_to([128, NCH, C]), op=ALU.mult), [("bank", 3), "cst"], ["Lm"])
                mk4 = MK_T.rearrange("p (q t) -> p q t", t=C).unsqueeze(1).broadcast_to([128, NCH, 2, C])
                V(lambda e: e.tensor_tensor(out=MT[:, :, 0:2, :], in0=pM[0], in1=mk4, op=ALU.mult), [("bank", 4), "cst"], ["MT01"])
                V(lambda e: e.tensor_tensor(out=MT[:, :, 2:4, :], in0=pM[1], in1=mk4, op=ALU.mult), [("bank", 5), "cst"], ["MT23"])
                if STOP < 3.3:
                    raise _Stop()
                pT = [banks[6][:, 0:2 * U].rearrange("p (c q t) -> p c q t", q=2, t=C), banks[7][:, 0:2 * U].rearrange("p (c q t) -> p c q t", q=2, t=C)]
                srcs = [(AR[:, :, 0, :], "AR0"), (Vv, "vs"), (BHv, "BH"), (KHv, "KH")]
                for c in range(NCH):
                    for hh in H2:
                        for q, (sap, skey) in enumerate(srcs):
                            T(lambda e, c=c, hh=hh, q=q, sap=sap: e.matmul(pT[q // 2][hh, c, q % 2, :], lhsT=sap[hh, c, :], rhs=ident[hh, hh], start=True, stop=True),
                              [skey, "cst"], [("bank", 6 + q // 2)])
                A(lambda e: e.copy(out=TM[:, :, 0:2, :], in_=pT[0]), [("bank", 6)], ["TM01"])
                A(lambda e: e.copy(out=TM[:, :, 2:4, :], in_=pT[1]), [("bank", 7)], ["TM23"])
                if STOP < 3.35:
                    raise _Stop()
                pX = banks[4][:, 0:2 * U].rearrange("p (c q t) -> p c q t", q=2, t=C)
                pP = banks[5][:, 0:2 * U].rearrange("p (c q t) -> p c q t", q=2, t=C)
                for c in range(NCH):
                    for hh in H2:
                        T(lambda e, c=c, hh=hh: e.matmul(pX[hh, c, 1, :], lhsT=MT[hh, c, 2, :], rhs=TM[hh, c, 1, :], start=True, stop=True), ["MT23", "TM01"], [("bank", 4)])
                G(lambda e: e.tensor_copy(out=Xa[0][:, :, 0, :], in_=TM[:, :, 0, :]), ["TM01"], [("Xa", 0)])
                V(lambda e: e.tensor_copy(out=Xa[0][:, :, 1, :], in_=pX[:, :, 1, :]), [("bank", 4)], [("Xa", 0)])
                if STOP < 3.4:
                    raise _Stop()
                curP = v3(et["Lm"]); curPT = MT[:, :, 0, :]; kP, kPT = "Lm", "MT01"
                for lvl in range(int(os.environ.get('KLVL', '6'))):
                    xi, xo = lvl % 2, (lvl + 1) % 2
                    for c in range(NCH):
                        for hh in H2:
                            T(lambda e, c=c, hh=hh, curPT=curPT, xi=xi: e.matmul(pX[hh, c, :, :], lhsT=(BTv if KEXP == 'B' else curPT)[hh, c, :], rhs=(AR if KEXP == 'A' else Xa[xi])[hh, c, :, :], start=True, stop=True),
                              [kPT, ("Xa", xi)], [("bank", 4)])
                    if os.environ.get('KADD', '1') == '1':
                        V(lambda e, xi=xi, xo=xo: e.tensor_tensor(out=Xa[xo], in0=pX, in1=Xa[xi], op=ALU.add), [("bank", 4), ("Xa", xi)], [("Xa", xo)])
                    if lvl < 5 and os.environ.get('KSQ', '1') == '1':
                        last = lvl == 4
                        for c in range(NCH):
                            for hh in H2:
                                if not last:
                                    T(lambda e, c=c, hh=hh, curP=curP, curPT=curPT: e.matmul(pP[hh, c, 0, :], lhsT=curPT[hh, c, :], rhs=curP[hh, c, :], start=True, stop=True),
                                      [kP, kPT], [("bank", 5)])
                                T(lambda e, c=c, hh=hh, curP=curP, curPT=curPT: e.matmul(pP[hh, c, 1, :], lhsT=curP[hh, c, :], rhs=curPT[hh, c, :], start=True, stop=True),
                                  [kP, kPT], [("bank", 5)])
                        ps = lvl % 2
                        if last:
                            A(lambda e, ps=ps: e.copy(out=PP[ps][:, :, 1, :], in_=pP[:, :, 1, :]), [("bank", 5)], [("PP", ps)])
                        else:
                            A(lambda e, ps=ps: e.copy(out=PP[ps], in_=pP), [("bank", 5)], [("PP", ps)])
                        curP = PP[ps][:, :, 0, :]; curPT = PP[ps][:, :, 1, :]; kP = kPT = ("PP", ps)
                XF = Xa[0]
                kXF = ("Xa", 0)
                if STOP < 3.5:
                    raise _Stop()
                pG = banks[3][:, 0:2 * U].rearrange("p (c q t) -> p c q t", q=2, t=C)
                for c in range(NCH):
                    for hh in H2:
                        if need_y:
                            T(lambda e, c=c, hh=hh: e.matmul(pG[hh, c, 0, :], lhsT=XF[hh, c, 0, :], rhs=MT[hh, c, 1, :], start=True, stop=True), [kXF, "MT01"], [("bank", 3)])
                        T(lambda e, c=c, hh=hh: e.matmul(pG[hh, c, 1, :], lhsT=XF[hh, c, 0, :], rhs=TM[hh, c, 2, :], start=True, stop=True), [kXF, "TM23"], [("bank", 3)])
                KS5 = os.environ.get('KS5', 'va')
                if need_y and 'v' in KS5:
                    V(lambda e: e.tensor_tensor(out=v3(et["GT"]), in0=pG[:, :, 0, :], in1=AR[:, :, 1, :], op=ALU.add), [("bank", 3), "AR1"], ["GT"])
                if 'a' in KS5:
                    V(lambda e: e.tensor_copy(out=v3(et["PTs"]), in_=pG[:, :, 1, :]), [("bank", 3)], ["PTs"])
                if STOP < 3.6:
                    raise _Stop()
                pZ = banks[6][:, 0:U].rearrange("p (c t) -> p c t", t=C)
                pY = banks[7][:, 0:U].rearrange("p (c t) -> p c t", t=C)
                GTv, PTv = v3(et["GT"]), v3(et["PTs"])
                Einv = v3(et["Ein"])
                for c in range(NCH):
                    first = (u == 0 and c == 0)
                    zi = zstate["i"] % 2
                    zo = (zstate["i"] + 1) % 2
                    zstate["i"] += 1
                    if first:
                        G(lambda e, zi=zi: e.memset(Zs[zi], 0.0), [], [("Z", zi)])
                    for hh in H2:
                        T(lambda e, c=c, hh=hh: e.matmul(pZ[hh, c, :], lhsT=TM[hh, c, 2, :], rhs=XF[hh, c, 1, :], start=True, stop=False), ["TM23", kXF], [("bank", 6)])
                        T(lambda e, c=c, hh=hh: e.matmul(pZ[hh, c, :], lhsT=TM[hh, c, 3, :], rhs=TM[hh, c, 1, :], start=False, stop=False), ["TM23", "TM01"], [("bank", 6)])
                        T(lambda e, c=c, hh=hh, zi=zi: e.matmul(pZ[hh, c, :], lhsT=PTv[hh, c, :], rhs=Zs[zi][hh, :], start=False, stop=True), ["PTs", ("Z", zi)], [("bank", 6)])
                        if need_y:
                            T(lambda e, c=c, hh=hh: e.matmul(pY[hh, c, :], lhsT=XF[hh, c, 1, :], rhs=MT[hh, c, 1, :], start=True, stop=False), [kXF, "MT01"], [("bank", 7)])
                            T(lambda e, c=c, hh=hh: e.matmul(pY[hh, c, :], lhsT=TM[hh, c, 1, :], rhs=MT[hh, c, 3, :], start=False, stop=False), ["TM01", "MT23"], [("bank", 7)])
                            T(lambda e, c=c, hh=hh, zi=zi: e.matmul(pY[hh, c, :], lhsT=Zs[zi][hh, :], rhs=GTv[hh, c, :], start=False, stop=True), [("Z", zi), "GT"], [("bank", 7)])
                    V(lambda e, c=c, zi=zi, zo=zo: e.scalar_tensor_tensor(out=Zs[zo], in0=Zs[zi], scalar=Einv[:, c, C - 1:C], in1=pZ[:, c, :], op0=ALU.mult, op1=ALU.add),
                      [("Z", zi), "Ein", ("bank", 6)], [("Z", zo)])
                if STOP < 3.7:
                    raise _Stop()
                if need_y:
                    A(lambda e: e.copy(out=et["Ysb"], in_=banks[7][:, 0:U]), [("bank", 7)], ["Ysb"])
                    T(lambda e: e.matmul(banks[2][:, 0:U], lhsT=BDS, rhs=et["Ysb"], start=True, stop=True), ["cst", "Ysb"], [("bank", 2)])
                    V(lambda e: e.tensor_tensor(out=et["yc"], in0=et["Ysb"], in1=banks[2][:, 0:U], op=ALU.subtract), ["Ysb", ("bank", 2)], ["yc"])
                    A(lambda e: e.activation(out=et["sq2"], in_=et["yc"], func=AF.Square), ["yc"], ["sq2"])
                    T(lambda e: e.matmul(banks[2][:, U:2 * U], lhsT=BDS, rhs=et["sq2"], start=True, stop=True), ["cst", "sq2"], [("bank", 2)])
                    V(lambda e: e.tensor_scalar(out=et["sq2"], in0=banks[2][:, U:2 * U], scalar1=GN_EPS, scalar2=None, op0=ALU.add), [("bank", 2)], ["sq2"])
                    A(lambda e: e.activation(out=et["sq2"], in_=et["sq2"], func=AF.Sqrt), ["sq2"], ["sq2"])
                    V(lambda e: e.reciprocal(out=et["sq2"], in_=et["sq2"]), ["sq2"], ["sq2"])
                    V(lambda e: e.tensor_tensor(out=et["yn"], in0=et["yc"], in1=et["sq2"], op=ALU.mult), ["yc", "sq2"], ["yn"])
                    V(lambda e, p=p: e.tensor_scalar(out=et["yn"], in0=et["yn"], scalar1=pc(cfg.o_gw + p), scalar2=pc(cfg.o_gb + p), op0=ALU.mult, op1=ALU.add), ["yn", "pv"], ["yn"])
                    G(lambda e: e.tensor_tensor(out=et["yn"], in0=et["yn"], in1=et["bon"], op=ALU.add), ["yn", "bon"], ["yn"])
                    ys = ui % 2
                    V(lambda e, ys=ys: e.tensor_tensor(out=yob[ys], in0=et["yn"], in1=et["gv"], op=ALU.mult), ["yn", "gv"], [("yob", ys)])
                    o = t0 - BASE
                    lo = max(0, -o)
                    dma(("yob", ys), yTs[LH + p, :, o + lo:o + U], yob[ys][:, lo:U], [("yob", ys)], [("yTs", LH + p, t0)])
              except _Stop:
                pass
        P.barrier(bar[:, 2:3])

        reset()
        TBC = cfg.TBC
        wst = None
        xt_ = [tf(TBC) for _ in range(4)]
        sqc = [tf(TBC) for _ in range(2)]
        rst = tf(TBC)
        gat = [tf(TBC + 8) for _ in range(2)]
        upt = [tf(TBC) for _ in range(2)]
        sgt = [tf(TBC) for _ in range(2)]
        ot = [tf(512) for _ in range(2)]
        ghs = tf(NF * 2).rearrange("p (f t) -> p f t", t=2)
        pm2 = tf(8)
        wbf = [tb(32 * 128) for _ in range(NWS)]
        h2T = tb(NDC * TBC).rearrange("p (d t) -> p d t", t=TBC)
        R1 = tb(max(NF, NKM) * TBC)
        yTb = R1[:, 0:NKM * TBC].rearrange("p (k t) -> p k t", t=TBC)
        actT = R1[:, 0:NF * TBC].rearrange("p (k t) -> p k t", t=TBC)
        dma("pm2", pm2[:, 0:2], pmask[:, S - NT - 2:S - NT], [], ["pm2"])
        cnt = {"x": 0, "p": 0}
        blocks = [(0, EXT, True)] + [(EXT + i * TBC, TBC, False) for i in range(NT // TBC)]
        for (so, n, halo_only) in (blocks if STOP >= 4 else []):
            for ka in range(0, NKM, 8):
                kb = min(NKM, ka + 8)
                dma("yTb", yTb[:, ka:kb, 0:n], yTs[ka:kb, :, so:so + n].rearrange("k p t -> p k t"), [], ["yTb"])
            for d in range(NDC):
                def sink(sb, ps, key, d=d):
                    s4 = cnt["x"] % 4
                    s2 = cnt["x"] % 2
                    cnt["x"] += 1
                    X_ = xt_[s4][:, 0:n]
                    dma(("xt", s4), X_, xTs[d, :, so:so + n], [], [("xt", s4)])
                    V(lambda e: e.tensor_tensor(out=X_, in0=ps, in1=X_, op=ALU.add), [key, ("xt", s4)], [("xt", s4)])
                    dma(("xt", s4), x1s[d, :, so:so + n], X_, [("xt", s4)], [("x1s", d, so)], eng="gpsimd")
                    A(lambda e: e.activation(out=sqc[s2][:, 0:n], in_=X_, func=AF.Square), [("xt", s4)], [("sqc", s2)])
                    T(lambda e: e.matmul(banks[0][:, 0:n], lhsT=ONES, rhs=sqc[s2][:, 0:n], start=(d == 0), stop=(d == NDC - 1)), [("sqc", s2), "cst"], [("bank", 0)])
                cnt["p"] += 1
                project(Wb_out, d, 128, NKM, lambda k, sb: yTb[:, k, 0:n], ["yTb"], 1, n, sink, wst, wbf, [1, 2], cnt["p"])
            V(lambda e: e.tensor_scalar(out=rst[:, 0:n], in0=banks[0][:, 0:n], scalar1=1.0 / D, scalar2=NORM_EPS, op0=ALU.mult, op1=ALU.add), [("bank", 0)], ["rst"])
            A(lambda e: e.activation(out=rst[:, 0:n], in_=rst[:, 0:n], func=AF.Sqrt), ["rst"], ["rst"])
            V(lambda e: e.reciprocal(out=rst[:, 0:n], in_=rst[:, 0:n]), ["rst"], ["rst"])
            for d in range(NDC):
                s4 = cnt["x"] % 4
                cnt["x"] += 1
                dma(("xt", s4), xt_[s4][:, 0:n], x1s[d, :, so:so + n], [("x1s", d, so)], [("xt", s4)])
                V(lambda e, d=d, s4=s4: e.scalar_tensor_tensor(out=h2T[:, d, 0:n], in0=xt_[s4][:, 0:n], scalar=pc(cfg.o_gffn + d), in1=rst[:, 0:n], op0=ALU.mult, op1=ALU.mult),
                  [("xt", s4), "pv", "rst"], ["h2T"])
            P.barrier(bar[:, 3:4])
            for f in range(NF):
                s2 = f % 2
                G_ = gat[s2]
                def sink_g(sb, ps, key, G_=G_, s2=s2):
                    A(lambda e: e.copy(out=G_[:, 2:n + 2], in_=ps), [key], [("gat", s2)])
                def sink_u(sb, ps, key, s2=s2):
                    A(lambda e: e.copy(out=upt[s2][:, 0:n], in_=ps), [key], [("upt", s2)])
                cnt["p"] += 1
                project(Wb_gate, f, 128, NDC, lambda k, sb: h2T[:, k, 0:n], ["h2T"], 1, n, sink_g, wst, wbf, [1, 2], cnt["p"])
                if halo_only:
                    V(lambda e, G_=G_, f=f: e.tensor_tensor(out=ghs[:, f, :], in0=G_[:, n:n + 2], in1=pm2[:, 0:2], op=ALU.mult), [("gat", s2), "pm2"], ["ghs"])
                    continue
                cnt["p"] += 1
                project(Wb_up, f, 128, NDC, lambda k, sb: h2T[:, k, 0:n], ["h2T"], 1, n, sink_u, wst, wbf, [1, 2], cnt["p"])
                V(lambda e, G_=G_, f=f: e.tensor_copy(out=G_[:, 0:2], in_=ghs[:, f, :]), ["ghs", ("gat", s2)], [("gat", s2)])
                V(lambda e, G_=G_, f=f: e.tensor_copy(out=ghs[:, f, :], in_=G_[:, n:n + 2]), [("gat", s2)], ["ghs"])
                A(lambda e, G_=G_, f=f, s2=s2: e.activation(out=sgt[s2][:, 0:n], in_=G_[:, 2:n + 2], func=AF.Identity, scale=pc(cfg.o_fw + 2 * NF + f), bias=pc(cfg.o_fb + f)),
                  [("gat", s2), "pv"], [("sgt", s2)])
                for j in range(2):
                    V(lambda e, G_=G_, f=f, j=j, s2=s2: e.scalar_tensor_tensor(out=sgt[s2][:, 0:n], in0=G_[:, j:j + n], scalar=pc(cfg.o_fw + j * NF + f), in1=sgt[s2][:, 0:n], op0=ALU.mult, op1=ALU.add),
                      [("gat", s2), "pv", ("sgt", s2)], [("sgt", s2)])
                A(lambda e, s2=s2: e.activation(out=sgt[s2][:, 0:n], in_=sgt[s2][:, 0:n], func=AF.Silu), [("sgt", s2)], [("sgt", s2)])
                V(lambda e, f=f, s2=s2: e.tensor_tensor(out=actT[:, f, 0:n], in0=sgt[s2][:, 0:n], in1=upt[s2][:, 0:n], op=ALU.mult), [("sgt", s2), ("upt", s2)], ["actT"])
            if halo_only:
                P.barrier(bar[:, 4:5])
                continue
            for d in range(NDC):
                def sink_d(sb, ps, key, d=d):
                    s4 = cnt["x"] % 4
                    s2 = cnt["x"] % 2
                    cnt["x"] += 1
                    X_ = xt_[s4][:, 0:n]
                    dma(("xt", s4), X_, x1s[d, :, so:so + n], [("x1s", d, so)], [("xt", s4)])
                    V(lambda e: e.tensor_tensor(out=X_, in0=ps, in1=X_, op=ALU.add), [key, ("xt", s4)], [("xt", s4)])
                    dma(("xt", s4), x2s[d, :, so:so + n], X_, [("xt", s4)], [("x2s", d, so)], eng="gpsimd")
                    A(lambda e: e.activation(out=sqc[s2][:, 0:n], in_=X_, func=AF.Square), [("xt", s4)], [("sqc", s2)])
                    T(lambda e: e.matmul(banks[0][:, 0:n], lhsT=ONES, rhs=sqc[s2][:, 0:n], start=(d == 0), stop=(d == NDC - 1)), [("sqc", s2), "cst"], [("bank", 0)])
                project(Wb_down, d, 128, NF, lambda k, sb: actT[:, k, 0:n], ["actT"], 1, n, sink_d, wst, wbf, [1 + d % 2], d)
            V(lambda e: e.tensor_scalar(out=rst[:, 0:n], in0=banks[0][:, 0:n], scalar1=1.0 / D, scalar2=NORM_EPS, op0=ALU.mult, op1=ALU.add), [("bank", 0)], ["rst"])
            A(lambda e: e.activation(out=rst[:, 0:n], in_=rst[:, 0:n], func=AF.Sqrt), ["rst"], ["rst"])
            V(lambda e: e.reciprocal(out=rst[:, 0:n], in_=rst[:, 0:n]), ["rst"], ["rst"])
            orow = so - EXT
            for d in range(NDC):
                s4 = cnt["x"] % 4
                cnt["x"] += 1
                X_ = xt_[s4][:, 0:n]
                dma(("xt", s4), X_, x2s[d, :, so:so + n], [("x2s", d, so)], [("xt", s4)])
                V(lambda e, d=d, X_=X_: e.scalar_tensor_tensor(out=X_, in0=X_, scalar=pc(cfg.o_gfin + d), in1=rst[:, 0:n], op0=ALU.mult, op1=ALU.mult),
                  [("xt", s4), "pv", "rst"], [("xt", s4)])
                bk = 3 + d % 2
                for tt in range(n // 128):
                    T(lambda e, X_=X_, tt=tt, bk=bk: e.matmul(banks[bk][:, tt * 128:(tt + 1) * 128], lhsT=X_[:, tt * 128:(tt + 1) * 128], rhs=ident, start=True, stop=True),
                      [("xt", s4), "cst"], [("bank", bk)])
                os_ = d % 2
                A(lambda e, bk=bk, os_=os_: e.copy(out=ot[os_][:, 0:n], in_=banks[bk][:, 0:n]), [("bank", bk)], [("ot", os_)])
                dma(("ot", os_), out[orow:orow + n, d * 128:(d + 1) * 128].rearrange("(tt p) c -> p tt c", p=128), ot[os_][:, 0:n].rearrange("p (tt c) -> p tt c", c=128),
                    [("ot", os_)], [], eng="scalar")
            P.barrier(bar[:, 5:6])
        P.emit(st)
    return nc


def run(cfg, inp, B, SEQ):
    S, NT, D = cfg.S, cfg.NT, cfg.D
    NQ = SEQ // NT
    x = np.asarray(inp["x"], np.float32)
    f = lambda k: np.ascontiguousarray(np.asarray(inp[k], np.float32)[0])
    shared = {
        "consts": host_consts(), "pvec": host_pvec(cfg, {k: np.asarray(v, np.float32) for k, v in inp.items()}),
        "w_in": f("w_in"), "lru_wa": f("lru_w_a"), "lru_wi": f("lru_w_i"), "w2": f("rwkv_w2"), "a2": f("rwkv_a2"), "g2": f("rwkv_g2"),
        "w_out": f("w_out"), "w_gate": f("w_ffn_gate"), "w_up": f("w_ffn_up"), "w_down": f("w_ffn_down"),
    }
    in_maps = []
    for b in range(B):
        for q in range(NQ):
            n_real = (q + 1) * NT
            xp = np.zeros((S, D), np.float32)
            xp[S - n_real:] = x[b, :n_real]
            pm = np.zeros((128, S), np.float32)
            pm[:, S - n_real:] = 1.0
            m = dict(shared)
            m["x"] = xp
            m["pmask"] = pm
            in_maps.append(m)
    nc = build(cfg)
    res = run_bass_kernel_spmd(nc, in_maps, core_ids=list(range(len(in_maps))))
    out = np.zeros((B, SEQ, D), np.float32)
    i = 0
    for b in range(B):
        for q in range(NQ):
            out[b, q * NT:(q + 1) * NT] = res.results[i]["out"]
            i += 1
    return out


def kernel(**inputs):
    return run(REAL, inputs, 2, 8192)
```

```python
import contextlib
import os
import numpy as np
import concourse.bass as bass
import concourse.mybir as mybir
from concourse.bass_utils import run_bass_kernel_spmd

F32 = mybir.dt.float32
BF16 = mybir.dt.bfloat16
ALU = mybir.AluOpType
AF = mybir.ActivationFunctionType

SAME_ENGINE_SYNC = True
NORM_EPS = 1e-6
GN_EPS = 64e-5
C = 64


class _Op:
    __slots__ = ("eng", "fn", "deps", "idx", "is_dma", "sem_key", "sem_val", "needs_inc")


class _Stop(Exception):
    pass


class _Rec:
    def __init__(self):
        self.call = None

    def __getattr__(self, name):
        def f(*a, **k):
            self.call = (name, a, k)
            return None
        return f


class Prog:
    ENGINES = ("tensor", "vector", "scalar", "gpsimd", "sync")

    def __init__(self, nc):
        self.nc = nc
        self.ops = []
        self.last_writer = {}
        self.readers = {}
        self.dma_counts = {}
        self.ns = None
        self.ns_names = set()

    def add(self, eng, fn, reads=(), writes=(), dma=None):
        op = _Op()
        op.eng = eng
        rec = _Rec()
        fn(rec)
        op.fn = rec.call
        op.idx = len(self.ops)
        op.is_dma = dma is not None
        op.sem_key = dma
        op.needs_inc = False
        op.sem_val = None
        if self.ns is not None:
            reads = [(k, self.ns) if k in self.ns_names else k for k in reads]
            writes = [(k, self.ns) if k in self.ns_names else k for k in writes]
        reads = list(reads) + ["__phase__"]
        deps = set()
        for b in reads:
            lw = self.last_writer.get(b)
            if lw is not None:
                deps.add(lw)
        for b in writes:
            lw = self.last_writer.get(b)
            if lw is not None:
                deps.add(lw)
            for r in self.readers.get(b, ()):
                deps.add(r)
        deps.discard(op.idx)
        op.deps = deps
        for b in reads:
            self.readers.setdefault(b, []).append(op.idx)
        for b in writes:
            self.last_writer[b] = op.idx
            self.readers[b] = []
        if op.is_dma:
            c = self.dma_counts.get(dma, 0) + 16
            self.dma_counts[dma] = c
            op.sem_val = c
        self.ops.append(op)
        return op.idx

    def barrier(self, tile_ap):
        self.add("gpsimd", lambda e: e.memset(tile_ap, 0.0), reads=[], writes=["__phase__", "__bar__"])

    def emit(self, stack):
        nc = self.nc
        ops = self.ops
        for op in ops:
            for d in op.deps:
                p = ops[d]
                if p.is_dma:
                    continue
                if p.eng == op.eng and not op.is_dma and (p.eng == "tensor" or not SAME_ENGINE_SYNC):
                    continue
                p.needs_inc = True
        esem = {e: stack.enter_context(nc.semaphore("es_" + e)) for e in self.ENGINES}
        dsem = {}
        for i, k in enumerate(self.dma_counts):
            dsem[k] = stack.enter_context(nc.semaphore("ds%d" % i))
        cnt = {e: 0 for e in self.ENGINES}
        for op in ops:
            if not op.is_dma and op.needs_inc:
                cnt[op.eng] += 1
                op.sem_val = cnt[op.eng]
        per_eng = {e: [] for e in self.ENGINES}
        for op in ops:
            per_eng[op.eng].append(op)
        block = stack.enter_context(nc.Block())

        def make(e):
            def body(eng):
                waited = {}
                for op in per_eng[e]:
                    need = {}
                    for d in op.deps:
                        p = ops[d]
                        if p.is_dma:
                            s, v, key = dsem[p.sem_key], p.sem_val, ("d", p.sem_key)
                        else:
                            if not p.needs_inc:
                                continue
                            if p.eng == e and not op.is_dma and (e == "tensor" or not SAME_ENGINE_SYNC):
                                continue
                            s, v, key = esem[p.eng], p.sem_val, ("e", p.eng)
                        if waited.get(key, 0) >= v:
                            continue
                        if key not in need or need[key][1] < v:
                            need[key] = (s, v)
                    for key, (s, v) in need.items():
                        eng.wait_ge(s, v)
                        waited[key] = v
                    ins = getattr(eng, op.fn[0])(*op.fn[1], **op.fn[2])
                    if op.is_dma:
                        ins.then_inc(dsem[op.sem_key], 16)
                    elif op.needs_inc:
                        ins.then_inc(esem[e], 1)
                if e == "sync":
                    for k, c in self.dma_counts.items():
                        eng.wait_ge(dsem[k], c)
            return body

        block.tensor(make("tensor"))
        block.vector(make("vector"))
        block.scalar(make("scalar"))
        block.gpsimd(make("gpsimd"))
        block.sync(make("sync"))


class Cfg:
    def __init__(self, D, S, NT, LH, RP, DFF):
        self.D, self.S, self.NT, self.LH, self.RP, self.DFF = D, S, NT, LH, RP, DFF
        self.NDC = D // 128
        self.NF = DFF // 128
        self.NKM = LH + RP
        self.NCT = 2 * LH + 3 * RP + 4
        self.INW = 2 * LH * 128 + 3 * RP * 128 + 96 + 96 + 256
        self.TBW = min(1024, S)
        self.TBC = min(512, NT)
        o = 0
        def take(n):
            nonlocal o
            r = o
            o += n
            return r
        self.o_gmix = take(self.NDC); self.o_gffn = take(self.NDC); self.o_gfin = take(self.NDC)
        self.o_cw = take(4 * LH); self.o_cb = take(LH); self.o_ba = take(LH); self.o_bi = take(LH); self.o_lam = take(LH)
        self.o_mu = take(3 * RP + 4)
        self.o_w0 = take(RP); self.o_a0 = take(RP); self.o_kk = take(RP); self.o_ka = take(RP)
        self.o_rk = take(RP); self.o_gw = take(RP); self.o_gb = take(RP)
        self.o_fw = take(3 * self.NF); self.o_fb = take(self.NF)
        self.NP = o

    def coltile(self, ct):
        LH, RP = self.LH, self.RP
        nfull = 2 * LH + 3 * RP
        if ct < nfull:
            return ct * 128, 128
        base = nfull * 128
        return [(base, 96), (base + 96, 96), (base + 192, 128), (base + 320, 128)][ct - nfull]


REAL = Cfg(4096, 8192, 2048, 16, 16, 11008)


def host_consts():
    c = np.zeros((128, 7, 128), np.float32)
    c[:, 0, :] = np.eye(128)
    bd = np.zeros((128, 128), np.float32); bd[:64, :64] = 1; bd[64:, 64:] = 1
    c[:, 1, :] = bd
    c[:, 2, :] = bd / 64.0
    c[:, 3, :] = 1.0
    t = np.arange(128) % 64
    s = np.arange(64)
    c[:, 4, 0:64] = (s[None, :] < t[:, None])
    c[:, 5, 0:64] = (s[None, :] > t[:, None])
    c[:, 5, 64:128] = (s[None, :] >= t[:, None])
    c[:, 6, :] = 1.0
    c[:, 6, 0::64] = 0.0
    return c


def host_pvec(cfg, inp):
    LH, RP = cfg.LH, cfg.RP
    pv = np.zeros((128, cfg.NP), np.float32)
    def put(o, vec):
        v = np.asarray(vec, np.float32).reshape(-1, 128)
        pv[:, o:o + v.shape[0]] = v.T
    put(cfg.o_gmix, inp["g_mix"][0]); put(cfg.o_gffn, inp["g_ffn"][0]); put(cfg.o_gfin, inp["g_final"])
    for j in range(4):
        put(cfg.o_cw + j * LH, inp["conv_lru_w"][0, j])
    put(cfg.o_cb, inp["conv_lru_b"][0]); put(cfg.o_ba, inp["lru_b_a"][0].reshape(-1)); put(cfg.o_bi, inp["lru_b_i"][0].reshape(-1))
    put(cfg.o_lam, inp["lru_lambda"][0])
    mu = inp["rwkv_mu"][0]
    W = RP * 128
    put(cfg.o_mu, mu[:3 * W])
    for i, (a, n) in enumerate([(3 * W, 96), (3 * W + 96, 96), (3 * W + 192, 128), (3 * W + 320, 128)]):
        pv[:n, cfg.o_mu + 3 * RP + i] = mu[a:a + n]
    for o, k in [(cfg.o_w0, "rwkv_w0"), (cfg.o_a0, "rwkv_a0"), (cfg.o_kk, "rwkv_k_k"), (cfg.o_ka, "rwkv_k_a"),
                 (cfg.o_rk, "rwkv_r_k"), (cfg.o_gw, "rwkv_gn_w"), (cfg.o_gb, "rwkv_gn_b")]:
        put(o, inp[k][0])
    for j in range(3):
        put(cfg.o_fw + j * cfg.NF, inp["ffn_conv_w"][0, j])
    put(cfg.o_fb, inp["ffn_conv_b"][0])
    return pv


def build(cfg):
    D, S, NT, LH, RP, DFF = cfg.D, cfg.S, cfg.NT, cfg.LH, cfg.RP, cfg.DFF
    NDC, NF, NKM, NCT = cfg.NDC, cfg.NF, cfg.NKM, cfg.NCT
    EXT = 128
    BASE = S - NT - EXT
    NTE = NT + EXT
    assert BASE >= 0
    nc = bass.Bass("TRN2", target_bir_lowering=False)
    dt_in = lambda n, s, d=F32: nc.dram_tensor(n, s, d, kind="ExternalInput").ap()
    x_in = dt_in("x", [S, D])
    pmask = dt_in("pmask", [128, S])
    consts_in = dt_in("consts", [128, 7, 128])
    pvec_in = dt_in("pvec", [128, cfg.NP])
    w_in = dt_in("w_in", [D, cfg.INW])
    lru_wa = dt_in("lru_wa", [LH, 128, 128])
    lru_wi = dt_in("lru_wi", [LH, 128, 128])
    w2_in = dt_in("w2", [96, RP * 128])
    a2_in = dt_in("a2", [96, RP * 128])
    g2_in = dt_in("g2", [256, RP * 128])
    w_out = dt_in("w_out", [NKM * 128, D])
    w_gate = dt_in("w_gate", [D, DFF])
    w_up = dt_in("w_up", [D, DFF])
    w_down = dt_in("w_down", [DFF, D])
    out = nc.dram_tensor("out", [NT, D], F32, kind="ExternalOutput").ap()
    scr = lambda n, s, d=F32: nc.dram_tensor(n, s, d, kind="Internal").ap()
    zTl = [scr("zT%d" % i, [128, S]) for i in range(NCT)]
    xTs = scr("xTs", [NDC, 128, NTE])
    yTs = scr("yTs", [NKM, 128, NTE], BF16)
    x1s = scr("x1s", [NDC, 128, NTE])
    x2s = scr("x2s", [NDC, 128, NTE])
    Wb_in = scr("Wb_in", [NCT, 128, NDC, 128], BF16)
    Wb_out = scr("Wb_out", [NDC, 128, NKM, 128], BF16)
    Wb_gate = scr("Wb_gate", [NF, 128, NDC, 128], BF16)
    Wb_up = scr("Wb_up", [NF, 128, NDC, 128], BF16)
    Wb_down = scr("Wb_down", [NDC, 128, NF, 128], BF16)

    P = Prog(nc)
    with contextlib.ExitStack() as st:
        AF32 = 50400
        arena = nc.alloc_sbuf_tensor("arena", [128, AF32], F32)
        cst = nc.alloc_sbuf_tensor("cst", [128, 7, 128], F32)
        pv = nc.alloc_sbuf_tensor("pv", [128, cfg.NP], F32)
        dv = nc.alloc_sbuf_tensor("dv", [128, 4 * RP + 8 + LH * 2], F32)
        bar = nc.alloc_sbuf_tensor("bar", [128, 8], F32)
        banks = [nc.alloc_psum_tensor("pb%d" % i, [128, 512], F32) for i in range(8)]
        off = {"f": 0}

        def reset():
            off["f"] = 0

        def tf(n):
            n = (n + 7) // 8 * 8
            a = arena[:, off["f"]:off["f"] + n]
            off["f"] += n
            assert off["f"] <= AF32, off["f"]
            return a

        def tb(n):
            n = (n + 15) // 16 * 16
            return tf(n // 2).bitcast(BF16)

        ident = cst[:, 0, :]; BD = cst[:, 1, :]; BDS = cst[:, 2, :]; ONES = cst[:, 3, :]
        MK_L = cst[:, 4, 0:64]; MK_T = cst[:, 5, :]; MK_C = cst[:, 6, :]
        pc = lambda o: pv[:, o:o + 1]

        def dma(key, out_ap, in_ap, r, w, eng="sync"):
            P.add(eng, lambda e: e.dma_start(out=out_ap, in_=in_ap), reads=r, writes=w, dma=key)

        V = lambda fn, r, w: P.add("vector", fn, r, w)
        A = lambda fn, r, w: P.add("scalar", fn, r, w)
        G = lambda fn, r, w: P.add("gpsimd", fn, r, w)
        T = lambda fn, r, w: P.add("tensor", fn, r, w)
        rr = {"i": 0}

        def E(fn, r, w):
            rr["i"] += 1
            P.add("vector", fn, r, w)

        dma("cst", cst[:], consts_in, [], ["cst"])
        dma("pv", pv[:], pvec_in, [], ["pv"])
        o_c1 = 0; o_c2 = LH
        A(lambda e: e.activation(out=dv[:, 0:LH], in_=pv[:, cfg.o_lam:cfg.o_lam + LH], func=AF.Exp, scale=-1.0), ["pv"], ["dv"])
        A(lambda e: e.activation(out=dv[:, 0:LH], in_=dv[:, 0:LH], func=AF.Ln, bias=1.0), ["dv"], ["dv"])
        V(lambda e: e.tensor_scalar(out=dv[:, LH:2 * LH], in0=dv[:, 0:LH], scalar1=-16.0, scalar2=None, op0=ALU.mult), ["dv"], ["dv2"])
        V(lambda e: e.tensor_scalar(out=dv[:, 0:LH], in0=dv[:, 0:LH], scalar1=-8.0, scalar2=None, op0=ALU.mult), ["dv", "dv2"], ["dv"])
        dc = lambda o: dv[:, o:o + 1]

        wslot = {"i": 0}

        NWS = 4

        def project(Wb, ct, m, nk, rhs_fn, rhs_keys, nsub, n, sink, wst, wbf, bank_ids, tagc):
            k0 = 0
            pieces = []
            while k0 < nk:
                kn = min(32 if nk <= 32 else 29, nk - k0)
                pieces.append((k0, kn))
                k0 += kn
            assert len(pieces) == 1 or nsub <= len(bank_ids)
            for pi, (k0, kn) in enumerate(pieces):
                sl = wslot["i"] % NWS
                wslot["i"] += 1
                wb_ = wbf[sl].rearrange("p (k c) -> p k c", c=128)
                dma(("wbf", sl), wb_[:, 0:kn, :], Wb[ct, :, k0:k0 + kn, :], [], [("wbf", sl)])
                for sb in range(nsub):
                    bk = bank_ids[(tagc * nsub + sb) % len(bank_ids)] if len(pieces) == 1 else bank_ids[sb]
                    for k in range(kn):
                        kk = k0 + k
                        T(lambda e, bk=bk, wb_=wb_, k=k, kk=kk, sb=sb: e.matmul(banks[bk][0:m, 0:n], lhsT=wb_[:, k, 0:m], rhs=rhs_fn(kk, sb),
                                                                        start=(kk == 0), stop=(kk == nk - 1)),
                          [("wbf", sl)] + rhs_keys, [("bank", bk)])
                    if pi == len(pieces) - 1:
                        sink(sb, banks[bk][0:m, 0:n], ("bank", bk))

        reset()
        wsW = [tf(8 * 512) for _ in range(3)]
        wbW = [tb(8 * 512) for _ in range(3)]
        wcnt = {"i": 0}
        cast_eng = ["gpsimd", "scalar", "scalar"]

        def precast(W, Wb, nk, tiles):
            groups = []
            i = 0
            while i < len(tiles):
                if tiles[i][2] == 128:
                    j = i
                    while j < len(tiles) and j - i < 4 and tiles[j][2] == 128 and tiles[j][1] == tiles[i][1] + (j - i) * 128:
                        j += 1
                    groups.append(tiles[i:j])
                    i = j
                else:
                    groups.append(tiles[i:i + 1])
                    i += 1
            for grp in groups:
                ct0, c0, _ = grp[0]
                ng = len(grp)
                mm = grp[0][2] if ng == 1 else 128
                width = mm if ng == 1 else ng * 128
                for k0 in range(0, nk, 8):
                    kn = min(8, nk - k0)
                    sl = wcnt["i"] % 3
                    eng = cast_eng[wcnt["i"] % 3]
                    wcnt["i"] += 1
                    src = wsW[sl][:, 0:kn * width].rearrange("p (k c) -> p k c", c=width)
                    dma(("wsW", sl), src, W[k0 * 128:(k0 + kn) * 128, c0:c0 + width].rearrange("(k p) c -> p k c", p=128), [], [("wsW", sl)])
                    dst = wbW[sl][:, 0:ng * kn * 128].rearrange("p (j k c) -> p j k c", j=ng, c=128)
                    srcv = src if ng == 1 else None
                    if ng == 1:
                        o_ap, i_ap = dst[:, 0, :, 0:mm], src
                    else:
                        o_ap, i_ap = dst, src.rearrange("p k (j c) -> p j k c", c=128)
                    if eng == "scalar":
                        P.add(eng, lambda e: e.copy(out=o_ap, in_=i_ap), [("wsW", sl)], [("wbW", sl)])
                    else:
                        P.add(eng, lambda e: e.tensor_copy(out=o_ap, in_=i_ap), [("wsW", sl)], [("wbW", sl)])
                    if ng == 1:
                        dma(("wbW", sl), Wb[ct0, :, k0:k0 + kn, 0:mm], dst[:, 0, :, 0:mm], [("wbW", sl)], [], eng=eng)
                    else:
                        dma(("wbW", sl), Wb[ct0:ct0 + ng, :, k0:k0 + kn, :].rearrange("j p k c -> p j k c"), dst, [("wbW", sl)], [], eng=eng)

        precast(w_in, Wb_in, NDC, [(ct,) + cfg.coltile(ct) for ct in range(NCT)])
        precast(w_out, Wb_out, NKM, [(d, d * 128, 128) for d in range(NDC)])
        precast(w_gate, Wb_gate, NDC, [(f, f * 128, 128) for f in range(NF)])
        precast(w_up, Wb_up, NDC, [(f, f * 128, 128) for f in range(NF)])
        precast(w_down, Wb_down, NF, [(d, d * 128, 128) for d in range(NDC)])
        P.barrier(bar[:, 6:7])

        reset()
        TBW = cfg.TBW
        OUT_ONLY = set(range(LH, 2 * LH)) | set(range(2 * LH, 2 * LH + RP)) | {2 * LH + 3 * RP + 2, 2 * LH + 3 * RP + 3}
        xs = [tf(D) for _ in range(2)]
        xTt = tf(NDC * 128).rearrange("p (d t) -> p d t", t=128)
        sqt = tf(NDC * 128).rearrange("p (d t) -> p d t", t=128)
        rstd = tf(128)
        wst = None
        zst = [tf(512) for _ in range(4)]
        hT = tb(NDC * TBW).rearrange("p (d t) -> p d t", t=TBW)
        wbf = [tb(32 * 128) for _ in range(NWS)]
        ti = 0
        zc = 0
        for wb in range(S // TBW):
            for tt in range(TBW // 128):
                t0 = wb * TBW + tt * 128
                sl = ti % 2
                ti += 1
                dma(("xs", sl), xs[sl], x_in[t0:t0 + 128, :], [], [("xs", sl)])
                for g4 in range(NDC // 4 if NDC >= 4 else 1):
                    nd = min(4, NDC)
                    bk = g4 % 2
                    for j in range(nd):
                        d = g4 * 4 + j
                        T(lambda e, bk=bk, j=j, d=d, sl=sl: e.matmul(banks[bk][:, j * 128:(j + 1) * 128], lhsT=xs[sl][:, d * 128:(d + 1) * 128],
                                                                    rhs=ident, start=True, stop=True), [("xs", sl), "cst"], [("bank", bk)])
                    V(lambda e, bk=bk, g4=g4, nd=nd: e.tensor_copy(out=xTt[:, g4 * 4:g4 * 4 + nd, :], in_=banks[bk][:, 0:nd * 128].rearrange("p (d t) -> p d t", t=128)),
                      [("bank", bk)], [("xTt", g4)])
                    A(lambda e, g4=g4, nd=nd: e.activation(out=sqt[:, g4 * 4:g4 * 4 + nd, :], in_=xTt[:, g4 * 4:g4 * 4 + nd, :], func=AF.Square),
                      [("xTt", g4)], [("sqt", g4)])
                for d in range(NDC):
                    T(lambda e, d=d: e.matmul(banks[2][:, 0:128], lhsT=ONES, rhs=sqt[:, d, :], start=(d == 0), stop=(d == NDC - 1)),
                      [("sqt", d // 4), "cst"], [("bank", 2)])
                V(lambda e: e.tensor_scalar(out=rstd, in0=banks[2][:, 0:128], scalar1=1.0 / D, scalar2=NORM_EPS, op0=ALU.mult, op1=ALU.add), [("bank", 2)], ["rstd"])
                A(lambda e: e.activation(out=rstd, in_=rstd, func=AF.Sqrt), ["rstd"], ["rstd"])
                V(lambda e: e.reciprocal(out=rstd, in_=rstd), ["rstd"], ["rstd"])
                for d in range(NDC):
                    E(lambda e, d=d, tt=tt: e.scalar_tensor_tensor(out=hT[:, d, tt * 128:(tt + 1) * 128], in0=xTt[:, d, :], scalar=pc(cfg.o_gmix + d), in1=rstd,
                                                                  op0=ALU.mult, op1=ALU.mult), [("xTt", d // 4), "rstd", "pv"], [("hT", tt)])
                if t0 >= BASE:
                    o = t0 - BASE
                    for da in range(0, NDC, 8):
                        db = min(NDC, da + 8)
                        dma("xTs", xTs[da:db, :, o:o + 128].rearrange("d p t -> p d t"), xTt[:, da:db, :], [("xTt", g) for g in range(max(1, NDC // 4))], [("xTs", o // 128, da)])
            nsub = max(1, TBW // 512)
            n = min(512, TBW)
            prefix_blk = (wb + 1) * TBW <= BASE - 520
            for ct in range(NCT):
                if prefix_blk and ct in OUT_ONLY:
                    continue
                c0, m = cfg.coltile(ct)

                def sink(sb, ps, key, ct=ct, m=m, wb=wb, n=n):
                    nonlocal zc
                    zs = zc % 4
                    zc += 1
                    A(lambda e, zs=zs: e.copy(out=zst[zs][0:m, 0:n], in_=ps), [key], [("zst", zs)])
                    tok = wb * TBW + sb * n
                    dma(("zst", zs), zTl[ct][0:m, tok:tok + n], zst[zs][0:m, 0:n], [("zst", zs)], [("zT", ct, tok // 256), ("zT", ct, tok // 256 + 1)] if n == 512 else [("zT", ct, tok // 256)], eng="scalar")
                project(Wb_in, ct, m, NDC, lambda k, sb: hT[:, k, sb * n:(sb + 1) * n], [("hT", t) for t in range(TBW // 128)], nsub, n, sink, wst, wbf, [3, 4, 5, 6], ct)
        P.barrier(bar[:, 0:1])

        reset()
        STOP = float(os.environ.get('KSTOP', '9'))
        KEXP = os.environ.get('KEXP', '')
        LH_ = LH if STOP >= 2 else 0
        TB = min(512, S)
        NBL = S // TB
        wa_t = tf(128); wi_t = tf(128)
        hprev = tf(8)[:, 0:1]
        lx = [tf(TB + 3) for _ in range(2)]
        lg = [tf(TB) for _ in range(2)]
        lm = [tf(TB) for _ in range(2)]
        nm = ["xc", "r", "ig", "a", "a2", "u", "h", "t1", "t2", "y"]
        lts = [{k: tf(TB) for k in nm} for _ in range(2)]
        P.ns_names = set(nm)
        ybf = [tb(TB) for _ in range(2)]
        li = 0
        for h in range(LH_):
            dma("wa", wa_t, lru_wa[h], [], ["wa"])
            dma("wi", wi_t, lru_wi[h], [], ["wi"])
            for b in range(NBL):
                t0 = b * TB
                sl = li % 2
                li += 1
                if b == 0:
                    G(lambda e, sl=sl: e.memset(lx[sl][:, 0:3], 0.0), [], [("lx", sl)])
                    dma(("lx", sl), lx[sl][:, 3:TB + 3], zTl[h][:, 0:TB], [("zT", h, j) for j in range(0, max(1, TB // 256))], [("lx", sl)])
                else:
                    dma(("lx", sl), lx[sl][:, 0:TB + 3], zTl[h][:, t0 - 3:t0 + TB], [("zT", h, j) for j in range((t0 - 3) // 256, (t0 + TB - 1) // 256 + 1)], [("lx", sl)])
                lt = lts[sl]
                P.ns = sl
                dma(("lm", sl), lm[sl], pmask[:, t0:t0 + TB], [], [("lm", sl)])
                need_y = t0 + TB > BASE
                if need_y:
                    dma(("lg", sl), lg[sl], zTl[LH + h][:, t0:t0 + TB], [("zT", LH + h, j) for j in range(t0 // 256, (t0 + TB - 1) // 256 + 1)], [("lg", sl)])
                X = lx[sl]
                A(lambda e, X=X, h=h: e.activation(out=lt["xc"], in_=X[:, 3:TB + 3], func=AF.Identity, scale=pc(cfg.o_cw + 3 * LH + h), bias=pc(cfg.o_cb + h)),
                  [("lx", sl), "pv"], ["xc"])
                for j in range(3):
                    V(lambda e, X=X, h=h, j=j: e.scalar_tensor_tensor(out=lt["xc"], in0=X[:, j:j + TB], scalar=pc(cfg.o_cw + j * LH + h), in1=lt["xc"], op0=ALU.mult, op1=ALU.add),
                      [("lx", sl), "pv", "xc"], ["xc"])
                T(lambda e: e.matmul(banks[0][:, 0:TB], lhsT=wa_t, rhs=lt["xc"], start=True, stop=True), ["wa", "xc"], [("bank", 0)])
                T(lambda e: e.matmul(banks[1][:, 0:TB], lhsT=wi_t, rhs=lt["xc"], start=True, stop=True), ["wi", "xc"], [("bank", 1)])
                A(lambda e, h=h: e.activation(out=lt["r"], in_=banks[0][:, 0:TB], func=AF.Sigmoid, bias=pc(cfg.o_ba + h)), [("bank", 0), "pv"], ["r"])
                A(lambda e, h=h: e.activation(out=lt["ig"], in_=banks[1][:, 0:TB], func=AF.Sigmoid, bias=pc(cfg.o_bi + h)), [("bank", 1), "pv"], ["ig"])
                A(lambda e, h=h: e.activation(out=lt["a"], in_=lt["r"], func=AF.Exp, scale=dc(h)), ["r", "dv"], ["a"])
                A(lambda e, h=h: e.activation(out=lt["a2"], in_=lt["r"], func=AF.Exp, scale=dc(LH + h)), ["r", "dv2"], ["a2"])
                V(lambda e: e.tensor_scalar(out=lt["a2"], in0=lt["a2"], scalar1=-1.0, scalar2=1.0, op0=ALU.mult, op1=ALU.add), ["a2"], ["a2"])
                V(lambda e: e.tensor_scalar(out=lt["a2"], in0=lt["a2"], scalar1=0.0, scalar2=None, op0=ALU.max), ["a2"], ["a2"])
                A(lambda e: e.activation(out=lt["a2"], in_=lt["a2"], func=AF.Sqrt), ["a2"], ["a2"])
                G(lambda e: e.tensor_tensor(out=lt["u"], in0=lt["ig"], in1=lt["xc"], op=ALU.mult), ["ig", "xc"], ["u"])
                G(lambda e, sl=sl: e.tensor_tensor(out=lt["u"], in0=lt["u"], in1=lm[sl], op=ALU.mult), ["u", ("lm", sl)], ["u"])
                V(lambda e: e.tensor_tensor(out=lt["u"], in0=lt["u"], in1=lt["a2"], op=ALU.mult), ["u", "a2"], ["u"])
                if b == 0:
                    V(lambda e: e.tensor_tensor_scan(out=lt["h"], data0=lt["a"], data1=lt["u"], initial=0.0, op0=ALU.mult, op1=ALU.add), ["a", "u", "hprev"], ["h"])
                else:
                    V(lambda e: e.tensor_tensor_scan(out=lt["h"], data0=lt["a"], data1=lt["u"], initial=hprev, op0=ALU.mult, op1=ALU.add), ["a", "u", "hprev"], ["h"])
                V(lambda e: e.tensor_copy(out=hprev, in_=lt["h"][:, TB - 1:TB]), ["h"], ["hprev"])
                if need_y:
                    Gz = lg[sl]
                    A(lambda e, Gz=Gz: e.activation(out=lt["t1"], in_=Gz, func=AF.Square), [("lg", sl)], ["t1"])
                    V(lambda e: e.tensor_scalar(out=lt["t1"], in0=lt["t1"], scalar1=0.044715, scalar2=1.0, op0=ALU.mult, op1=ALU.add), ["t1"], ["t1"])
                    G(lambda e, Gz=Gz: e.tensor_tensor(out=lt["t1"], in0=lt["t1"], in1=Gz, op=ALU.mult), ["t1", ("lg", sl)], ["t1"])
                    A(lambda e: e.activation(out=lt["t2"], in_=lt["t1"], func=AF.Sigmoid, scale=1.5957691216057308), ["t1"], ["t2"])
                    G(lambda e, Gz=Gz: e.tensor_tensor(out=lt["t2"], in0=lt["t2"], in1=Gz, op=ALU.mult), ["t2", ("lg", sl)], ["t2"])
                    V(lambda e, sl=sl: e.tensor_tensor(out=ybf[sl], in0=lt["t2"], in1=lt["h"], op=ALU.mult), ["t2", "h"], [("ybf", sl)])
                    o = t0 - BASE
                    lo = max(0, -o)
                    dma(("ybf", sl), yTs[h, :, o + lo:o + TB], ybf[sl][:, lo:TB], [("ybf", sl)], [("yTs", h, t0)], eng="gpsimd")
        P.ns = None
        P.ns_names = set()
        P.barrier(bar[:, 1:2])

        reset()
        U = 256 if S >= 256 else S
        NCH = U // C
        NU = S // U
        w2p = tf(128); a2p = tf(128); g2p = tf(256).rearrange("p (k c) -> p k c", c=128)
        Zs = [tf(64) for _ in range(2)]
        ldn = ["r", "k", "v", "zw", "za", "g0", "g1"]
        ld = [{k: tf(U + 1) for k in ldn} for _ in range(2)]
        en = ["rs", "ks", "vs", "zws", "zas", "g0s", "g1s", "tmp", "tw", "lw", "av", "gv", "kk", "sq", "k2", "be", "cum", "Ein", "Eni", "Epv", "Een",
              "KT", "BT", "KH", "BH", "rk", "bon", "Lm", "P0", "P1", "GT", "PTs", "Ysb", "yc", "sq2", "yn"]
        et = {k: tf(U) for k in en}
        AR = tf(2 * U).rearrange("p (c q t) -> p c q t", q=2, t=C)
        G(lambda e: e.memset(AR, 0.0), [], ["AR0", "AR1"])
        MT = tf(4 * U).rearrange("p (c q t) -> p c q t", q=4, t=C)
        TM = tf(4 * U).rearrange("p (c q t) -> p c q t", q=4, t=C)
        Xa = [tf(2 * U).rearrange("p (c q t) -> p c q t", q=2, t=C) for _ in range(2)]
        PP = [tf(2 * U).rearrange("p (c q t) -> p c q t", q=2, t=C) for _ in range(2)]
        yob = [tb(U) for _ in range(2)]
        mkc = tf(U)
        for i in range(max(1, U // 128)):
            G(lambda e, i=i: e.tensor_copy(out=mkc[:, i * 128:min(U, (i + 1) * 128)], in_=MK_C[:, 0:min(U, 128)]), ["cst"], ["mkc"])
        v3 = lambda ap: ap.rearrange("p (c t) -> p c t", t=C)
        H2 = [slice(0, 64), slice(64, 128)]
        ui = 0
        for p in range(RP if STOP >= 3 else 0):
            pcols = slice(p * 128, (p + 1) * 128)
            zstate = {"i": 0}
            dma("w2p", w2p[0:96, :], w2_in[:, pcols], [], ["w2p"])
            dma("a2p", a2p[0:96, :], a2_in[:, pcols], [], ["a2p"])
            dma("g2p", g2p, g2_in[:, pcols].rearrange("(k q) c -> q k c", q=128), [], ["g2p"])
            for u in range(NU):
              try:
                t0 = u * U
                need_y = t0 + U > BASE
                sl = ui % 2
                ui += 1
                L = ld[sl]
                cts = {"r": 2 * LH + p, "k": 2 * LH + RP + p, "v": 2 * LH + 2 * RP + p, "zw": 2 * LH + 3 * RP, "za": 2 * LH + 3 * RP + 1,
                       "g0": 2 * LH + 3 * RP + 2, "g1": 2 * LH + 3 * RP + 3}
                rows = {"r": 128, "k": 128, "v": 128, "zw": 96, "za": 96, "g0": 128, "g1": 128}
                mucol = {"r": p, "k": RP + p, "v": 2 * RP + p, "zw": 3 * RP, "za": 3 * RP + 1, "g0": 3 * RP + 2, "g1": 3 * RP + 3}
                outn = {"r": "rs", "k": "ks", "v": "vs", "zw": "zws", "za": "zas", "g0": "g0s", "g1": "g1s"}
                for k in ldn:
                    if k in ("g0", "g1", "r") and not need_y:
                        continue
                    R = rows[k]
                    ct = cts[k]
                    if t0 == 0:
                        G(lambda e, k=k, R=R: e.memset(L[k][0:R, 0:1], 0.0), [], [("ld", sl, k)])
                        dma(("ld", sl, k), L[k][0:R, 1:U + 1], zTl[ct][0:R, 0:U], [("zT", ct, j) for j in range(max(1, U // 256))], [("ld", sl, k)])
                    else:
                        dma(("ld", sl, k), L[k][0:R, 0:U + 1], zTl[ct][0:R, t0 - 1:t0 + U], [("zT", ct, j) for j in range((t0 - 1) // 256, (t0 + U - 1) // 256 + 1)], [("ld", sl, k)])
                    tmpk = "tmp_" + k
                    G(lambda e, k=k, R=R: e.tensor_tensor(out=et["tmp"][0:R, :], in0=L[k][0:R, 0:U], in1=L[k][0:R, 1:U + 1], op=ALU.subtract), [("ld", sl, k)], ["tmp"])
                    V(lambda e, k=k, R=R: e.scalar_tensor_tensor(out=et[outn[k]][0:R, :], in0=et["tmp"][0:R, :], scalar=pv[0:R, cfg.o_mu + mucol[k]:cfg.o_mu + mucol[k] + 1],
                                                                 in1=L[k][0:R, 1:U + 1], op0=ALU.mult, op1=ALU.add), ["tmp", ("ld", sl, k), "pv"], [outn[k]])
                if STOP < 3.05:
                    raise _Stop()
                A(lambda e: e.activation(out=et["tw"][0:96, :], in_=et["zws"][0:96, :], func=AF.Tanh), ["zws"], ["tw"])
                T(lambda e: e.matmul(banks[0][:, 0:U], lhsT=w2p[0:96, :], rhs=et["tw"][0:96, :], start=True, stop=True), ["w2p", "tw"], [("bank", 0)])
                T(lambda e: e.matmul(banks[0][:, U:2 * U], lhsT=a2p[0:96, :], rhs=et["zas"][0:96, :], start=True, stop=True), ["a2p", "zas"], [("bank", 0)])
                A(lambda e, p=p: e.activation(out=et["lw"], in_=banks[0][:, 0:U], func=AF.Sigmoid, bias=pc(cfg.o_w0 + p)), [("bank", 0), "pv"], ["lw"])
                A(lambda e, p=p: e.activation(out=et["av"], in_=banks[0][:, U:2 * U], func=AF.Sigmoid, bias=pc(cfg.o_a0 + p)), [("bank", 0), "pv"], ["av"])
                V(lambda e: e.tensor_scalar(out=et["lw"], in0=et["lw"], scalar1=-0.6065306597126334, scalar2=None, op0=ALU.mult), ["lw"], ["lw"])
                if need_y:
                    A(lambda e: e.activation(out=et["g0s"], in_=et["g0s"], func=AF.Sigmoid), ["g0s"], ["g0s"])
                    A(lambda e: e.activation(out=et["g1s"], in_=et["g1s"], func=AF.Sigmoid), ["g1s"], ["g1s"])
                    T(lambda e: e.matmul(banks[1][:, 0:U], lhsT=g2p[:, 0, :], rhs=et["g0s"], start=True, stop=False), ["g2p", "g0s"], [("bank", 1)])
                    T(lambda e: e.matmul(banks[1][:, 0:U], lhsT=g2p[:, 1, :], rhs=et["g1s"], start=False, stop=True), ["g2p", "g1s"], [("bank", 1)])
                    A(lambda e: e.copy(out=et["gv"], in_=banks[1][:, 0:U]), [("bank", 1)], ["gv"])
                if STOP < 3.1:
                    raise _Stop()
                V(lambda e, p=p: e.tensor_scalar(out=et["kk"], in0=et["ks"], scalar1=pc(cfg.o_kk + p), scalar2=None, op0=ALU.mult), ["ks", "pv"], ["kk"])
                A(lambda e: e.activation(out=et["sq"], in_=et["kk"], func=AF.Square), ["kk"], ["sq"])
                T(lambda e: e.matmul(banks[1][:, U:2 * U], lhsT=BD, rhs=et["sq"], start=True, stop=True), ["cst", "sq"], [("bank", 1)])
                A(lambda e: e.activation(out=et["sq"], in_=banks[1][:, U:2 * U], func=AF.Sqrt), [("bank", 1)], ["sq"])
                V(lambda e: e.tensor_scalar(out=et["sq"], in0=et["sq"], scalar1=1e-12, scalar2=None, op0=ALU.max), ["sq"], ["sq"])
                V(lambda e: e.reciprocal(out=et["sq"], in_=et["sq"]), ["sq"], ["sq"])
                V(lambda e: e.tensor_tensor(out=et["kk"], in0=et["kk"], in1=et["sq"], op=ALU.mult), ["kk", "sq"], ["kk"])
                V(lambda e, p=p: e.tensor_scalar(out=et["k2"], in0=et["av"], scalar1=-1.0, scalar2=pc(cfg.o_ka + p), op0=ALU.add, op1=ALU.mult), ["av", "pv"], ["k2"])
                V(lambda e: e.scalar_tensor_tensor(out=et["k2"], in0=et["k2"], scalar=1.0, in1=et["ks"], op0=ALU.add, op1=ALU.mult), ["k2", "ks"], ["k2"])
                G(lambda e: e.tensor_tensor(out=et["be"], in0=et["kk"], in1=et["av"], op=ALU.mult), ["kk", "av"], ["be"])
                if STOP < 3.15:
                    raise _Stop()
                V(lambda e: e.tensor_tensor_scan(out=et["cum"], data0=mkc, data1=et["lw"], initial=0.0, op0=ALU.mult, op1=ALU.add), ["lw", "mkc"], ["cum"])
                A(lambda e: e.activation(out=et["Ein"], in_=et["cum"], func=AF.Exp), ["cum"], ["Ein"])
                A(lambda e: e.activation(out=et["Eni"], in_=et["cum"], func=AF.Exp, scale=-1.0), ["cum"], ["Eni"])
                G(lambda e: e.tensor_tensor(out=et["Epv"], in0=et["cum"], in1=et["lw"], op=ALU.subtract), ["cum", "lw"], ["Epv"])
                A(lambda e: e.activation(out=et["Epv"], in_=et["Epv"], func=AF.Exp), ["Epv"], ["Epv"])
                for c in range(NCH):
                    V(lambda e, c=c: e.tensor_scalar(out=v3(et["Een"])[:, c, :], in0=v3(et["cum"])[:, c, :], scalar1=v3(et["cum"])[:, c, C - 1:C], scalar2=-1.0,
                                                     op0=ALU.subtract, op1=ALU.mult), ["cum"], ["Een"])
                A(lambda e: e.activation(out=et["Een"], in_=et["Een"], func=AF.Exp), ["Een"], ["Een"])
                if STOP < 3.17:
                    raise _Stop()
                V(lambda e: e.scalar_tensor_tensor(out=AR[:, :, 0, :], in0=v3(et["kk"]), scalar=-1.0, in1=v3(et["Epv"]), op0=ALU.mult, op1=ALU.mult), ["kk", "Epv"], ["AR0"])
                if need_y:
                    G(lambda e: e.tensor_tensor(out=AR[:, :, 1, :], in0=v3(et["rs"]), in1=v3(et["Ein"]), op=ALU.mult), ["rs", "Ein"], ["AR1"])
                G(lambda e: e.tensor_tensor(out=et["KT"], in0=et["k2"], in1=et["Eni"], op=ALU.mult), ["k2", "Eni"], ["KT"])
                V(lambda e: e.tensor_tensor(out=et["BT"], in0=et["be"], in1=et["Eni"], op=ALU.mult), ["be", "Eni"], ["BT"])
                G(lambda e: e.tensor_tensor(out=et["KH"], in0=et["k2"], in1=et["Een"], op=ALU.mult), ["k2", "Een"], ["KH"])
                V(lambda e: e.tensor_tensor(out=et["BH"], in0=et["be"], in1=et["Een"], op=ALU.mult), ["be", "Een"], ["BH"])
                if need_y:
                    V(lambda e, p=p: e.scalar_tensor_tensor(out=et["rk"], in0=et["rs"], scalar=pc(cfg.o_rk + p), in1=et["k2"], op0=ALU.mult, op1=ALU.mult), ["rs", "k2", "pv"], ["rk"])
                    T(lambda e: e.matmul(banks[2][:, 0:U], lhsT=BD, rhs=et["rk"], start=True, stop=True), ["cst", "rk"], [("bank", 2)])
                    V(lambda e: e.tensor_tensor(out=et["bon"], in0=banks[2][:, 0:U], in1=et["vs"], op=ALU.mult), [("bank", 2), "vs"], ["bon"])
                if STOP < 3.2:
                    raise _Stop()
                KTv, BTv, KHv, BHv, Vv = v3(et["KT"]), v3(et["BT"]), v3(et["KH"]), v3(et["BH"]), v3(et["vs"])
                pL = banks[3][:, 0:U].rearrange("p (c t) -> p c t", t=C)
                pM = [banks[4][:, 0:2 * U].rearrange("p (c q t) -> p c q t", q=2, t=C), banks[5][:, 0:2 * U].rearrange("p (c q t) -> p c q t", q=2, t=C)]
                for c in range(NCH):
                    for hh in H2:
                        T(lambda e, c=c, hh=hh: e.matmul(pL[hh, c, :], lhsT=AR[hh, c, 0, :], rhs=BTv[hh, c, :], start=True, stop=True), ["AR0", "BT"], [("bank", 3)])
                        T(lambda e, c=c, hh=hh: e.matmul(pM[0][hh, c, :, :], lhsT=BTv[hh, c, :], rhs=AR[hh, c, :, :], start=True, stop=True), ["AR0", "AR1", "BT"], [("bank", 4)])
                        T(lambda e, c=c, hh=hh: e.matmul(pM[1][hh, c, :, :], lhsT=KTv[hh, c, :], rhs=AR[hh, c, :, :], start=True, stop=True), ["AR0", "AR1", "KT"], [("bank", 5)])
                V(lambda e: e.tensor_tensor(out=v3(et["Lm"]), in0=pL, in1=MK_L.unsqueeze(1).broadcast_to([128, NCH, C]), op=ALU.mult), [("bank", 3), "cst"], ["Lm"])
                mk4 = MK_T.rearrange("p (q t) -> p q t", t=C).unsqueeze(1).broadcast_to([128, NCH, 2, C])
                V(lambda e: e.tensor_tensor(out=MT[:, :, 0:2, :], in0=pM[0], in1=mk4, op=ALU.mult), [("bank", 4), "cst"], ["MT01"])
                V(lambda e: e.tensor_tensor(out=MT[:, :, 2:4, :], in0=pM[1], in1=mk4, op=ALU.mult), [("bank", 5), "cst"], ["MT23"])
                if STOP < 3.3:
                    raise _Stop()
                pT = [banks[6][:, 0:2 * U].rearrange("p (c q t) -> p c q t", q=2, t=C), banks[7][:, 0:2 * U].rearrange("p (c q t) -> p c q t", q=2, t=C)]
                srcs = [(AR[:, :, 0, :], "AR0"), (Vv, "vs"), (BHv, "BH"), (KHv, "KH")]
                for c in range(NCH):
                    for hh in H2:
                        for q, (sap, skey) in enumerate(srcs):
                            T(lambda e, c=c, hh=hh, q=q, sap=sap: e.matmul(pT[q // 2][hh, c, q % 2, :], lhsT=sap[hh, c, :], rhs=ident[hh, hh], start=True, stop=True),
                              [skey, "cst"], [("bank", 6 + q // 2)])
                A(lambda e: e.copy(out=TM[:, :, 0:2, :], in_=pT[0]), [("bank", 6)], ["TM01"])
                A(lambda e: e.copy(out=TM[:, :, 2:4, :], in_=pT[1]), [("bank", 7)], ["TM23"])
                if STOP < 3.35:
                    raise _Stop()
                pX = banks[4][:, 0:2 * U].rearrange("p (c q t) -> p c q t", q=2, t=C)
                pP = banks[5][:, 0:2 * U].rearrange("p (c q t) -> p c q t", q=2, t=C)
                for c in range(NCH):
                    for hh in H2:
                        T(lambda e, c=c, hh=hh: e.matmul(pX[hh, c, 1, :], lhsT=MT[hh, c, 2, :], rhs=TM[hh, c, 1, :], start=True, stop=True), ["MT23", "TM01"], [("bank", 4)])
                G(lambda e: e.tensor_copy(out=Xa[0][:, :, 0, :], in_=TM[:, :, 0, :]), ["TM01"], [("Xa", 0)])
                V(lambda e: e.tensor_copy(out=Xa[0][:, :, 1, :], in_=pX[:, :, 1, :]), [("bank", 4)], [("Xa", 0)])
                if STOP < 3.4:
                    raise _Stop()
                curP = v3(et["Lm"]); curPT = MT[:, :, 0, :]; kP, kPT = "Lm", "MT01"
                for lvl in range(int(os.environ.get('KLVL', '6'))):
                    xi, xo = lvl % 2, (lvl + 1) % 2
                    for c in range(NCH):
                        for hh in H2:
                            T(lambda e, c=c, hh=hh, curPT=curPT, xi=xi: e.matmul(pX[hh, c, :, :], lhsT=(BTv if KEXP == 'B' else curPT)[hh, c, :], rhs=(AR if KEXP == 'A' else Xa[xi])[hh, c, :, :], start=True, stop=True),
                              [kPT, ("Xa", xi)], [("bank", 4)])
                    if os.environ.get('KADD', '1') == '1':
                        V(lambda e, xi=xi, xo=xo: e.tensor_tensor(out=Xa[xo], in0=pX, in1=Xa[xi], op=ALU.add), [("bank", 4), ("Xa", xi)], [("Xa", xo)])
                    if lvl < 5 and os.environ.get('KSQ', '1') == '1':
                        last = lvl == 4
                        for c in range(NCH):
                            for hh in H2:
                                if not last:
                                    T(lambda e, c=c, hh=hh, curP=curP, curPT=curPT: e.matmul(pP[hh, c, 0, :], lhsT=curPT[hh, c, :], rhs=curP[hh, c, :], start=True, stop=True),
                                      [kP, kPT], [("bank", 5)])
                                T(lambda e, c=c, hh=hh, curP=curP, curPT=curPT: e.matmul(pP[hh, c, 1, :], lhsT=curP[hh, c, :], rhs=curPT[hh, c, :], start=True, stop=True),
                                  [kP, kPT], [("bank", 5)])
                        ps = lvl % 2
                        if last:
                            A(lambda e, ps=ps: e.copy(out=PP[ps][:, :, 1, :], in_=pP[:, :, 1, :]), [("bank", 5)], [("PP", ps)])
                        else:
                            A(lambda e, ps=ps: e.copy(out=PP[ps], in_=pP), [("bank", 5)], [("PP", ps)])
                        curP = PP[ps][:, :, 0, :]; curPT = PP[ps][:, :, 1, :]; kP = kPT = ("PP", ps)
                XF = Xa[0]
                kXF = ("Xa", 0)
                if STOP < 3.5:
                    raise _Stop()
                pG = banks[3][:, 0:2 * U].rearrange("p (c q t) -> p c q t", q=2, t=C)
                for c in range(NCH):
                    for hh in H2:
                        if need_y:
                            T(lambda e, c=c, hh=hh: e.matmul(pG[hh, c, 0, :], lhsT=XF[hh, c, 0, :], rhs=MT[hh, c, 1, :], start=True, stop=True), [kXF, "MT01"], [("bank", 3)])
                        T(lambda e, c=c, hh=hh: e.matmul(pG[hh, c, 1, :], lhsT=XF[hh, c, 0, :], rhs=TM[hh, c, 2, :], start=True, stop=True), [kXF, "TM23"], [("bank", 3)])
                KS5 = os.environ.get('KS5', 'va')
                if need_y and 'v' in KS5:
                    V(lambda e: e.tensor_tensor(out=v3(et["GT"]), in0=pG[:, :, 0, :], in1=AR[:, :, 1, :], op=ALU.add), [("bank", 3), "AR1"], ["GT"])
                if 'a' in KS5:
                    V(lambda e: e.tensor_copy(out=v3(et["PTs"]), in_=pG[:, :, 1, :]), [("bank", 3)], ["PTs"])
                if STOP < 3.6:
                    raise _Stop()
                pZ = banks[6][:, 0:U].rearrange("p (c t) -> p c t", t=C)
                pY = banks[7][:, 0:U].rearrange("p (c t) -> p c t", t=C)
                GTv, PTv = v3(et["GT"]), v3(et["PTs"])
                Einv = v3(et["Ein"])
                for c in range(NCH):
                    first = (u == 0 and c == 0)
                    zi = zstate["i"] % 2
                    zo = (zstate["i"] + 1) % 2
                    zstate["i"] += 1
                    if first:
                        G(lambda e, zi=zi: e.memset(Zs[zi], 0.0), [], [("Z", zi)])
                    for hh in H2:
                        T(lambda e, c=c, hh=hh: e.matmul(pZ[hh, c, :], lhsT=TM[hh, c, 2, :], rhs=XF[hh, c, 1, :], start=True, stop=False), ["TM23", kXF], [("bank", 6)])
                        T(lambda e, c=c, hh=hh: e.matmul(pZ[hh, c, :], lhsT=TM[hh, c, 3, :], rhs=TM[hh, c, 1, :], start=False, stop=False), ["TM23", "TM01"], [("bank", 6)])
                        T(lambda e, c=c, hh=hh, zi=zi: e.matmul(pZ[hh, c, :], lhsT=PTv[hh, c, :], rhs=Zs[zi][hh, :], start=False, stop=True), ["PTs", ("Z", zi)], [("bank", 6)])
                        if need_y:
                            T(lambda e, c=c, hh=hh: e.matmul(pY[hh, c, :], lhsT=XF[hh, c, 1, :], rhs=MT[hh, c, 1, :], start=True, stop=False), [kXF, "MT01"], [("bank", 7)])
                            T(lambda e, c=c, hh=hh: e.matmul(pY[hh, c, :], lhsT=TM[hh, c, 1, :], rhs=MT[hh, c, 3, :], start=False, stop=False), ["TM01", "MT23"], [("bank", 7)])
                            T(lambda e, c=c, hh=hh, zi=zi: e.matmul(pY[hh, c, :], lhsT=Zs[zi][hh, :], rhs=GTv[hh, c, :], start=False, stop=True), [("Z", zi), "GT"], [("bank", 7)])
                    V(lambda e, c=c, zi=zi, zo=zo: e.scalar_tensor_tensor(out=Zs[zo], in0=Zs[zi], scalar=Einv[:, c, C - 1:C], in1=pZ[:, c, :], op0=ALU.mult, op1=ALU.add),
                      [("Z", zi), "Ein", ("bank", 6)], [("Z", zo)])
                if STOP < 3.7:
                    raise _Stop()
                if need_y:
                    A(lambda e: e.copy(out=et["Ysb"], in_=banks[7][:, 0:U]), [("bank", 7)], ["Ysb"])
                    T(lambda e: e.matmul(banks[2][:, 0:U], lhsT=BDS, rhs=et["Ysb"], start=True, stop=True), ["cst", "Ysb"], [("bank", 2)])
                    V(lambda e: e.tensor_tensor(out=et["yc"], in0=et["Ysb"], in1=banks[2][:, 0:U], op=ALU.subtract), ["Ysb", ("bank", 2)], ["yc"])
                    A(lambda e: e.activation(out=et["sq2"], in_=et["yc"], func=AF.Square), ["yc"], ["sq2"])
                    T(lambda e: e.matmul(banks[2][:, U:2 * U], lhsT=BDS, rhs=et["sq2"], start=True, stop=True), ["cst", "sq2"], [("bank", 2)])
                    V(lambda e: e.tensor_scalar(out=et["sq2"], in0=banks[2][:, U:2 * U], scalar1=GN_EPS, scalar2=None, op0=ALU.add), [("bank", 2)], ["sq2"])
                    A(lambda e: e.activation(out=et["sq2"], in_=et["sq2"], func=AF.Sqrt), ["sq2"], ["sq2"])
                    V(lambda e: e.reciprocal(out=et["sq2"], in_=et["sq2"]), ["sq2"], ["sq2"])
                    V(lambda e: e.tensor_tensor(out=et["yn"], in0=et["yc"], in1=et["sq2"], op=ALU.mult), ["yc", "sq2"], ["yn"])
                    V(lambda e, p=p: e.tensor_scalar(out=et["yn"], in0=et["yn"], scalar1=pc(cfg.o_gw + p), scalar2=pc(cfg.o_gb + p), op0=ALU.mult, op1=ALU.add), ["yn", "pv"], ["yn"])
                    G(lambda e: e.tensor_tensor(out=et["yn"], in0=et["yn"], in1=et["bon"], op=ALU.add), ["yn", "bon"], ["yn"])
                    ys = ui % 2
                    V(lambda e, ys=ys: e.tensor_tensor(out=yob[ys], in0=et["yn"], in1=et["gv"], op=ALU.mult), ["yn", "gv"], [("yob", ys)])
                    o = t0 - BASE
                    lo = max(0, -o)
                    dma(("yob", ys), yTs[LH + p, :, o + lo:o + U], yob[ys][:, lo:U], [("yob", ys)], [("yTs", LH + p, t0)])
              except _Stop:
                pass
        P.barrier(bar[:, 2:3])

        reset()
        TBC = cfg.TBC
        wst = None
        xt_ = [tf(TBC) for _ in range(4)]
        sqc = [tf(TBC) for _ in range(2)]
        rst = tf(TBC)
        gat = [tf(TBC + 8) for _ in range(2)]
        upt = [tf(TBC) for _ in range(2)]
        sgt = [tf(TBC) for _ in range(2)]
        ot = [tf(512) for _ in range(2)]
        ghs = tf(NF * 2).rearrange("p (f t) -> p f t", t=2)
        pm2 = tf(8)
        wbf = [tb(32 * 128) for _ in range(NWS)]
        h2T = tb(NDC * TBC).rearrange("p (d t) -> p d t", t=TBC)
        R1 = tb(max(NF, NKM) * TBC)
        yTb = R1[:, 0:NKM * TBC].rearrange("p (k t) -> p k t", t=TBC)
        actT = R1[:, 0:NF * TBC].rearrange("p (k t) -> p k t", t=TBC)
        dma("pm2", pm2[:, 0:2], pmask[:, S - NT - 2:S - NT], [], ["pm2"])
        cnt = {"x": 0, "p": 0}
        blocks = [(0, EXT, True)] + [(EXT + i * TBC, TBC, False) for i in range(NT // TBC)]
        for (so, n, halo_only) in (blocks if STOP >= 4 else []):
            for ka in range(0, NKM, 8):
                kb = min(NKM, ka + 8)
                dma("yTb", yTb[:, ka:kb, 0:n], yTs[ka:kb, :, so:so + n].rearrange("k p t -> p k t"), [], ["yTb"])
            for d in range(NDC):
                def sink(sb, ps, key, d=d):
                    s4 = cnt["x"] % 4
                    s2 = cnt["x"] % 2
                    cnt["x"] += 1
                    X_ = xt_[s4][:, 0:n]
                    dma(("xt", s4), X_, xTs[d, :, so:so + n], [], [("xt", s4)])
                    V(lambda e: e.tensor_tensor(out=X_, in0=ps, in1=X_, op=ALU.add), [key, ("xt", s4)], [("xt", s4)])
                    dma(("xt", s4), x1s[d, :, so:so + n], X_, [("xt", s4)], [("x1s", d, so)], eng="gpsimd")
                    A(lambda e: e.activation(out=sqc[s2][:, 0:n], in_=X_, func=AF.Square), [("xt", s4)], [("sqc", s2)])
                    T(lambda e: e.matmul(banks[0][:, 0:n], lhsT=ONES, rhs=sqc[s2][:, 0:n], start=(d == 0), stop=(d == NDC - 1)), [("sqc", s2), "cst"], [("bank", 0)])
                cnt["p"] += 1
                project(Wb_out, d, 128, NKM, lambda k, sb: yTb[:, k, 0:n], ["yTb"], 1, n, sink, wst, wbf, [1, 2], cnt["p"])
            V(lambda e: e.tensor_scalar(out=rst[:, 0:n], in0=banks[0][:, 0:n], scalar1=1.0 / D, scalar2=NORM_EPS, op0=ALU.mult, op1=ALU.add), [("bank", 0)], ["rst"])
            A(lambda e: e.activation(out=rst[:, 0:n], in_=rst[:, 0:n], func=AF.Sqrt), ["rst"], ["rst"])
            V(lambda e: e.reciprocal(out=rst[:, 0:n], in_=rst[:, 0:n]), ["rst"], ["rst"])
            for d in range(NDC):
                s4 = cnt["x"] % 4
                cnt["x"] += 1
                dma(("xt", s4), xt_[s4][:, 0:n], x1s[d, :, so:so + n], [("x1s", d, so)], [("xt", s4)])
                V(lambda e, d=d, s4=s4: e.scalar_tensor_tensor(out=h2T[:, d, 0:n], in0=xt_[s4][:, 0:n], scalar=pc(cfg.o_gffn + d), in1=rst[:, 0:n], op0=ALU.mult, op1=ALU.mult),
                  [("xt", s4), "pv", "rst"], ["h2T"])
            P.barrier(bar[:, 3:4])
            for f in range(NF):
                s2 = f % 2
                G_ = gat[s2]
                def sink_g(sb, ps, key, G_=G_, s2=s2):
                    A(lambda e: e.copy(out=G_[:, 2:n + 2], in_=ps), [key], [("gat", s2)])
                def sink_u(sb, ps, key, s2=s2):
                    A(lambda e: e.copy(out=upt[s2][:, 0:n], in_=ps), [key], [("upt", s2)])
                cnt["p"] += 1
                project(Wb_gate, f, 128, NDC, lambda k, sb: h2T[:, k, 0:n], ["h2T"], 1, n, sink_g, wst, wbf, [1, 2], cnt["p"])
                if halo_only:
                    V(lambda e, G_=G_, f=f: e.tensor_tensor(out=ghs[:, f, :], in0=G_[:, n:n + 2], in1=pm2[:, 0:2], op=ALU.mult), [("gat", s2), "pm2"], ["ghs"])
                    continue
                cnt["p"] += 1
                project(Wb_up, f, 128, NDC, lambda k, sb: h2T[:, k, 0:n], ["h2T"], 1, n, sink_u, wst, wbf, [1, 2], cnt["p"])
                V(lambda e, G_=G_, f=f: e.tensor_copy(out=G_[:, 0:2], in_=ghs[:, f, :]), ["ghs", ("gat", s2)], [("gat", s2)])
                V(lambda e, G_=G_, f=f: e.tensor_copy(out=ghs[:, f, :], in_=G_[:, n:n + 2]), [("gat", s2)], ["ghs"])
                A(lambda e, G_=G_, f=f, s2=s2: e.activation(out=sgt[s2][:, 0:n], in_=G_[:, 2:n + 2], func=AF.Identity, scale=pc(cfg.o_fw + 2 * NF + f), bias=pc(cfg.o_fb + f)),
                  [("gat", s2), "pv"], [("sgt", s2)])
                for j in range(2):
                    V(lambda e, G_=G_, f=f, j=j, s2=s2: e.scalar_tensor_tensor(out=sgt[s2][:, 0:n], in0=G_[:, j:j + n], scalar=pc(cfg.o_fw + j * NF + f), in1=sgt[s2][:, 0:n], op0=ALU.mult, op1=ALU.add),
                      [("gat", s2), "pv", ("sgt", s2)], [("sgt", s2)])
                A(lambda e, s2=s2: e.activation(out=sgt[s2][:, 0:n], in_=sgt[s2][:, 0:n], func=AF.Silu), [("sgt", s2)], [("sgt", s2)])
                V(lambda e, f=f, s2=s2: e.tensor_tensor(out=actT[:, f, 0:n], in0=sgt[s2][:, 0:n], in1=upt[s2][:, 0:n], op=ALU.mult), [("sgt", s2), ("upt", s2)], ["actT"])
            if halo_only:
                P.barrier(bar[:, 4:5])
                continue
            for d in range(NDC):
                def sink_d(sb, ps, key, d=d):
                    s4 = cnt["x"] % 4
                    s2 = cnt["x"] % 2
                    cnt["x"] += 1
                    X_ = xt_[s4][:, 0:n]
                    dma(("xt", s4), X_, x1s[d, :, so:so + n], [("x1s", d, so)], [("xt", s4)])
                    V(lambda e: e.tensor_tensor(out=X_, in0=ps, in1=X_, op=ALU.add), [key, ("xt", s4)], [("xt", s4)])
                    dma(("xt", s4), x2s[d, :, so:so + n], X_, [("xt", s4)], [("x2s", d, so)], eng="gpsimd")
                    A(lambda e: e.activation(out=sqc[s2][:, 0:n], in_=X_, func=AF.Square), [("xt", s4)], [("sqc", s2)])
                    T(lambda e: e.matmul(banks[0][:, 0:n], lhsT=ONES, rhs=sqc[s2][:, 0:n], start=(d == 0), stop=(d == NDC - 1)), [("sqc", s2), "cst"], [("bank", 0)])
                project(Wb_down, d, 128, NF, lambda k, sb: actT[:, k, 0:n], ["actT"], 1, n, sink_d, wst, wbf, [1 + d % 2], d)
            V(lambda e: e.tensor_scalar(out=rst[:, 0:n], in0=banks[0][:, 0:n], scalar1=1.0 / D, scalar2=NORM_EPS, op0=ALU.mult, op1=ALU.add), [("bank", 0)], ["rst"])
            A(lambda e: e.activation(out=rst[:, 0:n], in_=rst[:, 0:n], func=AF.Sqrt), ["rst"], ["rst"])
            V(lambda e: e.reciprocal(out=rst[:, 0:n], in_=rst[:, 0:n]), ["rst"], ["rst"])
            orow = so - EXT
            for d in range(NDC):
                s4 = cnt["x"] % 4
                cnt["x"] += 1
                X_ = xt_[s4][:, 0:n]
                dma(("xt", s4), X_, x2s[d, :, so:so + n], [("x2s", d, so)], [("xt", s4)])
                V(lambda e, d=d, X_=X_: e.scalar_tensor_tensor(out=X_, in0=X_, scalar=pc(cfg.o_gfin + d), in1=rst[:, 0:n], op0=ALU.mult, op1=ALU.mult),
                  [("xt", s4), "pv", "rst"], [("xt", s4)])
                bk = 3 + d % 2
                for tt in range(n // 128):
                    T(lambda e, X_=X_, tt=tt, bk=bk: e.matmul(banks[bk][:, tt * 128:(tt + 1) * 128], lhsT=X_[:, tt * 128:(tt + 1) * 128], rhs=ident, start=True, stop=True),
                      [("xt", s4), "cst"], [("bank", bk)])
                os_ = d % 2
                A(lambda e, bk=bk, os_=os_: e.copy(out=ot[os_][:, 0:n], in_=banks[bk][:, 0:n]), [("bank", bk)], [("ot", os_)])
                dma(("ot", os_), out[orow:orow + n, d * 128:(d + 1) * 128].rearrange("(tt p) c -> p tt c", p=128), ot[os_][:, 0:n].rearrange("p (tt c) -> p tt c", c=128),
                    [("ot", os_)], [], eng="scalar")
            P.barrier(bar[:, 5:6])
        P.emit(st)
    return nc


def run(cfg, inp, B, SEQ):
    S, NT, D = cfg.S, cfg.NT, cfg.D
    NQ = SEQ // NT
    x = np.asarray(inp["x"], np.float32)
    f = lambda k: np.ascontiguousarray(np.asarray(inp[k], np.float32)[0])
    shared = {
        "consts": host_consts(), "pvec": host_pvec(cfg, {k: np.asarray(v, np.float32) for k, v in inp.items()}),
        "w_in": f("w_in"), "lru_wa": f("lru_w_a"), "lru_wi": f("lru_w_i"), "w2": f("rwkv_w2"), "a2": f("rwkv_a2"), "g2": f("rwkv_g2"),
        "w_out": f("w_out"), "w_gate": f("w_ffn_gate"), "w_up": f("w_ffn_up"), "w_down": f("w_ffn_down"),
    }
    in_maps = []
    for b in range(B):
        for q in range(NQ):
            n_real = (q + 1) * NT
            xp = np.zeros((S, D), np.float32)
            xp[S - n_real:] = x[b, :n_real]
            pm = np.zeros((128, S), np.float32)
            pm[:, S - n_real:] = 1.0
            m = dict(shared)
            m["x"] = xp
            m["pmask"] = pm
            in_maps.append(m)
    nc = build(cfg)
    res = run_bass_kernel_spmd(nc, in_maps, core_ids=list(range(len(in_maps))))
    out = np.zeros((B, SEQ, D), np.float32)
    i = 0
    for b in range(B):
        for q in range(NQ):
            out[b, q * NT:(q + 1) * NT] = res.results[i]["out"]
            i += 1
    return out


def kernel(**inputs):
    return run(REAL, inputs, 2, 8192)
```
